# Optimizing a Trainium2 kernel written in Bass

```python
import jax, jax.numpy as jnp
from jax import lax
import numpy as np

D_MODEL = 1024
BATCH = 8
SEQ = 2048
DEPTH = 1

CONV_CH = 512
N_CONV_GROUPS = 8
ATTN_HEADS = 8
HEAD_DIM = 64
ATTN_WIDTH = ATTN_HEADS * HEAD_DIM
MIX_WIDTH = CONV_CH + ATTN_WIDTH
IN_WIDTH = 2 * CONV_CH + 3 * ATTN_WIDTH
CONV_KERNEL = 31
MOBA_BLOCK = 256
MOBA_TOPK = 3
QUERY_CHUNK = 32
ROPE_THETA = 500000.0
ROPE_DIM = HEAD_DIM // 4
D_FF = -(-8 * D_MODEL // (3 * 256)) * 256
PLE_DIM = 256
EPS = 1e-6

kernel_name = "hymba_conformer_moba_hybrid"


def rmsnorm(x, g):
    xf = x.astype(jnp.float32)
    y = xf * lax.rsqrt(jnp.mean(xf * xf, axis=-1, keepdims=True) + EPS)
    return (y * g.astype(jnp.float32)).astype(x.dtype)


def layernorm(x, g, b):
    xf = x.astype(jnp.float32)
    mu = jnp.mean(xf, axis=-1, keepdims=True)
    var = jnp.mean(jnp.square(xf - mu), axis=-1, keepdims=True)
    y = (xf - mu) * lax.rsqrt(var + EPS)
    return (y * g.astype(jnp.float32) + b.astype(jnp.float32)).astype(x.dtype)


def rope_tables(positions, dtype):
    inv_freq = ROPE_THETA ** (-jnp.arange(0, ROPE_DIM, 2, dtype=jnp.float32) / ROPE_DIM)
    ang = positions.astype(jnp.float32)[..., None] * inv_freq
    return jnp.cos(ang)[:, None].astype(dtype), jnp.sin(ang)[:, None].astype(dtype)


def apply_partial_rope(x, cos, sin):
    half = ROPE_DIM // 2
    x1, x2, rest = x[..., :half], x[..., half:ROPE_DIM], x[..., ROPE_DIM:]
    return jnp.concatenate([x1 * cos - x2 * sin, x2 * cos + x1 * sin, rest], axis=-1)


def conv_mixer(a, g, w_dw, b_dw, ln_g, ln_b):
    u = a * jax.nn.sigmoid(g)
    c = u.shape[-1]
    u = lax.conv_general_dilated(
        u, w_dw.astype(u.dtype)[:, None, :], window_strides=(1,),
        padding=[(CONV_KERNEL - 1, 0)], dimension_numbers=("NWC", "WIO", "NWC"),
        feature_group_count=c) + b_dw.astype(u.dtype)
    return jax.nn.silu(layernorm(u, ln_g, ln_b))


def moba_attention(q, k, v):
    B, H, S, Dh = q.shape
    nb = -(-S // MOBA_BLOCK)
    s_pad = nb * MOBA_BLOCK
    topk = min(MOBA_TOPK, nb)
    pad = ((0, 0), (0, 0), (0, s_pad - S), (0, 0))
    kp = jnp.pad(k, pad)
    vp = jnp.pad(v, pad)
    k_blk = kp.reshape(B, H, nb, MOBA_BLOCK, Dh)
    v_blk = vp.reshape(B, H, nb, MOBA_BLOCK, Dh)
    k_mean = jnp.mean(k_blk.astype(jnp.float32), axis=3)
    scale = Dh ** -0.5
    n_chunks = S // QUERY_CHUNK
    q_chunks = q.reshape(B, H, n_chunks, QUERY_CHUNK, Dh).transpose(2, 0, 1, 3, 4)
    b_idx = jnp.arange(B)[:, None, None, None]
    h_idx = jnp.arange(H)[None, :, None, None]
    neg = jnp.finfo(jnp.float32).min

    def one_chunk(args):
        qc, c = args
        q_start = c * QUERY_CHUNK
        own = q_start // MOBA_BLOCK
        q_pos = q_start + jnp.arange(QUERY_CHUNK)
        gate = jnp.einsum('bhqd,bhnd->bhqn', qc.astype(jnp.float32), k_mean)
        gate = jnp.where(jnp.arange(nb) < own, gate, neg)
        _, sel = lax.top_k(gate, topk)
        slot_valid = jnp.arange(topk) < own
        k_sel = k_blk[b_idx, h_idx, sel]
        v_sel = v_blk[b_idx, h_idx, sel]
        s_sel = jnp.einsum('bhqd,bhqtkd->bhqtk', qc, k_sel).astype(jnp.float32) * scale
        s_sel = jnp.where(slot_valid[:, None], s_sel, neg)
        s_sel = s_sel.reshape(B, H, QUERY_CHUNK, topk * MOBA_BLOCK)
        k_own = lax.dynamic_slice_in_dim(kp, own * MOBA_BLOCK, MOBA_BLOCK, axis=2)
        v_own = lax.dynamic_slice_in_dim(vp, own * MOBA_BLOCK, MOBA_BLOCK, axis=2)
        s_own = jnp.einsum('bhqd,bhkd->bhqk', qc, k_own).astype(jnp.float32) * scale
        k_pos = own * MOBA_BLOCK + jnp.arange(MOBA_BLOCK)
        s_own = jnp.where(k_pos[None, :] <= q_pos[:, None], s_own, neg)
        probs = jax.nn.softmax(jnp.concatenate([s_sel, s_own], axis=-1), axis=-1)
        p_sel = probs[..., :topk * MOBA_BLOCK].reshape(B, H, QUERY_CHUNK, topk, MOBA_BLOCK)
        p_own = probs[..., topk * MOBA_BLOCK:]
        return (jnp.einsum('bhqtk,bhqtkd->bhqd', p_sel.astype(v.dtype), v_sel)
                + jnp.einsum('bhqk,bhkd->bhqd', p_own.astype(v.dtype), v_own))

    out = lax.map(one_chunk, (q_chunks, jnp.arange(n_chunks)))
    return out.transpose(1, 2, 0, 3, 4).reshape(B, H, S, Dh)


def setup_inputs(seed: int = 0) -> dict:
    key = jax.random.key(seed)
    ks = jax.random.split(key, 20)
    f32 = jnp.float32
    nrm = lambda k, shape, s: jax.random.normal(k, shape, f32) * s
    gain = lambda k, shape: 1.0 + 0.05 * jax.random.normal(k, shape, f32)
    return {
        "x": jax.random.normal(ks[0], (BATCH, SEQ, D_MODEL), f32),
        "p": jax.random.normal(ks[1], (DEPTH, BATCH, SEQ, PLE_DIM), f32),
        "positions": jnp.broadcast_to(jnp.arange(SEQ, dtype=jnp.int32), (BATCH, SEQ)),
        "norm_mix_g": gain(ks[2], (DEPTH, D_MODEL)),
        "w_in": nrm(ks[3], (DEPTH, D_MODEL, IN_WIDTH), D_MODEL ** -0.5),
        "conv_w": nrm(ks[4], (DEPTH, CONV_KERNEL, CONV_CH), CONV_KERNEL ** -0.5),
        "conv_b": nrm(ks[5], (DEPTH, CONV_CH), 0.02),
        "conv_ln_g": gain(ks[6], (DEPTH, CONV_CH)),
        "conv_ln_b": nrm(ks[7], (DEPTH, CONV_CH), 0.02),
        "w_out": nrm(ks[8], (DEPTH, MIX_WIDTH, D_MODEL), MIX_WIDTH ** -0.5),
        "norm_ffn_g": gain(ks[9], (DEPTH, D_MODEL)),
        "w_ffn_up": nrm(ks[10], (DEPTH, D_MODEL, 2 * D_FF), D_MODEL ** -0.5),
        "w_ffn_down": nrm(ks[11], (DEPTH, D_FF, D_MODEL), D_FF ** -0.5),
        "norm_ple_g": gain(ks[12], (DEPTH, D_MODEL)),
        "w_ple_gate": nrm(ks[13], (DEPTH, D_MODEL, D_MODEL), D_MODEL ** -0.5),
        "w_ple_proj": nrm(ks[14], (DEPTH, PLE_DIM, D_MODEL), PLE_DIM ** -0.5),
        "final_norm_g": gain(ks[15], (D_MODEL,)),
    }


def reference(x, p, positions, norm_mix_g, w_in, conv_w, conv_b, conv_ln_g, conv_ln_b,
              w_out, norm_ffn_g, w_ffn_up, w_ffn_down, norm_ple_g, w_ple_gate,
              w_ple_proj, final_norm_g):
    B, S, _ = x.shape
    cos, sin = rope_tables(positions, x.dtype)
    splits = [CONV_CH, 2 * CONV_CH, 2 * CONV_CH + ATTN_WIDTH, 2 * CONV_CH + 2 * ATTN_WIDTH]
    to_heads = lambda t: t.reshape(B, S, ATTN_HEADS, HEAD_DIM).transpose(0, 2, 1, 3)
    h = x
    for i in range(DEPTH):
        hn = rmsnorm(h, norm_mix_g[i])
        z = hn @ w_in[i]
        a, g, q, k, v = jnp.split(z, splits, axis=-1)
        conv_out = conv_mixer(a, g, conv_w[i], conv_b[i], conv_ln_g[i], conv_ln_b[i])
        q = apply_partial_rope(to_heads(q), cos, sin)
        k = apply_partial_rope(to_heads(k), cos, sin)
        attn = moba_attention(q, k, to_heads(v))
        attn = attn.transpose(0, 2, 1, 3).reshape(B, S, ATTN_WIDTH)
        h = h + jnp.concatenate([conv_out, attn], axis=-1) @ w_out[i]
        hn = rmsnorm(h, norm_ffn_g[i])
        gt, up = jnp.split(hn @ w_ffn_up[i], 2, axis=-1)
        h = h + (jax.nn.silu(gt) * up) @ w_ffn_down[i]
        gate = jax.nn.sigmoid(rmsnorm(h, norm_ple_g[i]) @ w_ple_gate[i])
        h = h + gate * (p[i].astype(h.dtype) @ w_ple_proj[i])
    return rmsnorm(h, final_norm_g)
```

```python
import math
import numpy as np
import concourse.bass as bass
import concourse.mybir as mybir
from concourse.bass_utils import run_bass_kernel_spmd

F32 = mybir.dt.float32
BF16 = mybir.dt.bfloat16
I32 = mybir.dt.int32
AF = mybir.ActivationFunctionType
ALU = mybir.AluOpType
AX = mybir.AxisListType

T = 2048
D = 1024
NT = 16
DFF = 2816
EPS = 1e-6
NEG = -480.0
BASE = 16512
SB_LIMIT = 229344 - BASE
TWO_PI = 2.0 * math.pi


class Res:
    __slots__ = ("w", "r")

    def __init__(self):
        self.w = None
        self.r = {}


class Eng:
    def __init__(self, nc, eng, name):
        self.e = eng
        self.key = name
        self.sem = nc.alloc_semaphore(name + "_cnt")
        self.n = 0
        self.seen = {}

    def wait(self, *ts):
        for t in ts:
            if t is None:
                continue
            if isinstance(t, (list, tuple)) and len(t) and isinstance(t[0], (list, tuple)):
                self.wait(*t)
                continue
            sem, n, key = t
            if self.seen.get(key, 0) >= n:
                continue
            self.seen[key] = n
            self.e.wait_ge(sem, n)

    def mark(self, inst):
        self.n += 1
        inst.then_inc(self.sem, 1)
        return (self.sem, self.n, self.key)

    def last(self):
        return (self.sem, self.n, self.key) if self.n else None


class DSem:
    def __init__(self, nc, name):
        self.sem = nc.alloc_semaphore(name)
        self.cnt = 0
        self.key = name


def op(E, fn, reads=(), writes=(), extra=()):
    for r in reads:
        E.wait(r.w)
    for w in writes:
        E.wait(w.w)
        E.wait(*w.r.values())
    E.wait(*extra)
    t = E.mark(fn())
    for r in reads:
        r.r[E.key] = t
    for w in writes:
        w.w = t
        w.r = {}
    return t


def dma(Q, ds, out_ap, in_ap, reads=(), writes=(), extra=(), nowait_w=False):
    for r in reads:
        Q.wait(r.w)
    for w in writes:
        if not nowait_w:
            Q.wait(w.w)
        Q.wait(*w.r.values())
    Q.wait(*extra)
    inst = Q.e.dma_start(out=out_ap, in_=in_ap)
    ds.cnt += 16
    inst.then_inc(ds.sem, 16)
    t = (ds.sem, ds.cnt, ds.key)
    for r in reads:
        r.r[ds.key] = t
    for w in writes:
        w.w = t
        w.r = {}
    return t


def build(debug=False):
    nc = bass.Bass("TRN2", target_bir_lowering=False)
    dt_in = lambda name, shape, dt=F32: nc.dram_tensor(name, shape, dt, kind="ExternalInput")
    x_h = dt_in("x", [T, D])
    p_h = dt_in("p", [T, 256])
    pos_h = dt_in("pos", [T], I32)
    gains_h = dt_in("gains", [4, D])
    w_in_h = dt_in("w_in", [D, 2560])
    conv_w_h = dt_in("conv_w", [31, 512])
    cvec_h = dt_in("cvec", [3, 512])
    w_out_h = dt_in("w_out", [D, D])
    w_up_h = dt_in("w_up", [D, 2 * DFF])
    w_down_h = dt_in("w_down", [DFF, D])
    w_gate_h = dt_in("w_gate", [D, D])
    w_ple_h = dt_in("w_ple", [256, D])
    out_h = nc.dram_tensor("out", [T, D], F32, kind="ExternalOutput")
    x_d, p_d, out_d = x_h.ap(), p_h.ap(), out_h.ap()
    dbg = {}

    def sb(name, shape, dt, off):
        assert off % 32 == 0, (name, off)
        nbytes = int(np.prod(shape[1:])) * (4 if dt in (F32, I32) else 2)
        assert off + nbytes <= SB_LIMIT, (name, off, nbytes)
        return nc.alloc_sbuf_tensor_at(name, list(shape), dt, offset=BASE + off)

    PE = Eng(nc, nc.tensor, "pe")
    ACT = Eng(nc, nc.scalar, "act")
    DVE = Eng(nc, nc.vector, "dve")
    POOL = Eng(nc, nc.gpsimd, "pool")
    SP = Eng(nc, nc.sync, "sp")
    engines = [PE, ACT, DVE, POOL]

    def barrier(also=()):
        ts = [e.last() for e in engines]
        for e in engines:
            e.wait(*ts)
        for q in also:
            q.wait(*ts)
        return ts

    dbl = [nc.alloc_psum_tensor("pd%d" % i, [128, 1024], F32) for i in range(4)]
    bank_res = [Res() for _ in range(8)]
    free_banks = list(range(7))

    def bank_get():
        return free_banks.pop(0)

    def bank_put(i):
        free_banks.append(i)

    def bf(i):
        return dbl[i // 2][:, (i % 2) * 512:(i % 2 + 1) * 512]

    def bb(i):
        return bf(i).bitcast(BF16)

    def bres(i):
        return bank_res[i]

    ident = sb("ident", [128, 128], BF16, 0)
    tri = sb("tri", [128, 128], BF16, 256)
    ones512 = sb("ones512", [128, 128], BF16, 512)
    blk = sb("blk", [8, 8, 128], BF16, 768)
    invf = sb("invf", [128, 8], F32, 2816)
    epst = sb("epst", [128, 1], F32, 2848)
    posi = sb("posi", [128, 16], I32, 2880)
    posf = sb("posf", [128, 16], F32, 2944)
    COS = sb("cos", [128, 16, 8], F32, 3008)
    SIN = sb("sin", [128, 16, 8], F32, 3520)
    ANG = sb("ang", [128, 16, 8], F32, 4032)
    ssq = sb("ssq", [128, 16], F32, 4544)
    std = sb("std", [128, 16], F32, 4608)
    rstd = sb("rstd", [128, 16], F32, 4672)
    cvec = sb("cvec", [128, 12], F32, 4736)
    cvh = sb("cvh", [128, 8], F32, 4800)
    wT = sb("wT", [128, 124], BF16, 4832)
    KM = sb("km", [128, 4, 8], F32, 5088)
    KMD = sb("kmd", [128, 4, 16], BF16, 5216)
    ssq2 = sb("ssq2", [128, 16], F32, 5344 + 32)
    G0 = sb("g0", [128, D], F32, 5440)
    G1 = sb("g1", [128, D], F32, 9536)
    MIXO = 13632
    MIX = sb("mix", [128, 8, T], BF16, MIXO)
    ANG2 = sb("ang2", [128, 256], F32, MIXO)
    QQ = sb("qq", [128, 256], F32, MIXO + 1024)
    KI = sb("ki", [128, 256], I32, MIXO + 2048)
    KF = sb("kf", [128, 256], F32, MIXO + 3072)
    MM = sb("mm", [128, 256], F32, MIXO + 4096)
    OV = MIXO + 32768
    assert OV % 32 == 0

    r_const = Res()
    r_g = [Res(), Res()]
    d_g = [DSem(nc, "dg0"), DSem(nc, "dg1")]
    G = [G0, G1]

    def load_gain(idx, slot, extra=()):
        src = bass.AP(gains_h, idx * D, [[0, 128], [1, D]])
        return dma(SP, d_g[slot], G[slot][:], src, writes=[r_g[slot]], extra=extra)


    XS = [sb("xs%d" % i, [128, D], F32, OV + 131328 + 4096 * i) for i in range(3)] + [sb("xs3", [128, D], F32, OV + 161024)]
    r_xs = [Res() for _ in range(4)]
    d_xs = [DSem(nc, "dxs%d" % i) for i in range(4)]
    for tt in range(4):
        dma(SP, d_xs[tt], XS[tt][:], x_d[tt * 128:(tt + 1) * 128, :], writes=[r_xs[tt]])

    d_small = DSem(nc, "dsmall")
    r_small = Res()
    with nc.allow_non_contiguous_dma(reason="tiny param gathers"):
        for q4 in range(4):
            t_sm = dma(SP, d_small, posi[:, q4 * 4:(q4 + 1) * 4],
                       bass.AP(pos_h, q4 * 512, [[1, 128], [128, 4]]), nowait_w=True)
        for v in range(3):
            t_sm = dma(SP, d_small, cvec[:, v * 4:(v + 1) * 4],
                       bass.AP(cvec_h, v * 512, [[1, 128], [128, 4]]), nowait_w=True)
    r_small.w = t_sm
    load_gain(0, 0)

    WIN = sb("win", [128, 8, 2560], BF16, OV + 73984)
    w_in_v = w_in_h.ap().rearrange("(kc p) n -> p kc n", p=128)
    r_win = [Res() for _ in range(10)]
    d_win = [DSem(nc, "dwin%d" % i) for i in range(10)]
    for i in (0, 2, 1, 3, 4, 5, 6, 7, 8, 9):
        dma(POOL, d_win[i], WIN[:, :, i * 256:(i + 1) * 256], w_in_v[:, :, i * 256:(i + 1) * 256], writes=[r_win[i]])

    op(POOL, lambda: nc.gpsimd.memset(ident[:], 1.0), writes=[r_const])
    op(POOL, lambda: nc.gpsimd.affine_select(out=ident[:], in_=ident[:], pattern=[[-1, 128]],
                                             compare_op=ALU.is_equal, fill=0.0, base=0, channel_multiplier=1),
       writes=[r_const])
    r_tri = Res()
    op(POOL, lambda: nc.gpsimd.memset(tri[:], NEG), writes=[r_tri])
    op(POOL, lambda: nc.gpsimd.affine_select(out=tri[:], in_=tri[:], pattern=[[-1, 128]],
                                             compare_op=ALU.is_gt, fill=0.0, base=0, channel_multiplier=1),
       writes=[r_tri])
    r_blk = Res()
    op(POOL, lambda: nc.gpsimd.memset(blk[:], 1.0), writes=[r_blk])
    op(POOL, lambda: nc.gpsimd.affine_select(out=blk[:], in_=blk[:], pattern=[[-1, 8], [0, 128]],
                                             compare_op=ALU.is_equal, fill=0.0, base=0, channel_multiplier=1),
       writes=[r_blk])
    r_misc = Res()
    op(DVE, lambda: nc.vector.memset(ones512[:], 1.0 / 512.0), writes=[r_misc])
    op(DVE, lambda: nc.vector.memset(epst[:], EPS), writes=[r_misc])
    for f in range(8):
        val = 500000.0 ** (-(2.0 * f) / 16.0)
        op(DVE, lambda f=f, val=val: nc.vector.memset(invf[:, f:f + 1], float(np.float32(val))), writes=[r_misc])
    op(DVE, lambda: nc.vector.memset(ssq[:], 0.0), writes=[r_misc])
    op(DVE, lambda: nc.vector.memset(ssq2[:], 0.0), writes=[r_misc])

    r_rope = Res()

    QT = sb("qt", [128, 4, T], BF16, OV + 0)
    KT = sb("kt", [128, 4, T], BF16, OV + 16384)
    VS = sb("vs", [128, 16, 4, 192], BF16, OV + 32768)
    UT = sb("ut", [128, 4, 2080], BF16, OV + 57344)
    HNT = [sb("hnt%d" % i, [128, 8, 512], BF16, OV + 114944 + 8192 * i) for i in range(2)]
    XN = [sb("xn%d" % i, [128, D], BF16, OV + 143616 + 2048 * i) for i in range(2)]
    QTOK = sb("qtok", [128, 4, 512], BF16, OV + 147712)
    KTOK = sb("ktok", [128, 4, 512], BF16, OV + 151808)
    THA = [sb("tha%d" % i, [128, 512], F32, OV + 155904 + 2048 * i) for i in range(2)]
    RTMP = [sb("rtmp%d" % i, [128, 8, 8], F32, OV + 160000 + 256 * i) for i in range(4)]

    r_qt = [Res() for _ in range(4)]
    r_kt = [Res() for _ in range(4)]
    r_vs = Res()
    r_ut = Res()
    r_hnt = [Res(), Res()]
    r_ssq_g = [Res() for _ in range(4)]
    r_rstd_g = [Res() for _ in range(4)]
    r_xn = [Res(), Res()]
    r_qtok = Res()
    r_ktok = Res()
    r_tha = [Res(), Res()]
    r_rtmp = [Res() for _ in range(4)]
    r_junk = Res()
    r_ssq = Res()
    r_rstd = Res()

    d_cw = DSem(nc, "dcw")

    op(DVE, lambda: nc.vector.memset(VS[:, :, :, 64:128], 1.0), writes=[r_vs])
    op(DVE, lambda: nc.vector.memset(UT[:, :, 0:30], 0.0), writes=[r_ut])

    def sumsq_pass(src_fn, nslots_res, t_list):
        pass

    def pass1(g):
        cs = slice(4 * g, 4 * g + 4)
        for tt in range(4):
            t = 4 * g + tt
            if g > 0:
                dma(SP, d_xs[tt], XS[tt][:], x_d[t * 128:(t + 1) * 128, :], writes=[r_xs[tt]])
            op(ACT, lambda tt=tt, t=t: nc.scalar.activation(out=bf(7)[:, :], in_=XS[tt][:, 0:512], func=AF.Square, accum_out=ssq[:, t:t + 1]),
               reads=[r_xs[tt], r_misc], writes=[r_ssq_g[g]])
            op(ACT, lambda tt=tt, t=t: nc.scalar.activation(out=bf(7)[:, :], in_=XS[tt][:, 512:1024], func=AF.Square, accum_out=ssq2[:, t:t + 1]),
               reads=[r_xs[tt]], writes=[r_ssq_g[g]])
        op(DVE, lambda: nc.vector.tensor_tensor(out=ssq[:, cs], in0=ssq[:, cs], in1=ssq2[:, cs], op=ALU.add), writes=[r_ssq_g[g]])
        op(ACT, lambda: nc.scalar.activation(out=std[:, cs], in_=ssq[:, cs], func=AF.Sqrt, scale=1.0 / D, bias=epst[:, 0:1]),
           reads=[r_ssq_g[g], r_misc], writes=[r_rstd_g[g]])
        op(DVE, lambda: nc.vector.reciprocal(out=rstd[:, cs], in_=std[:, cs]), writes=[r_rstd_g[g]])
        op(DVE, lambda: nc.vector.memset(ssq[:, cs], 0.0), reads=[r_rstd_g[g]], writes=[r_ssq_g[g]])
        op(DVE, lambda: nc.vector.memset(ssq2[:, cs], 0.0), writes=[r_ssq_g[g]])

    def finish_rstd():
        op(DVE, lambda: nc.vector.tensor_tensor(out=ssq[:], in0=ssq[:], in1=ssq2[:], op=ALU.add), writes=[r_ssq])
        op(ACT, lambda: nc.scalar.activation(out=std[:], in_=ssq[:], func=AF.Sqrt, scale=1.0 / D, bias=epst[:, 0:1]),
           reads=[r_ssq, r_misc], writes=[r_rstd])
        op(DVE, lambda: nc.vector.reciprocal(out=rstd[:], in_=std[:]), writes=[r_rstd])
        op(DVE, lambda: nc.vector.memset(ssq[:], 0.0), reads=[r_rstd], writes=[r_ssq])
        op(DVE, lambda: nc.vector.memset(ssq2[:], 0.0), writes=[r_ssq])

    def norm_tile(src_ap, src_res, t, gslot, dst_ap, dst_res, xslot, rres=None):
        op(DVE, lambda: nc.vector.scalar_tensor_tensor(out=XN[xslot][:], in0=src_ap, scalar=rstd[:, t:t + 1], in1=G[gslot][:],
                                                      op0=ALU.mult, op1=ALU.mult),
           reads=[src_res, rres if rres is not None else r_rstd, r_g[gslot]], writes=[r_xn[xslot]])
        b = bank_get()

        def tr():
            last = None
            for kc in range(8):
                last = nc.tensor.transpose(bb(b)[:, kc * 128:(kc + 1) * 128], XN[xslot][:, kc * 128:(kc + 1) * 128], ident[:])
            return last
        op(PE, tr, reads=[r_xn[xslot], r_const], writes=[bres(b)])
        op(ACT, lambda: nc.scalar.activation(out=dst_ap, in_=bb(b)[:, 0:1024].rearrange("p (a b) -> p a b", a=8), func=AF.Copy),
           reads=[bres(b)], writes=[dst_res])
        bank_put(b)

    def rope(ps_i, dst, dst_res, t, tt):
        src = bf(ps_i)[:, :].rearrange("p (h d) -> p h d", h=8)
        dv = dst[:, tt, :].rearrange("p (h d) -> p h d", h=8)
        op(ACT, lambda: nc.scalar.activation(out=dv[:, :, 16:64], in_=src[:, :, 16:64], func=AF.Copy),
           reads=[bres(ps_i)], writes=[])
        cb = bass.AP(COS, t * 8, [[128, 128], [0, 8], [1, 8]])
        snb = bass.AP(SIN, t * 8, [[128, 128], [0, 8], [1, 8]])
        x1 = src[:, :, 0:8]
        x2 = src[:, :, 8:16]
        op(DVE, lambda: nc.vector.tensor_tensor(out=RTMP[0][:], in0=x1, in1=cb, op=ALU.mult), reads=[bres(ps_i), r_rope], writes=[r_rtmp[0]])
        op(DVE, lambda: nc.vector.tensor_tensor(out=RTMP[1][:], in0=x2, in1=snb, op=ALU.mult), reads=[bres(ps_i)], writes=[r_rtmp[1]])
        op(DVE, lambda: nc.vector.tensor_tensor(out=RTMP[2][:], in0=x2, in1=cb, op=ALU.mult), reads=[bres(ps_i)], writes=[r_rtmp[2]])
        op(DVE, lambda: nc.vector.tensor_tensor(out=RTMP[3][:], in0=x1, in1=snb, op=ALU.mult), reads=[bres(ps_i)], writes=[r_rtmp[3]])
        op(DVE, lambda: nc.vector.tensor_tensor(out=dv[:, :, 0:8], in0=RTMP[0][:], in1=RTMP[1][:], op=ALU.subtract),
           reads=[r_rtmp[0], r_rtmp[1]], writes=[])
        return op(DVE, lambda: nc.vector.tensor_tensor(out=dv[:, :, 8:16], in0=RTMP[2][:], in1=RTMP[3][:], op=ALU.add),
                  reads=[r_rtmp[2], r_rtmp[3]], writes=[])

    def mm_group(out_ap, pairs, reads, bank_i, extra=()):
        def f():
            last = None
            n = len(pairs)
            for i, (l, r) in enumerate(pairs):
                last = nc.tensor.matmul(out_ap, l, r, start=(i == 0), stop=(i == n - 1))
            return last
        return op(PE, f, reads=reads, writes=[bres(bank_i)], extra=extra)

    def norm_x_tile(t, gbuf):
        tt = t % 4
        norm_tile(XS[tt][:], r_xs[tt], t, 0, HNT[gbuf][:, :, tt * 128:(tt + 1) * 128], r_hnt[gbuf], t % 2, rres=r_rstd_g[t // 4])

    CWB = sb("cwb", [124, 128], BF16, OV + 165120)
    CWF2 = sb("cwf2", [124, 128], F32, OV + 165376)
    r_cw = Res()
    cw_v = conv_w_h.ap().rearrange("j (c p) -> (j c) p", p=128)
    dma(SP, d_cw, CWF2[:], cw_v, writes=[r_cw])

    op(DVE, lambda: nc.vector.tensor_scalar(out=CWB[:], in0=CWF2[:], scalar1=0.5, scalar2=None, op0=ALU.mult), reads=[r_cw], writes=[r_cw])

    def conv_weight_prep():
        b = bank_get()
        op(PE, lambda: nc.tensor.transpose(bb(b)[:, 0:124], CWB[:], ident[0:124, 0:124]), reads=[r_cw, r_const], writes=[bres(b)])
        op(ACT, lambda: nc.scalar.activation(out=wT[:], in_=bb(b)[:, 0:124], func=AF.Copy), reads=[bres(b)], writes=[r_cw])
        bank_put(b)

    def build_rope_tables():
        op(DVE, lambda: nc.vector.tensor_copy(out=posf[:], in_=posi[:]), reads=[r_small], writes=[r_rope])
        posb = bass.AP(posf, 0, [[16, 128], [1, 16], [0, 8]])
        invb = bass.AP(invf, 0, [[8, 128], [0, 16], [1, 8]])
        op(DVE, lambda: nc.vector.tensor_tensor(out=ANG[:], in0=posb, in1=invb, op=ALU.mult), reads=[r_misc], writes=[r_rope])
        angf = ANG[:].rearrange("p a b -> p (a b)")
        op(DVE, lambda: nc.vector.tensor_copy(out=ANG2[:, 0:128], in_=angf), writes=[r_rope])
        op(DVE, lambda: nc.vector.tensor_scalar(out=ANG2[:, 128:256], in0=angf, scalar1=0.5 * math.pi, scalar2=None, op0=ALU.add), writes=[r_rope])
        op(DVE, lambda: nc.vector.tensor_scalar(out=QQ[:], in0=ANG2[:], scalar1=1.0 / TWO_PI, scalar2=None, op0=ALU.mult), writes=[r_rope])
        op(DVE, lambda: nc.vector.tensor_copy(out=KI[:], in_=QQ[:]), writes=[r_rope])
        op(DVE, lambda: nc.vector.tensor_copy(out=KF[:], in_=KI[:]), writes=[r_rope])
        op(DVE, lambda: nc.vector.scalar_tensor_tensor(out=QQ[:], in0=KF[:], scalar=-TWO_PI, in1=ANG2[:], op0=ALU.mult, op1=ALU.add), writes=[r_rope])
        op(DVE, lambda: nc.vector.tensor_scalar(out=MM[:], in0=QQ[:], scalar1=math.pi, scalar2=None, op0=ALU.is_gt), writes=[r_rope])
        op(DVE, lambda: nc.vector.scalar_tensor_tensor(out=ANG2[:], in0=MM[:], scalar=-TWO_PI, in1=QQ[:], op0=ALU.mult, op1=ALU.add), writes=[r_rope])
        op(DVE, lambda: nc.vector.tensor_scalar(out=ANG2[:], in0=ANG2[:], scalar1=-math.pi, scalar2=math.pi, op0=ALU.max, op1=ALU.min), writes=[r_rope])
        op(ACT, lambda: nc.scalar.activation(out=SIN[:].rearrange("p a b -> p (a b)"), in_=ANG2[:, 0:128], func=AF.Sin), reads=[r_rope], writes=[r_rope])
        op(ACT, lambda: nc.scalar.activation(out=COS[:].rearrange("p a b -> p (a b)"), in_=ANG2[:, 128:256], func=AF.Sin), writes=[r_rope])
        op(DVE, lambda: nc.vector.tensor_scalar(out=cvh[:], in0=cvec[:, 4:12], scalar1=0.5, scalar2=None, op0=ALU.mult), reads=[r_small], writes=[r_misc])


    pe_win_done = [None]
    pass1(0)
    for t in range(4):
        norm_x_tile(t, 0)
    build_rope_tables()
    load_gain(1, 1)

    for g in range(4):
        gb = g % 2
        hn = HNT[gb]
        if g < 3:
            pass1(g + 1)
        if g == 2:
            conv_weight_prep()
        for c in range(4):
            ba = bank_get()
            bg = bank_get()
            ia, ig = c // 2, 2 + c // 2
            mm_group(bf(ba)[:, :], [(WIN[:, kc, c * 128:(c + 1) * 128], hn[:, kc, :]) for kc in range(8)],
                     [r_hnt[gb], r_win[ia]], ba)
            mm_group(bf(bg)[:, :], [(WIN[:, kc, 512 + c * 128:512 + (c + 1) * 128], hn[:, kc, :]) for kc in range(8)],
                     [r_hnt[gb], r_win[ig]], bg)
            s = c % 2
            op(ACT, lambda: nc.scalar.activation(out=THA[s][:], in_=bf(bg)[:, :], func=AF.Tanh, scale=0.5),
               reads=[bres(bg)], writes=[r_tha[s]])
            op(DVE, lambda: nc.vector.scalar_tensor_tensor(out=UT[:, c, 30 + g * 512:30 + (g + 1) * 512], in0=THA[s][:], scalar=1.0,
                                                          in1=bf(ba)[:, :], op0=ALU.add, op1=ALU.mult),
               reads=[r_tha[s], bres(ba)], writes=[])
            bank_put(ba)
            bank_put(bg)
        for tt in range(4):
            t = 4 * g + tt
            bq, bk, bv = bank_get(), bank_get(), bank_get()
            lh = [hn[:, kc, tt * 128:(tt + 1) * 128] for kc in range(8)]
            mm_group(bf(bq)[:, :], [(lh[kc], WIN[:, kc, 1024:1536]) for kc in range(8)], [r_hnt[gb], r_win[4], r_win[5]], bq)
            mm_group(bf(bk)[:, :], [(lh[kc], WIN[:, kc, 1536:2048]) for kc in range(8)], [r_hnt[gb], r_win[6], r_win[7]], bk)
            pe_win_done[0] = mm_group(bf(bv)[:, :], [(lh[kc], WIN[:, kc, 2048:2560]) for kc in range(8)], [r_hnt[gb], r_win[8], r_win[9]], bv)
            vout = bass.AP(VS, t * 768, [[16 * 768, 128], [192, 4], [128, 2], [1, 64]])
            vin = bf(bv)[:, :].rearrange("p (a b c) -> p a b c", a=4, b=2)
            op(ACT, lambda: nc.scalar.activation(out=vout, in_=vin, func=AF.Copy), reads=[bres(bv)], writes=[])
            if tt == 0:
                PE_prev = r_qtok.r.get("pe"), r_ktok.r.get("pe")
                ACT.wait(*[x for x in PE_prev if x])
                DVE.wait(*[x for x in PE_prev if x])
            tq = rope(bq, QTOK, r_qtok, t, tt)
            tk = rope(bk, KTOK, r_ktok, t, tt)
            bank_put(bq)
            bank_put(bk)
            bank_put(bv)
            if g < 3:
                norm_x_tile(4 * (g + 1) + tt, (g + 1) % 2)
        r_qtok.w = DVE.last()
        r_ktok.w = DVE.last()
        act_last = ACT.last()
        for (src, srcres, dstT, dstres) in ((QTOK, r_qtok, QT, r_qt), (KTOK, r_ktok, KT, r_kt)):
            for hp in range(4):
                b = bank_get()

                def tr(src=src, hp=hp, b=b):
                    last = None
                    for tt in range(4):
                        last = nc.tensor.transpose(bb(b)[:, tt * 128:(tt + 1) * 128], src[:, tt, hp * 128:(hp + 1) * 128], ident[:])
                    return last
                op(PE, tr, reads=[srcres, r_const], writes=[bres(b)], extra=[act_last])
                op(ACT, lambda dstT=dstT, hp=hp, b=b: nc.scalar.activation(out=dstT[:, hp, g * 512:(g + 1) * 512], in_=bb(b)[:, 0:512], func=AF.Copy),
                   reads=[bres(b)], writes=[])
                bank_put(b)
    a1_act, a1_dve, a1_pe = ACT.last(), DVE.last(), PE.last()
    for e in (ACT, DVE, POOL, SP):
        e.wait(pe_win_done[0], a1_act, a1_dve)
    PE.wait(a1_act, a1_dve)
    for r_ in r_qt + r_kt + [r_vs, r_ut]:
        r_.w = None
        r_.r = {}

    if debug:
        d_dbg = DSem(nc, "ddbg")

        def dump(name, ap, shape, dt):
            h = nc.dram_tensor("dbg_" + name, list(shape), dt, kind="ExternalOutput")
            dbg[name] = h
            ts = barrier(also=[SP])
            dma(SP, d_dbg, h.ap(), ap)
            SP.wait((d_dbg.sem, d_dbg.cnt, d_dbg.key))
    else:
        def dump(*a, **k):
            pass

    dump("qt", QT[:].rearrange("p a b -> p (a b)"), [128, 4 * T], BF16)
    dump("kt", KT[:].rearrange("p a b -> p (a b)"), [128, 4 * T], BF16)
    dump("vs", VS[:].rearrange("p a b c -> p (a b c)"), [128, 16 * 768], BF16)
    dump("ut", UT[:].rearrange("p a b -> p (a b)"), [128, 4 * 2080], BF16)

    DIAG = sb("diag", [128, 4, 31, 128], BF16, OV + 73984)
    CF = [sb("cf%d" % i, [128, 4, 512], F32, OV + 105728 + 8192 * i) for i in range(2)]
    CB = sb("cb", [128, 4, 512], BF16, OV + 122112)
    CB2 = sb("cb2", [128, 4, 512], BF16, OV + 126208)
    TT_ = [sb("ttmp%d" % i, [128, 512], F32, OV + 130304 + 2048 * i) for i in range(2)]
    YH = [sb("yh%d" % i, [128, 512], F32, OV + 134400 + 2048 * i) for i in range(2)]
    THC = [sb("thc%d" % i, [128, 512], F32, OV + 138496 + 2048 * i) for i in range(2)]
    MSB = sb("msb", [128, 512], F32, OV + 142592)
    MEANS = [sb("means%d" % i, [128, 512], F32, OV + 144640 + 2048 * i) for i in range(2)]
    RS = [sb("rs%d" % i, [128, 512], F32, OV + 148736 + 2048 * i) for i in range(2)]
    r_diag = [Res() for _ in range(4)]
    r_cf = [[Res() for _ in range(4)] for _ in range(2)]
    r_cb = [Res() for _ in range(4)]
    r_cb2 = [Res() for _ in range(4)]
    r_tt = [Res(), Res()]
    r_yh = [Res(), Res()]
    r_thc = [Res(), Res()]
    r_msb = Res()
    r_means = [Res(), Res()]
    r_rs = [Res(), Res()]
    for c in range(4):
        idb = bass.AP(ident, 0, [[128, 128], [0, 31], [1, 128]])
        wb_ = bass.AP(wT, c, [[124, 128], [4, 31], [0, 128]])
        op(DVE, lambda c=c, idb=idb, wb_=wb_: nc.vector.tensor_tensor(out=DIAG[:, c, :, :], in0=idb, in1=wb_, op=ALU.mult),
           reads=[r_cw, r_const], writes=[r_diag[c]])

    def conv_mm(g, c):
        gs = g % 2
        b = bank_get()
        mm_group(bf(b)[:, :], [(DIAG[:, c, j, :], UT[:, c, g * 512 + j:g * 512 + j + 512]) for j in range(31)],
                 [r_diag[c], r_ut], b)
        cfs = CF[gs][:, c, :]
        op(ACT, lambda: nc.scalar.activation(out=cfs, in_=bf(b)[:, :], func=AF.Identity, bias=cvec[:, c:c + 1]),
           reads=[bres(b), r_small], writes=[r_cf[gs][c]])
        bank_put(b)
        op(DVE, lambda: nc.vector.tensor_copy(out=CB[:, c, :], in_=cfs), reads=[r_cf[gs][c]], writes=[r_cb[c]])
        op(ACT, lambda: nc.scalar.activation(out=CB2[:, c, :], in_=cfs, func=AF.Square), reads=[r_cf[gs][c]], writes=[r_cb2[c]])

    def conv_stats(g, fixed_banks=None):
        gs = g % 2
        bm, bs = fixed_banks if fixed_banks is not None else (bank_get(), bank_get())
        mm_group(bf(bm)[:, :], [(ones512[:], CB[:, c, :]) for c in range(4)], r_cb + [r_misc], bm)
        mm_group(bf(bs)[:, :], [(ones512[:], CB2[:, c, :]) for c in range(4)], r_cb2, bs)
        op(ACT, lambda: nc.scalar.activation(out=MEANS[gs][:], in_=bf(bm)[:, :], func=AF.Copy), reads=[bres(bm)], writes=[r_means[gs]])
        op(DVE, lambda: nc.vector.tensor_tensor(out=MSB[:], in0=MEANS[gs][:], in1=MEANS[gs][:], op=ALU.mult), reads=[r_means[gs]], writes=[r_msb])
        op(DVE, lambda: nc.vector.tensor_tensor(out=RS[gs][:], in0=bf(bs)[:, :], in1=MSB[:], op=ALU.subtract),
           reads=[bres(bs), r_msb], writes=[r_rs[gs]])
        if fixed_banks is None:
            bank_put(bm)
            bank_put(bs)
        op(ACT, lambda: nc.scalar.activation(out=RS[gs][:], in_=RS[gs][:], func=AF.Sqrt, bias=epst[:, 0:1]), reads=[r_misc], writes=[r_rs[gs]])
        op(DVE, lambda: nc.vector.reciprocal(out=RS[gs][:], in_=RS[gs][:]), writes=[r_rs[gs]])

    def conv_p2(g, c):
        gs = g % 2
        s_ = c % 2
        sl = slice(g * 512, (g + 1) * 512)
        op(DVE, lambda: nc.vector.tensor_tensor(out=TT_[s_][:], in0=CF[gs][:, c, :], in1=MEANS[gs][:], op=ALU.subtract),
           reads=[r_cf[gs][c], r_means[gs]], writes=[r_tt[s_]])
        op(DVE, lambda: nc.vector.tensor_tensor(out=TT_[s_][:], in0=TT_[s_][:], in1=RS[gs][:], op=ALU.mult),
           reads=[r_rs[gs]], writes=[r_tt[s_]])
        op(ACT, lambda: nc.scalar.activation(out=THC[s_][:], in_=TT_[s_][:], func=AF.Tanh, scale=cvh[:, c:c + 1], bias=cvh[:, 4 + c:5 + c]),
           reads=[r_tt[s_], r_misc], writes=[r_thc[s_]])
        op(DVE, lambda: nc.vector.tensor_scalar(out=YH[s_][:], in0=TT_[s_][:], scalar1=cvh[:, c:c + 1], scalar2=cvh[:, 4 + c:5 + c],
                                               op0=ALU.mult, op1=ALU.add),
           reads=[r_tt[s_]], writes=[r_yh[s_]])
        op(DVE, lambda: nc.vector.scalar_tensor_tensor(out=MIX[:, c, sl], in0=THC[s_][:], scalar=1.0, in1=YH[s_][:], op0=ALU.add, op1=ALU.mult),
           reads=[r_thc[s_], r_yh[s_]], writes=[])

    GATE = sb("gate", [128, 8, 64], F32, OV + 153600)
    CMP = sb("cmp", [128, 8, 8, 8], F32, OV + 155648)
    RANK = sb("rank", [128, 64], F32, OV + 157696)
    MBT = sb("mbt", [128, 8, 64], BF16, OV + 157952)
    MASKT2 = sb("maskt2", [64, 1024], BF16, OV + 163072)
    r_maskt = Res()
    r_gate = Res()
    r_cmp = Res()
    r_rank = Res()
    r_mbt = Res()
    r_km = Res()
    def gate_stage_a():
        op(DVE, lambda: nc.vector.memset(KMD[:], 0.0), writes=[r_km])
        op(DVE, lambda: nc.vector.memset(MBT[:], 0.0), writes=[r_mbt])
        for hp in range(4):
            op(DVE, lambda hp=hp: nc.vector.tensor_reduce(out=KM[:, hp, :], in_=KT[:, hp, :].rearrange("p (n k) -> p n k", n=8), axis=AX.X, op=ALU.add),
               writes=[r_km])
            op(DVE, lambda hp=hp: nc.vector.tensor_scalar(out=KMD[0:64, hp, 0:8], in0=KM[0:64, hp, :], scalar1=1.0 / 256.0, scalar2=None, op0=ALU.mult), writes=[r_km])
            op(DVE, lambda hp=hp: nc.vector.tensor_scalar(out=KMD[64:128, hp, 8:16], in0=KM[64:128, hp, :], scalar1=1.0 / 256.0, scalar2=None, op0=ALU.mult), writes=[r_km])

    def gate_stage_b():
        for ti in range(8):
            t = 8 + ti
            own = t // 2
            b = bank_get()

            def gm(t=t, b=b):
                last = None
                for hp in range(4):
                    last = nc.tensor.matmul(bf(b)[:, hp * 16:(hp + 1) * 16], QT[:, hp, t * 128:(t + 1) * 128], KMD[:, hp, :], start=True, stop=True)
                return last
            op(PE, gm, reads=[r_km], writes=[bres(b)])
            op(ACT, lambda ti=ti, b=b: nc.scalar.activation(out=GATE[:, ti, :], in_=bf(b)[:, 0:64], func=AF.Copy), reads=[bres(b)], writes=[r_gate])
            bank_put(b)
            gbase = ti * 64
            in0 = bass.AP(GATE, gbase, [[512, 128], [8, 8], [0, own], [1, own]])
            in1 = bass.AP(GATE, gbase, [[512, 128], [8, 8], [1, own], [0, own]])
            cm = bass.AP(CMP, 0, [[512, 128], [64, 8], [8, own], [1, own]])
            op(DVE, lambda in0=in0, in1=in1, cm=cm: nc.vector.tensor_tensor(out=cm, in0=in0, in1=in1, op=ALU.is_gt), reads=[r_gate], writes=[r_cmp])
            rk = bass.AP(RANK, 0, [[64, 128], [8, 8], [1, own]])
            op(DVE, lambda cm=cm, rk=rk: nc.vector.tensor_reduce(out=rk, in_=cm, axis=AX.X, op=ALU.add), reads=[r_cmp], writes=[r_rank])
            mb = bass.AP(MBT, ti * 64, [[512, 128], [8, 8], [1, own]])
            op(DVE, lambda rk=rk, mb=mb: nc.vector.tensor_scalar(out=mb, in0=rk, scalar1=2.5, scalar2=NEG, op0=ALU.is_gt, op1=ALU.mult), reads=[r_rank], writes=[r_mbt])

    BLKR = sb("blkr", [8, 8, 256], BF16, OV + 158976)
    r_blkr = Res()

    def gate_stage_c():
        op(POOL, lambda: nc.gpsimd.memset(BLKR[:], 1.0), writes=[r_blkr])
        op(POOL, lambda: nc.gpsimd.affine_select(out=BLKR[:], in_=BLKR[:], pattern=[[-1, 8], [0, 256]],
                                                 compare_op=ALU.is_equal, fill=0.0, base=0, channel_multiplier=1), writes=[r_blkr])
        for grp in range(2):
            b = bank_get()

            def tr(grp=grp, b=b):
                last = None
                for i in range(4):
                    ti = grp * 4 + i
                    last = nc.tensor.transpose(bb(b)[0:64, i * 128:(i + 1) * 128], MBT[:, ti, :], ident[:])
                return last
            op(PE, tr, reads=[r_mbt, r_const], writes=[bres(b)])
            op(ACT, lambda grp=grp, b=b: nc.scalar.activation(out=MASKT2[0:64, grp * 512:(grp + 1) * 512], in_=bb(b)[0:64, 0:512], func=AF.Copy),
               reads=[bres(b)], writes=[r_maskt])
            bank_put(b)

    KTZP = sb("ktzp", [128, T], BF16, MIXO + 6 * 4096)
    QTZP = sb("qtzp", [128, T], BF16, MIXO + 7 * 4096)
    r_ktzp, r_qtzp, r_qmp, r_kbp = Res(), Res(), Res(), Res()
    d_kbp, d_qmp = DSem(nc, "dkbp"), DSem(nc, "dqmp")

    def prebuild_head0():
        op(DVE, lambda: nc.vector.memset(KTZP[:], 0.0), writes=[r_ktzp])
        op(DVE, lambda: nc.vector.memset(QTZP[:], 0.0), writes=[r_qtzp])
        op(DVE, lambda: nc.vector.tensor_copy(out=KTZP[0:64, :], in_=KT[0:64, 0, :]), writes=[r_ktzp])
        op(DVE, lambda: nc.vector.tensor_copy(out=QTZP[0:64, :], in_=QT[0:64, 0, :]), writes=[r_qtzp])
        blkr_f0 = BLKR[:].rearrange("p a b -> p (a b)")
        SP.wait(r_ktzp.w)
        r_kbp.w = dma(SP, d_kbp, KTZP[64:72, :], blkr_f0, reads=[r_blkr], nowait_w=True)
        SP.wait(r_qtzp.w)
        dma(SP, d_qmp, QTZP[64:72, 1024:2048], MASKT2[0:8, :], reads=[r_maskt], writes=[r_qmp])

    for g in range(4):
        if g == 3:
            prebuild_head0()
        if g >= 1:
            conv_stats(g - 1)
        conv_mm(g, 0)
        conv_mm(g, 1)
        if g >= 1:
            conv_p2(g - 1, 0)
            conv_p2(g - 1, 1)
        conv_mm(g, 2)
        conv_mm(g, 3)
        if g >= 1:
            conv_p2(g - 1, 2)
            conv_p2(g - 1, 3)
        if g == 0:
            gate_stage_a()
        elif g == 1:
            gate_stage_b()
        elif g == 2:
            gate_stage_c()
    conv_last_pe = PE.last()
    conv_tail_pieces = [(lambda c=c: conv_p2(3, c)) for c in range(4)]
    conv_p2_last = [None]

    PT2 = [sb("pt%d" % i, [128, 1024], BF16, OV + 57344 + 2048 * i) for i in range(4)]
    NSB = [sb("nsb%d" % i, [128, 512], F32, OV + 65536 + 2048 * i) for i in range(3)]
    DSB = [sb("dsb%d" % i, [128, 512], F32, OV + 71680 + 2048 * i) for i in range(3)]
    r_nsb = [Res() for _ in range(3)]
    r_dsb = [Res() for _ in range(3)]
    KTZ = [sb("ktz%d" % i, [128, T], BF16, OV + 77824 + 4096 * i) for i in range(2)]
    QTZ = [sb("qtz%d" % i, [128, T], BF16, OV + 86016 + 4096 * i) for i in range(2)]
    WOUT = sb("wout", [128, 8, D], BF16, OV + 146688)
    r_pt = [Res() for _ in range(4)]
    r_ktz = [Res(), Res()]
    r_qtz = [Res(), Res()]
    r_qm = [Res(), Res()]
    d_qm = [DSem(nc, "dqm0"), DSem(nc, "dqm1")]
    r_att = Res()
    r_wout = Res()
    d_wout = DSem(nc, "dwout")
    for e in (ACT, DVE, POOL, SP):
        e.wait(conv_last_pe)

    op(POOL, lambda: nc.gpsimd.memset(KTZ[0][:], 0.0), writes=[r_ktz[0]])
    op(POOL, lambda: nc.gpsimd.memset(QTZ[0][:], 0.0), writes=[r_qtz[0]])

    def slot1_zero_fill():
        op(DVE, lambda: nc.vector.memset(KTZ[1][:], 0.0), writes=[r_ktz[1]])
        op(DVE, lambda: nc.vector.memset(QTZ[1][:], 0.0), writes=[r_qtz[1]])
        r_kb[1].w = dma(SP, d_kb[1], KTZ[1][0:8, :], blkr_f, reads=[r_blkr], writes=[r_ktz[1]])
    blkr_f = BLKR[:].rearrange("p a b -> p (a b)")
    d_kb = [DSem(nc, "dkb0"), DSem(nc, "dkb1")]
    dma(SP, d_kb[0], KTZ[0][64:72, :], blkr_f, reads=[r_blkr], writes=[r_ktz[0]])
    r_kb = [Res(), Res()]
    r_kb[0].w = r_ktz[0].w

    KTZ.append(KTZP)
    QTZ.append(QTZP)
    r_ktz.append(r_ktzp)
    r_qtz.append(r_qtzp)
    r_qm.append(r_qmp)
    r_kb.append(r_kbp)
    heads = [(hp, hh) for hp in range(4) for hh in range(2)]

    def head_setup(hi):
        hp, hh = heads[hi]
        h = 2 * hp + hh
        r0 = 64 * hh
        op(DVE, lambda: nc.vector.tensor_copy(out=KTZ[hh][r0:r0 + 64, :], in_=KT[r0:r0 + 64, hp, :]), writes=[r_ktz[hh]])
        op(DVE, lambda: nc.vector.tensor_copy(out=QTZ[hh][r0:r0 + 64, :], in_=QT[r0:r0 + 64, hp, :]), writes=[r_qtz[hh]])
        mrow = slice(64, 72) if hh == 0 else slice(0, 8)
        Q_ = SP
        Q_.wait(r_qtz[hh].w)
        dma(Q_, d_qm[hh], QTZ[hh][mrow, 1024:2048], MASKT2[8 * h:8 * h + 8, :], reads=[r_maskt], writes=[r_qm[hh]])

    assert sorted(free_banks) == list(range(7))
    units = []
    for hi, (hp, hh) in enumerate(heads):
        for qc in range(4):
            for p_ in range(2 * qc + 2):
                units.append((hi, hp, hh, qc, p_))
    LOOK = 2
    sinfo = {}
    ocount = [0]
    ncount = [0]
    r_attq = [Res() for _ in range(4)]
    ocur = {}
    setup_done = set()

    def issue_s(u):
        hi, hp, hh, qc, p_ = units[u]
        if hi not in setup_done:
            setup_done.add(hi)
            head_setup(hi)
        if qc == 2 and p_ == 0 and hi + 1 < len(heads) and (hi + 1) not in setup_done:
            setup_done.add(hi + 1)
            head_setup(hi + 1)
        sd = u % 3
        sl = 2 if hi == 0 else hh
        qb = qc * 512
        offs = []
        for j in range(2):
            kt = 2 * p_ + j
            offs.append(max(0, kt - 4 * qc) * 128)

        def f():
            last = None
            for j in range(2):
                kt = 2 * p_ + j
                off = offs[j]
                diag = kt >= 4 * qc
                last = nc.tensor.matmul(dbl[sd][:, j * 512 + off:(j + 1) * 512], KTZ[sl][:, kt * 128:(kt + 1) * 128],
                                        QTZ[sl][:, qb + off:qb + 512], start=True, stop=not diag)
                if diag:
                    last = nc.tensor.matmul(dbl[sd][:, j * 512 + off:j * 512 + off + 128], ident[:], tri[:], start=False, stop=True)
            return last
        op(PE, f, reads=[r_ktz[sl], r_qtz[sl], r_qm[sl], r_kb[sl], r_tri, r_const], writes=[bres(2 * sd), bres(2 * sd + 1)])
        sinfo[u] = (sd, offs)

    def issue_rest(u):
        hi, hp, hh, qc, p_ = units[u]
        sd, offs = sinfo.pop(u)
        slot = u % 4
        qb = qc * 512
        nkt = 4 * qc + 4
        o0 = offs[0]
        op(ACT, lambda: nc.scalar.activation(out=PT2[slot][:, o0:1024], in_=dbl[sd][:, o0:1024], func=AF.Exp, scale=0.125),
           reads=[bres(2 * sd), bres(2 * sd + 1)], writes=[r_pt[slot]])
        if p_ == 0:
            ocur[(hi, qc)] = 6 + (ocount[0] % 2)
            ocount[0] += 1
        ob = ocur[(hi, qc)]

        def pv():
            last = None
            for j in range(2):
                kt = 2 * p_ + j
                off = offs[j]
                last = nc.tensor.matmul(bf(ob)[:, off:512], VS[:, kt, hp, 64 * hh:64 * hh + 128], PT2[slot][:, j * 512 + off:(j + 1) * 512],
                                        start=(kt == 0), stop=(kt == nkt - 1))
            return last
        op(PE, pv, reads=[r_pt[slot], r_vs], writes=[bres(ob)])
        if p_ == 2 * qc + 1:
            s3 = ncount[0] % 3
            ncount[0] += 1
            num = slice(0, 64) if hh == 0 else slice(64, 128)
            den = slice(64, 128) if hh == 0 else slice(0, 64)
            op(DVE, lambda: nc.vector.tensor_copy(out=NSB[s3][:, :], in_=bf(ob)[:, :]), reads=[bres(ob)], writes=[r_nsb[s3]])
            op(DVE, lambda: nc.vector.reciprocal(out=DSB[s3][num, :], in_=NSB[s3][den, :]), reads=[r_nsb[s3]], writes=[r_dsb[s3]])
            r_attq[qc].w = op(POOL, lambda: nc.gpsimd.tensor_tensor(out=MIX[num, 4 + hp, qb:qb + 512], in0=NSB[s3][num, :], in1=DSB[s3][num, :], op=ALU.mult),
                              reads=[r_nsb[s3], r_dsb[s3]], writes=[])
            del ocur[(hi, qc)]

    setup_done.add(0)
    conv_stats(3, fixed_banks=(7, 6))
    slot1_zero_fill()
    for u in range(min(LOOK, len(units))):
        issue_s(u)
    for u in range(len(units)):
        if u + LOOK < len(units):
            issue_s(u + LOOK)
        issue_rest(u)
        if u % 2 == 1 and conv_tail_pieces:
            conv_tail_pieces.pop(0)()
            conv_p2_last[0] = DVE.last()
            if not conv_tail_pieces:
                w_out_v = w_out_h.ap().rearrange("(kc p) n -> p kc n", p=128)
                dma(POOL, d_wout, WOUT[:], w_out_v, writes=[r_wout], extra=[ACT.last(), DVE.last(), r_kb[0].w, r_kb[1].w, r_kb[2].w])
    att_done = [PE.last(), ACT.last(), DVE.last(), POOL.last()]
    SP.wait(*att_done)
    POOL.wait(*att_done)
    dump("maskt2", MASKT2[:], [64, 1024], BF16)
    dump("qtz0", QTZ[0][:], [128, T], BF16)
    dump("qtz1", QTZ[1][:], [128, T], BF16)
    dump("ktz0", KTZ[0][:], [128, T], BF16)
    dump("attt", MIX[:, 4:8, :].rearrange("p a b -> p (a b)"), [128, 4 * T], BF16)

    H = sb("h", [128, NT, D], F32, OV + 0)
    r_h = [Res() for _ in range(NT)]
    d_h = [DSem(nc, "dh%d" % i) for i in range(NT)]
    NK = [6, 6, 5, 5]
    KOFF = [0, 6, 12, 17]
    WU = [sb("wu0", [128, 8, 2, 768], BF16, OV + 65536), sb("wu1", [128, 8, 2, 768], BF16, OV + 109824)]
    WD = [sb("wd0", [128, 6, D], BF16, OV + 90112), sb("wd1", [128, 6, D], BF16, OV + 134400)]
    THF = [sb("thf%d" % i, [128, 512], F32, OV + 102400 + 2048 * i) for i in range(2)]
    ACTT = [sb("actt%d" % i, [128, 6, 512], BF16, OV + 146688 + 6144 * i) for i in range(2)]
    T1 = [sb("t1_%d" % i, [128, 512], F32, OV + 158976 + 2048 * i) for i in range(2)]
    XNB = sb("xnb", [128, D], BF16, OV + 163072)
    r_wu = [Res(), Res()]
    r_wd = [Res(), Res()]
    d_wu = [DSem(nc, "dwu0"), DSem(nc, "dwu1")]
    d_wd = [DSem(nc, "dwd0"), DSem(nc, "dwd1")]
    r_thf = [Res(), Res()]
    r_actt = [Res(), Res()]
    r_t1 = [Res(), Res()]
    w_up_v = w_up_h.ap().rearrange("(kc p) n -> p kc n", p=128)
    w_down_v = w_down_h.ap().rearrange("(kt p) n -> p kt n", p=128)

    def load_ffn(s, extra=()):
        bufi = (s + 1) % 2
        nk, k0 = NK[s], KOFF[s]
        dma(POOL, d_wu[bufi], WU[bufi][:, :, 0, 0:nk * 128], w_up_v[:, :, k0 * 128:(k0 + nk) * 128], writes=[r_wu[bufi]], extra=extra)
        t1 = dma(POOL, d_wu[bufi], WU[bufi][:, :, 1, 0:nk * 128], w_up_v[:, :, DFF + k0 * 128:DFF + (k0 + nk) * 128], nowait_w=True)
        r_wu[bufi].w = t1
        dma(POOL, d_wd[bufi], WD[bufi][:, 0:nk, :], w_down_v[:, k0:k0 + nk, :], writes=[r_wd[bufi]], extra=extra)

    load_ffn(0)
    load_ffn(1)
    r_hn = [Res() for _ in range(4)]
    r_xnb = [Res(), Res()]

    def sq_tile(t, g):
        op(ACT, lambda: nc.scalar.activation(out=bf(7)[:, :], in_=H[:, t, 0:512], func=AF.Square, accum_out=ssq[:, t:t + 1]),
           reads=[r_h[t]], writes=[r_ssq_g[g]])
        op(ACT, lambda: nc.scalar.activation(out=bf(7)[:, :], in_=H[:, t, 512:1024], func=AF.Square, accum_out=ssq2[:, t:t + 1]),
           reads=[r_h[t]], writes=[r_ssq_g[g]])

    def rstd_group(g):
        cs = slice(4 * g, 4 * g + 4)
        op(DVE, lambda: nc.vector.tensor_tensor(out=ssq[:, cs], in0=ssq[:, cs], in1=ssq2[:, cs], op=ALU.add), writes=[r_ssq_g[g]])
        op(ACT, lambda: nc.scalar.activation(out=std[:, cs], in_=ssq[:, cs], func=AF.Sqrt, scale=1.0 / D, bias=epst[:, 0:1]),
           reads=[r_ssq_g[g], r_misc], writes=[r_rstd_g[g]])
        op(DVE, lambda: nc.vector.reciprocal(out=rstd[:, cs], in_=std[:, cs]), writes=[r_rstd_g[g]])
        op(DVE, lambda: nc.vector.memset(ssq[:, cs], 0.0), reads=[r_rstd_g[g]], writes=[r_ssq_g[g]])
        op(DVE, lambda: nc.vector.memset(ssq2[:, cs], 0.0), writes=[r_ssq_g[g]])

    def norm_h_group(g, gslot, extra=()):
        for tt in range(4):
            t = 4 * g + tt
            b = bank_get()
            for half in range(2):
                cs = slice(half * 512, (half + 1) * 512)
                op(DVE, lambda: nc.vector.scalar_tensor_tensor(out=XNB[:, cs], in0=H[:, t, cs], scalar=rstd[:, t:t + 1], in1=G[gslot][:, cs],
                                                              op0=ALU.mult, op1=ALU.mult),
                   reads=[r_h[t], r_rstd_g[g], r_g[gslot]], writes=[r_xnb[half]])

                def tr(half=half, b=b):
                    last = None
                    for kc in range(4 * half, 4 * half + 4):
                        last = nc.tensor.transpose(bb(b)[:, kc * 128:(kc + 1) * 128], XNB[:, kc * 128:(kc + 1) * 128], ident[:])
                    return last
                op(PE, tr, reads=[r_xnb[half], r_const], writes=[bres(b)] if half == 0 else [])
            bres(b).w = PE.last()
            op(ACT, lambda: nc.scalar.activation(out=MIX[:, :, t * 128:(t + 1) * 128], in_=bb(b)[:, 0:1024].rearrange("p (a b) -> p a b", a=8), func=AF.Copy),
               reads=[bres(b)], writes=[r_hn[g]] if tt == 0 else [], extra=extra)
            bank_put(b)
        r_hn[g].w = ACT.last()

    for t in range(NT):
        dma(SP, d_h[t], H[:, t, :], x_d[t * 128:(t + 1) * 128, :], writes=[r_h[t]])

    def a3_group(g):
        for tt in range(4):
            t = 4 * g + tt
            for half in range(2):
                b = bank_get()
                mm_group(bf(b)[:, :], [(MIX[:, kc, t * 128:(t + 1) * 128], WOUT[:, kc, half * 512:(half + 1) * 512]) for kc in range(8)],
                         [r_wout, r_attq[g]], b, extra=[conv_p2_last[0]])
                hs = H[:, t, half * 512:(half + 1) * 512]
                op(DVE, lambda: nc.vector.tensor_tensor(out=hs, in0=bf(b)[:, :], in1=hs, op=ALU.add), reads=[bres(b)], writes=[r_h[t]])
                bank_put(b)
            sq_tile(t, g)
        rstd_group(g)
        return PE.last()

    XN2 = [sb("xn2_%d" % i, [128, D], BF16, OV + 102400 + 2048 * i) for i in range(3)] + [XNB]
    r_xn2 = [Res() for _ in range(4)]

    def norm2_stt(g):
        for tt in range(4):
            t = 4 * g + tt
            op(DVE, lambda: nc.vector.scalar_tensor_tensor(out=XN2[tt][:], in0=H[:, t, :], scalar=rstd[:, t:t + 1], in1=G[1][:],
                                                          op0=ALU.mult, op1=ALU.mult),
               reads=[r_h[t], r_rstd_g[g], r_g[1]], writes=[r_xn2[tt]])

    def norm2_tr(g, extra=()):
        for tt in range(4):
            t = 4 * g + tt
            b = bank_get()

            def tr(b=b, tt=tt):
                last = None
                for kc in range(8):
                    last = nc.tensor.transpose(bb(b)[:, kc * 128:(kc + 1) * 128], XN2[tt][:, kc * 128:(kc + 1) * 128], ident[:])
                return last
            op(PE, tr, reads=[r_xn2[tt], r_const], writes=[bres(b)])
            op(ACT, lambda: nc.scalar.activation(out=MIX[:, :, t * 128:(t + 1) * 128], in_=bb(b)[:, 0:1024].rearrange("p (a b) -> p a b", a=8), func=AF.Copy),
               reads=[bres(b)], writes=[r_hn[g]] if tt == 0 else [], extra=extra)
            bank_put(b)
        r_hn[g].w = ACT.last()
        return PE.last()

    a3_done = {}
    norm2_pe_last = None
    for g in range(5):
        if g < 4:
            a3_done[g] = a3_group(g)
        if g >= 1:
            norm2_pe_last = norm2_tr(g - 1, extra=[a3_done[g - 1]])
        if g < 4:
            norm2_stt(g)
    ACT.wait(norm2_pe_last)
    load_gain(2, 0)
    load_gain(3, 1)
    dump("h1", H[:, 0:2, :].rearrange("p a b -> p (a b)"), [128, 2 * D], F32)

    fsteps = [(s, tg) for s in range(4) for tg in range(4)]

    def ffn_up(i):
        s, tg = fsteps[i]
        bufi = (s + 1) % 2
        ab = i % 2
        nk = NK[s]
        for kt in range(nk):
            bg_, bu_ = bank_get(), bank_get()
            rhs = [MIX[:, kc, tg * 512:(tg + 1) * 512] for kc in range(8)]
            mm_group(bf(bg_)[:, :], [(WU[bufi][:, kc, 0, kt * 128:(kt + 1) * 128], rhs[kc]) for kc in range(8)], [r_hn[tg], r_wu[bufi]], bg_)
            mm_group(bf(bu_)[:, :], [(WU[bufi][:, kc, 1, kt * 128:(kt + 1) * 128], rhs[kc]) for kc in range(8)], [r_hn[tg], r_wu[bufi]], bu_)
            sl = kt % 2
            op(ACT, lambda: nc.scalar.activation(out=THF[sl][:], in_=bf(bg_)[:, :], func=AF.Tanh, scale=0.5), reads=[bres(bg_)], writes=[r_thf[sl]])
            op(DVE, lambda: nc.vector.scalar_tensor_tensor(out=T1[sl][:], in0=THF[sl][:], scalar=1.0, in1=bf(bg_)[:, :], op0=ALU.add, op1=ALU.mult),
               reads=[r_thf[sl], bres(bg_)], writes=[r_t1[sl]])
            op(DVE, lambda: nc.vector.scalar_tensor_tensor(out=ACTT[ab][:, kt, :], in0=T1[sl][:], scalar=0.5, in1=bf(bu_)[:, :], op0=ALU.mult, op1=ALU.mult),
               reads=[r_t1[sl], bres(bu_)], writes=[r_actt[ab]] if kt == 0 else [])
            bank_put(bg_)
            bank_put(bu_)
        r_actt[ab].w = DVE.last()

    def ffn_down(i):
        s, tg = fsteps[i]
        bufi = (s + 1) % 2
        ab = i % 2
        nk = NK[s]
        for tt in range(4):
            t = 4 * tg + tt
            for half in range(2):
                b = bank_get()
                mm_group(bf(b)[:, :], [(ACTT[ab][:, kt, tt * 128:(tt + 1) * 128], WD[bufi][:, kt, half * 512:(half + 1) * 512]) for kt in range(nk)],
                         [r_actt[ab], r_wd[bufi]], b)
                hs = H[:, t, half * 512:(half + 1) * 512]
                op(DVE, lambda: nc.vector.tensor_tensor(out=hs, in0=bf(b)[:, :], in1=hs, op=ALU.add), reads=[bres(b)], writes=[r_h[t]])
                bank_put(b)
            if s == 3:
                sq_tile(t, tg)
        if tg == 3 and s + 2 < 4:
            load_ffn(s + 2)

    WG = sb("wg", [128, 8, D], BF16, OV + 109824)
    WP = sb("wp", [128, 2, D], BF16, OV + 109824 + 16384)
    PTT = sb("ptt", [128, 2, T], BF16, OV + 130304)
    PBALL = sb("pball", [128, NT, 256], BF16, OV + 138496)
    r_pball = Res()
    d_pball = DSem(nc, "dpball")
    r_wg = Res()
    d_wg = DSem(nc, "dwg")
    r_ptt = Res()
    d_out = [DSem(nc, "dout%d" % i) for i in range(4)]
    outs = []

    def ple_prefetch():
        w_gate_v = w_gate_h.ap().rearrange("(kc p) n -> p kc n", p=128)
        w_ple_v = w_ple_h.ap().rearrange("(kc p) n -> p kc n", p=128)
        pe_t = PE.last()
        dma(POOL, d_wg, WG[:], w_gate_v, extra=[pe_t])
        tgp = dma(POOL, d_wg, WP[:], w_ple_v)
        r_wg.w = tgp
        dma(POOL, d_pball, PBALL[:], p_d.rearrange("(t p) f -> p t f", p=128), writes=[r_pball])
        ACT.wait(pe_t)

    def p_transposes():
        for t in range(NT):
            b = bank_get()

            def tr(b=b, t=t):
                nc.tensor.transpose(bb(b)[:, 0:128], PBALL[:, t, 0:128], ident[:])
                return nc.tensor.transpose(bb(b)[:, 128:256], PBALL[:, t, 128:256], ident[:])
            op(PE, tr, reads=[r_pball, r_const], writes=[bres(b)])
            op(ACT, lambda: nc.scalar.activation(out=PTT[:, :, t * 128:(t + 1) * 128], in_=bb(b)[:, 0:256].rearrange("p (a b) -> p a b", a=2), func=AF.Copy),
               reads=[bres(b)], writes=[])
            bank_put(b)
        r_ptt.w = ACT.last()

    def ple_group(g):
        for tt in range(4):
            t = 4 * g + tt
            for half in range(2):
                bga, bpl = bank_get(), bank_get()
                mm_group(bf(bga)[:, :], [(MIX[:, kc, t * 128:(t + 1) * 128], WG[:, kc, half * 512:(half + 1) * 512]) for kc in range(8)], [r_hn[g], r_wg], bga)
                mm_group(bf(bpl)[:, :], [(PTT[:, kc, t * 128:(t + 1) * 128], WP[:, kc, half * 512:(half + 1) * 512]) for kc in range(2)], [r_ptt, r_wg], bpl)
                sl = (2 * t + half) % 2
                op(ACT, lambda: nc.scalar.activation(out=THF[sl][:], in_=bf(bga)[:, :], func=AF.Tanh, scale=0.5), reads=[bres(bga)], writes=[r_thf[sl]])
                op(DVE, lambda: nc.vector.scalar_tensor_tensor(out=T1[sl][:], in0=THF[sl][:], scalar=1.0, in1=bf(bpl)[:, :], op0=ALU.add, op1=ALU.mult),
                   reads=[r_thf[sl], bres(bpl)], writes=[r_t1[sl]])
                hs = H[:, t, half * 512:(half + 1) * 512]
                op(DVE, lambda: nc.vector.scalar_tensor_tensor(out=hs, in0=T1[sl][:], scalar=0.5, in1=hs, op0=ALU.mult, op1=ALU.add),
                   reads=[r_t1[sl]], writes=[r_h[t]])
                bank_put(bga)
                bank_put(bpl)
            sq_tile(t, g)
        rstd_group(g)

    XNS = sb("xns", [128, 4, D], BF16, OV + 138496)
    r_xns = [Res() for _ in range(4)]

    def norm3_stt(g, extra=()):
        for tt in range(4):
            t = 4 * g + tt
            op(DVE, lambda: nc.vector.scalar_tensor_tensor(out=XNS[:, tt, :], in0=H[:, t, :], scalar=rstd[:, t:t + 1], in1=G[0][:],
                                                          op0=ALU.mult, op1=ALU.mult),
               reads=[r_h[t], r_rstd_g[g], r_g[0]], writes=[r_xns[tt]], extra=extra)

    def norm3_tr(g):
        for tt in range(4):
            t = 4 * g + tt
            b = bank_get()

            def tr(b=b, tt=tt):
                last = None
                for kc in range(8):
                    last = nc.tensor.transpose(bb(b)[:, kc * 128:(kc + 1) * 128], XNS[:, tt, kc * 128:(kc + 1) * 128], ident[:])
                return last
            op(PE, tr, reads=[r_xns[tt], r_const], writes=[bres(b)])
            op(ACT, lambda: nc.scalar.activation(out=MIX[:, :, t * 128:(t + 1) * 128], in_=bb(b)[:, 0:1024].rearrange("p (a b) -> p a b", a=8), func=AF.Copy),
               reads=[bres(b)], writes=[r_hn[g]] if tt == 0 else [])
            bank_put(b)
        r_hn[g].w = ACT.last()

    def final_group(g):
        for tt in range(4):
            t = 4 * g + tt
            op(DVE, lambda: nc.vector.scalar_tensor_tensor(out=H[:, t, :], in0=H[:, t, :], scalar=rstd[:, t:t + 1], in1=G[1][:], op0=ALU.mult, op1=ALU.mult),
               reads=[r_rstd_g[g], r_g[1]], writes=[r_h[t]])
            outs.append(dma(SP, d_out[t % 4], out_d[t * 128:(t + 1) * 128, :], H[:, t, :], reads=[r_h[t]]))

    ffn_up(0)
    for i in range(len(fsteps)):
        if i + 1 < len(fsteps):
            ffn_up(i + 1)
        ffn_down(i)
        s, tg = fsteps[i]
        if (s, tg) == (2, 3):
            ple_prefetch()
        if s == 3:
            rstd_group(tg)
            if tg == 0:
                p_transposes()
                norm3_stt(tg, extra=[PE.last()])
            else:
                norm3_stt(tg)
            if tg >= 1:
                ple_group(tg - 1)
            norm3_tr(tg)
            if tg >= 2:
                final_group(tg - 2)
    final_group(2)
    ple_group(3)
    final_group(3)
    SP.wait(*outs[-4:])
    for e in engines:
        e.wait(*outs[-4:])
    return nc, dbg


_CACHE = {}


def _in_maps(inputs, cores):
    f = lambda a: np.ascontiguousarray(np.asarray(a, dtype=np.float32))
    x = f(inputs["x"])
    p = f(inputs["p"])[0]
    pos = np.ascontiguousarray(np.asarray(inputs["positions"]).astype(np.int32))
    gains = np.ascontiguousarray(np.stack([f(inputs["norm_mix_g"])[0], f(inputs["norm_ffn_g"])[0],
                                           f(inputs["norm_ple_g"])[0], f(inputs["final_norm_g"])], axis=0))
    cvec = np.ascontiguousarray(np.stack([f(inputs["conv_b"])[0], f(inputs["conv_ln_g"])[0], f(inputs["conv_ln_b"])[0]], axis=0))
    shared = {
        "gains": gains, "w_in": f(inputs["w_in"])[0], "conv_w": f(inputs["conv_w"])[0], "cvec": cvec,
        "w_out": f(inputs["w_out"])[0], "w_up": f(inputs["w_ffn_up"])[0], "w_down": f(inputs["w_ffn_down"])[0],
        "w_gate": f(inputs["w_ple_gate"])[0], "w_ple": f(inputs["w_ple_proj"])[0],
    }
    maps = []
    for b in cores:
        m = dict(shared)
        m["x"] = np.ascontiguousarray(x[b])
        m["p"] = np.ascontiguousarray(p[b])
        m["pos"] = np.ascontiguousarray(pos[b])
        maps.append(m)
    return maps


def kernel(**inputs):
    if "nc" not in _CACHE:
        _CACHE["nc"] = build(False)[0]
    nc = _CACHE["nc"]
    maps = _in_maps(inputs, list(range(8)))
    res = run_bass_kernel_spmd(nc, maps, core_ids=list(range(8)))
    out = np.stack([np.asarray(r["out"], dtype=np.float32) for r in res.results], axis=0)
    return out
```

```python
import math
import numpy as np
import concourse.bass as bass
import concourse.mybir as mybir
from concourse.bass_utils import run_bass_kernel_spmd

F32 = mybir.dt.float32
BF16 = mybir.dt.bfloat16
I32 = mybir.dt.int32
AF = mybir.ActivationFunctionType
ALU = mybir.AluOpType
AX = mybir.AxisListType

T = 2048
D = 1024
NT = 16
DFF = 2816
EPS = 1e-6
NEG = -480.0
BASE = 16512
SB_LIMIT = 229344 - BASE
TWO_PI = 2.0 * math.pi


class Res:
    __slots__ = ("w", "r")

    def __init__(self):
        self.w = None
        self.r = {}


class Eng:
    def __init__(self, nc, eng, name):
        self.e = eng
        self.key = name
        self.sem = nc.alloc_semaphore(name + "_cnt")
        self.n = 0
        self.seen = {}

    def wait(self, *ts):
        for t in ts:
            if t is None:
                continue
            if isinstance(t, (list, tuple)) and len(t) and isinstance(t[0], (list, tuple)):
                self.wait(*t)
                continue
            sem, n, key = t
            if self.seen.get(key, 0) >= n:
                continue
            self.seen[key] = n
            self.e.wait_ge(sem, n)

    def mark(self, inst):
        self.n += 1
        inst.then_inc(self.sem, 1)
        return (self.sem, self.n, self.key)

    def last(self):
        return (self.sem, self.n, self.key) if self.n else None


class DSem:
    def __init__(self, nc, name):
        self.sem = nc.alloc_semaphore(name)
        self.cnt = 0
        self.key = name


def op(E, fn, reads=(), writes=(), extra=()):
    for r in reads:
        E.wait(r.w)
    for w in writes:
        E.wait(w.w)
        E.wait(*w.r.values())
    E.wait(*extra)
    t = E.mark(fn())
    for r in reads:
        r.r[E.key] = t
    for w in writes:
        w.w = t
        w.r = {}
    return t


def dma(Q, ds, out_ap, in_ap, reads=(), writes=(), extra=(), nowait_w=False):
    for r in reads:
        Q.wait(r.w)
    for w in writes:
        if not nowait_w:
            Q.wait(w.w)
        Q.wait(*w.r.values())
    Q.wait(*extra)
    inst = Q.e.dma_start(out=out_ap, in_=in_ap)
    ds.cnt += 16
    inst.then_inc(ds.sem, 16)
    t = (ds.sem, ds.cnt, ds.key)
    for r in reads:
        r.r[ds.key] = t
    for w in writes:
        w.w = t
        w.r = {}
    return t


def build(debug=False):
    nc = bass.Bass("TRN2", target_bir_lowering=False)
    dt_in = lambda name, shape, dt=F32: nc.dram_tensor(name, shape, dt, kind="ExternalInput")
    x_h = dt_in("x", [T, D])
    p_h = dt_in("p", [T, 256])
    pos_h = dt_in("pos", [T], I32)
    gains_h = dt_in("gains", [4, D])
    w_in_h = dt_in("w_in", [D, 2560])
    conv_w_h = dt_in("conv_w", [31, 512])
    cvec_h = dt_in("cvec", [3, 512])
    w_out_h = dt_in("w_out", [D, D])
    w_up_h = dt_in("w_up", [D, 2 * DFF])
    w_down_h = dt_in("w_down", [DFF, D])
    w_gate_h = dt_in("w_gate", [D, D])
    w_ple_h = dt_in("w_ple", [256, D])
    out_h = nc.dram_tensor("out", [T, D], F32, kind="ExternalOutput")
    x_d, p_d, out_d = x_h.ap(), p_h.ap(), out_h.ap()
    dbg = {}

    def sb(name, shape, dt, off):
        assert off % 32 == 0, (name, off)
        nbytes = int(np.prod(shape[1:])) * (4 if dt in (F32, I32) else 2)
        assert off + nbytes <= SB_LIMIT, (name, off, nbytes)
        return nc.alloc_sbuf_tensor_at(name, list(shape), dt, offset=BASE + off)

    PE = Eng(nc, nc.tensor, "pe")
    ACT = Eng(nc, nc.scalar, "act")
    DVE = Eng(nc, nc.vector, "dve")
    POOL = Eng(nc, nc.gpsimd, "pool")
    SP = Eng(nc, nc.sync, "sp")
    engines = [PE, ACT, DVE, POOL]

    def barrier(also=()):
        ts = [e.last() for e in engines]
        for e in engines:
            e.wait(*ts)
        for q in also:
            q.wait(*ts)
        return ts

    dbl = [nc.alloc_psum_tensor("pd%d" % i, [128, 1024], F32) for i in range(4)]
    bank_res = [Res() for _ in range(8)]
    free_banks = list(range(7))

    def bank_get():
        return free_banks.pop(0)

    def bank_put(i):
        free_banks.append(i)

    def bf(i):
        return dbl[i // 2][:, (i % 2) * 512:(i % 2 + 1) * 512]

    def bb(i):
        return bf(i).bitcast(BF16)

    def bres(i):
        return bank_res[i]

    ident = sb("ident", [128, 128], BF16, 0)
    tri = sb("tri", [128, 128], BF16, 256)
    ones512 = sb("ones512", [128, 128], BF16, 512)
    blk = sb("blk", [8, 8, 128], BF16, 768)
    invf = sb("invf", [128, 8], F32, 2816)
    epst = sb("epst", [128, 1], F32, 2848)
    posi = sb("posi", [128, 16], I32, 2880)
    posf = sb("posf", [128, 16], F32, 2944)
    COS = sb("cos", [128, 16, 8], F32, 3008)
    SIN = sb("sin", [128, 16, 8], F32, 3520)
    ANG = sb("ang", [128, 16, 8], F32, 4032)
    ssq = sb("ssq", [128, 16], F32, 4544)
    std = sb("std", [128, 16], F32, 4608)
    rstd = sb("rstd", [128, 16], F32, 4672)
    cvec = sb("cvec", [128, 12], F32, 4736)
    cvh = sb("cvh", [128, 8], F32, 4800)
    wT = sb("wT", [128, 124], BF16, 4832)
    KM = sb("km", [128, 4, 8], F32, 5088)
    KMD = sb("kmd", [128, 4, 16], BF16, 5216)
    ssq2 = sb("ssq2", [128, 16], F32, 5344 + 32)
    G0 = sb("g0", [128, D], F32, 5440)
    G1 = sb("g1", [128, D], F32, 9536)
    MIXO = 13632
    MIX = sb("mix", [128, 8, T], BF16, MIXO)
    ANG2 = sb("ang2", [128, 256], F32, MIXO)
    QQ = sb("qq", [128, 256], F32, MIXO + 1024)
    KI = sb("ki", [128, 256], I32, MIXO + 2048)
    KF = sb("kf", [128, 256], F32, MIXO + 3072)
    MM = sb("mm", [128, 256], F32, MIXO + 4096)
    OV = MIXO + 32768
    assert OV % 32 == 0

    r_const = Res()
    r_g = [Res(), Res()]
    d_g = [DSem(nc, "dg0"), DSem(nc, "dg1")]
    G = [G0, G1]

    def load_gain(idx, slot, extra=()):
        src = bass.AP(gains_h, idx * D, [[0, 128], [1, D]])
        return dma(SP, d_g[slot], G[slot][:], src, writes=[r_g[slot]], extra=extra)


    XS = [sb("xs%d" % i, [128, D], F32, OV + 131328 + 4096 * i) for i in range(3)] + [sb("xs3", [128, D], F32, OV + 161024)]
    r_xs = [Res() for _ in range(4)]
    d_xs = [DSem(nc, "dxs%d" % i) for i in range(4)]
    for tt in range(4):
        dma(SP, d_xs[tt], XS[tt][:], x_d[tt * 128:(tt + 1) * 128, :], writes=[r_xs[tt]])

    d_small = DSem(nc, "dsmall")
    r_small = Res()
    with nc.allow_non_contiguous_dma(reason="tiny param gathers"):
        for q4 in range(4):
            t_sm = dma(SP, d_small, posi[:, q4 * 4:(q4 + 1) * 4],
                       bass.AP(pos_h, q4 * 512, [[1, 128], [128, 4]]), nowait_w=True)
        for v in range(3):
            t_sm = dma(SP, d_small, cvec[:, v * 4:(v + 1) * 4],
                       bass.AP(cvec_h, v * 512, [[1, 128], [128, 4]]), nowait_w=True)
    r_small.w = t_sm
    load_gain(0, 0)

    WIN = sb("win", [128, 8, 2560], BF16, OV + 73984)
    w_in_v = w_in_h.ap().rearrange("(kc p) n -> p kc n", p=128)
    r_win = [Res() for _ in range(10)]
    d_win = [DSem(nc, "dwin%d" % i) for i in range(10)]
    for i in (0, 2, 1, 3, 4, 5, 6, 7, 8, 9):
        dma(POOL, d_win[i], WIN[:, :, i * 256:(i + 1) * 256], w_in_v[:, :, i * 256:(i + 1) * 256], writes=[r_win[i]])

    op(POOL, lambda: nc.gpsimd.memset(ident[:], 1.0), writes=[r_const])
    op(POOL, lambda: nc.gpsimd.affine_select(out=ident[:], in_=ident[:], pattern=[[-1, 128]],
                                             compare_op=ALU.is_equal, fill=0.0, base=0, channel_multiplier=1),
       writes=[r_const])
    r_tri = Res()
    op(POOL, lambda: nc.gpsimd.memset(tri[:], NEG), writes=[r_tri])
    op(POOL, lambda: nc.gpsimd.affine_select(out=tri[:], in_=tri[:], pattern=[[-1, 128]],
                                             compare_op=ALU.is_gt, fill=0.0, base=0, channel_multiplier=1),
       writes=[r_tri])
    r_blk = Res()
    op(POOL, lambda: nc.gpsimd.memset(blk[:], 1.0), writes=[r_blk])
    op(POOL, lambda: nc.gpsimd.affine_select(out=blk[:], in_=blk[:], pattern=[[-1, 8], [0, 128]],
                                             compare_op=ALU.is_equal, fill=0.0, base=0, channel_multiplier=1),
       writes=[r_blk])
    r_misc = Res()
    op(DVE, lambda: nc.vector.memset(ones512[:], 1.0 / 512.0), writes=[r_misc])
    op(DVE, lambda: nc.vector.memset(epst[:], EPS), writes=[r_misc])
    for f in range(8):
        val = 500000.0 ** (-(2.0 * f) / 16.0)
        op(DVE, lambda f=f, val=val: nc.vector.memset(invf[:, f:f + 1], float(np.float32(val))), writes=[r_misc])
    op(DVE, lambda: nc.vector.memset(ssq[:], 0.0), writes=[r_misc])
    op(DVE, lambda: nc.vector.memset(ssq2[:], 0.0), writes=[r_misc])

    r_rope = Res()

    QT = sb("qt", [128, 4, T], BF16, OV + 0)
    KT = sb("kt", [128, 4, T], BF16, OV + 16384)
    VS = sb("vs", [128, 16, 4, 192], BF16, OV + 32768)
    UT = sb("ut", [128, 4, 2080], BF16, OV + 57344)
    HNT = [sb("hnt%d" % i, [128, 8, 512], BF16, OV + 114944 + 8192 * i) for i in range(2)]
    XN = [sb("xn%d" % i, [128, D], BF16, OV + 143616 + 2048 * i) for i in range(2)]
    QTOK = sb("qtok", [128, 4, 512], BF16, OV + 147712)
    KTOK = sb("ktok", [128, 4, 512], BF16, OV + 151808)
    THA = [sb("tha%d" % i, [128, 512], F32, OV + 155904 + 2048 * i) for i in range(2)]
    RTMP = [sb("rtmp%d" % i, [128, 8, 8], F32, OV + 160000 + 256 * i) for i in range(4)]

    r_qt = [Res() for _ in range(4)]
    r_kt = [Res() for _ in range(4)]
    r_vs = Res()
    r_ut = Res()
    r_hnt = [Res(), Res()]
    r_ssq_g = [Res() for _ in range(4)]
    r_rstd_g = [Res() for _ in range(4)]
    r_xn = [Res(), Res()]
    r_qtok = Res()
    r_ktok = Res()
    r_tha = [Res(), Res()]
    r_rtmp = [Res() for _ in range(4)]
    r_junk = Res()
    r_ssq = Res()
    r_rstd = Res()

    d_cw = DSem(nc, "dcw")

    op(DVE, lambda: nc.vector.memset(VS[:, :, :, 64:128], 1.0), writes=[r_vs])
    op(DVE, lambda: nc.vector.memset(UT[:, :, 0:30], 0.0), writes=[r_ut])

    def sumsq_pass(src_fn, nslots_res, t_list):
        pass

    def pass1(g):
        cs = slice(4 * g, 4 * g + 4)
        for tt in range(4):
            t = 4 * g + tt
            if g > 0:
                dma(SP, d_xs[tt], XS[tt][:], x_d[t * 128:(t + 1) * 128, :], writes=[r_xs[tt]])
            op(ACT, lambda tt=tt, t=t: nc.scalar.activation(out=bf(7)[:, :], in_=XS[tt][:, 0:512], func=AF.Square, accum_out=ssq[:, t:t + 1]),
               reads=[r_xs[tt], r_misc], writes=[r_ssq_g[g]])
            op(ACT, lambda tt=tt, t=t: nc.scalar.activation(out=bf(7)[:, :], in_=XS[tt][:, 512:1024], func=AF.Square, accum_out=ssq2[:, t:t + 1]),
               reads=[r_xs[tt]], writes=[r_ssq_g[g]])
        op(DVE, lambda: nc.vector.tensor_tensor(out=ssq[:, cs], in0=ssq[:, cs], in1=ssq2[:, cs], op=ALU.add), writes=[r_ssq_g[g]])
        op(ACT, lambda: nc.scalar.activation(out=std[:, cs], in_=ssq[:, cs], func=AF.Sqrt, scale=1.0 / D, bias=epst[:, 0:1]),
           reads=[r_ssq_g[g], r_misc], writes=[r_rstd_g[g]])
        op(DVE, lambda: nc.vector.reciprocal(out=rstd[:, cs], in_=std[:, cs]), writes=[r_rstd_g[g]])
        op(DVE, lambda: nc.vector.memset(ssq[:, cs], 0.0), reads=[r_rstd_g[g]], writes=[r_ssq_g[g]])
        op(DVE, lambda: nc.vector.memset(ssq2[:, cs], 0.0), writes=[r_ssq_g[g]])

    def finish_rstd():
        op(DVE, lambda: nc.vector.tensor_tensor(out=ssq[:], in0=ssq[:], in1=ssq2[:], op=ALU.add), writes=[r_ssq])
        op(ACT, lambda: nc.scalar.activation(out=std[:], in_=ssq[:], func=AF.Sqrt, scale=1.0 / D, bias=epst[:, 0:1]),
           reads=[r_ssq, r_misc], writes=[r_rstd])
        op(DVE, lambda: nc.vector.reciprocal(out=rstd[:], in_=std[:]), writes=[r_rstd])
        op(DVE, lambda: nc.vector.memset(ssq[:], 0.0), reads=[r_rstd], writes=[r_ssq])
        op(DVE, lambda: nc.vector.memset(ssq2[:], 0.0), writes=[r_ssq])

    def norm_tile(src_ap, src_res, t, gslot, dst_ap, dst_res, xslot, rres=None):
        op(DVE, lambda: nc.vector.scalar_tensor_tensor(out=XN[xslot][:], in0=src_ap, scalar=rstd[:, t:t + 1], in1=G[gslot][:],
                                                      op0=ALU.mult, op1=ALU.mult),
           reads=[src_res, rres if rres is not None else r_rstd, r_g[gslot]], writes=[r_xn[xslot]])
        b = bank_get()

        def tr():
            last = None
            for kc in range(8):
                last = nc.tensor.transpose(bb(b)[:, kc * 128:(kc + 1) * 128], XN[xslot][:, kc * 128:(kc + 1) * 128], ident[:])
            return last
        op(PE, tr, reads=[r_xn[xslot], r_const], writes=[bres(b)])
        op(ACT, lambda: nc.scalar.activation(out=dst_ap, in_=bb(b)[:, 0:1024].rearrange("p (a b) -> p a b", a=8), func=AF.Copy),
           reads=[bres(b)], writes=[dst_res])
        bank_put(b)

    def rope(ps_i, dst, dst_res, t, tt):
        src = bf(ps_i)[:, :].rearrange("p (h d) -> p h d", h=8)
        dv = dst[:, tt, :].rearrange("p (h d) -> p h d", h=8)
        op(ACT, lambda: nc.scalar.activation(out=dv[:, :, 16:64], in_=src[:, :, 16:64], func=AF.Copy),
           reads=[bres(ps_i)], writes=[])
        cb = bass.AP(COS, t * 8, [[128, 128], [0, 8], [1, 8]])
        snb = bass.AP(SIN, t * 8, [[128, 128], [0, 8], [1, 8]])
        x1 = src[:, :, 0:8]
        x2 = src[:, :, 8:16]
        op(DVE, lambda: nc.vector.tensor_tensor(out=RTMP[0][:], in0=x1, in1=cb, op=ALU.mult), reads=[bres(ps_i), r_rope], writes=[r_rtmp[0]])
        op(DVE, lambda: nc.vector.tensor_tensor(out=RTMP[1][:], in0=x2, in1=snb, op=ALU.mult), reads=[bres(ps_i)], writes=[r_rtmp[1]])
        op(DVE, lambda: nc.vector.tensor_tensor(out=RTMP[2][:], in0=x2, in1=cb, op=ALU.mult), reads=[bres(ps_i)], writes=[r_rtmp[2]])
        op(DVE, lambda: nc.vector.tensor_tensor(out=RTMP[3][:], in0=x1, in1=snb, op=ALU.mult), reads=[bres(ps_i)], writes=[r_rtmp[3]])
        op(DVE, lambda: nc.vector.tensor_tensor(out=dv[:, :, 0:8], in0=RTMP[0][:], in1=RTMP[1][:], op=ALU.subtract),
           reads=[r_rtmp[0], r_rtmp[1]], writes=[])
        return op(DVE, lambda: nc.vector.tensor_tensor(out=dv[:, :, 8:16], in0=RTMP[2][:], in1=RTMP[3][:], op=ALU.add),
                  reads=[r_rtmp[2], r_rtmp[3]], writes=[])

    def mm_group(out_ap, pairs, reads, bank_i, extra=()):
        def f():
            last = None
            n = len(pairs)
            for i, (l, r) in enumerate(pairs):
                last = nc.tensor.matmul(out_ap, l, r, start=(i == 0), stop=(i == n - 1))
            return last
        return op(PE, f, reads=reads, writes=[bres(bank_i)], extra=extra)

    def norm_x_tile(t, gbuf):
        tt = t % 4
        norm_tile(XS[tt][:], r_xs[tt], t, 0, HNT[gbuf][:, :, tt * 128:(tt + 1) * 128], r_hnt[gbuf], t % 2, rres=r_rstd_g[t // 4])

    CWB = sb("cwb", [124, 128], BF16, OV + 165120)
    CWF2 = sb("cwf2", [124, 128], F32, OV + 165376)
    r_cw = Res()
    cw_v = conv_w_h.ap().rearrange("j (c p) -> (j c) p", p=128)
    dma(SP, d_cw, CWF2[:], cw_v, writes=[r_cw])

    op(DVE, lambda: nc.vector.tensor_scalar(out=CWB[:], in0=CWF2[:], scalar1=0.5, scalar2=None, op0=ALU.mult), reads=[r_cw], writes=[r_cw])

    def conv_weight_prep():
        b = bank_get()
        op(PE, lambda: nc.tensor.transpose(bb(b)[:, 0:124], CWB[:], ident[0:124, 0:124]), reads=[r_cw, r_const], writes=[bres(b)])
        op(ACT, lambda: nc.scalar.activation(out=wT[:], in_=bb(b)[:, 0:124], func=AF.Copy), reads=[bres(b)], writes=[r_cw])
        bank_put(b)

    def build_rope_tables():
        op(DVE, lambda: nc.vector.tensor_copy(out=posf[:], in_=posi[:]), reads=[r_small], writes=[r_rope])
        posb = bass.AP(posf, 0, [[16, 128], [1, 16], [0, 8]])
        invb = bass.AP(invf, 0, [[8, 128], [0, 16], [1, 8]])
        op(DVE, lambda: nc.vector.tensor_tensor(out=ANG[:], in0=posb, in1=invb, op=ALU.mult), reads=[r_misc], writes=[r_rope])
        angf = ANG[:].rearrange("p a b -> p (a b)")
        op(DVE, lambda: nc.vector.tensor_copy(out=ANG2[:, 0:128], in_=angf), writes=[r_rope])
        op(DVE, lambda: nc.vector.tensor_scalar(out=ANG2[:, 128:256], in0=angf, scalar1=0.5 * math.pi, scalar2=None, op0=ALU.add), writes=[r_rope])
        op(DVE, lambda: nc.vector.tensor_scalar(out=QQ[:], in0=ANG2[:], scalar1=1.0 / TWO_PI, scalar2=None, op0=ALU.mult), writes=[r_rope])
        op(DVE, lambda: nc.vector.tensor_copy(out=KI[:], in_=QQ[:]), writes=[r_rope])
        op(DVE, lambda: nc.vector.tensor_copy(out=KF[:], in_=KI[:]), writes=[r_rope])
        op(DVE, lambda: nc.vector.scalar_tensor_tensor(out=QQ[:], in0=KF[:], scalar=-TWO_PI, in1=ANG2[:], op0=ALU.mult, op1=ALU.add), writes=[r_rope])
        op(DVE, lambda: nc.vector.tensor_scalar(out=MM[:], in0=QQ[:], scalar1=math.pi, scalar2=None, op0=ALU.is_gt), writes=[r_rope])
        op(DVE, lambda: nc.vector.scalar_tensor_tensor(out=ANG2[:], in0=MM[:], scalar=-TWO_PI, in1=QQ[:], op0=ALU.mult, op1=ALU.add), writes=[r_rope])
        op(DVE, lambda: nc.vector.tensor_scalar(out=ANG2[:], in0=ANG2[:], scalar1=-math.pi, scalar2=math.pi, op0=ALU.max, op1=ALU.min), writes=[r_rope])
        op(ACT, lambda: nc.scalar.activation(out=SIN[:].rearrange("p a b -> p (a b)"), in_=ANG2[:, 0:128], func=AF.Sin), reads=[r_rope], writes=[r_rope])
        op(ACT, lambda: nc.scalar.activation(out=COS[:].rearrange("p a b -> p (a b)"), in_=ANG2[:, 128:256], func=AF.Sin), writes=[r_rope])
        op(DVE, lambda: nc.vector.tensor_scalar(out=cvh[:], in0=cvec[:, 4:12], scalar1=0.5, scalar2=None, op0=ALU.mult), reads=[r_small], writes=[r_misc])


    pe_win_done = [None]
    pass1(0)
    for t in range(4):
        norm_x_tile(t, 0)
    build_rope_tables()
    load_gain(1, 1)

    for g in range(4):
        gb = g % 2
        hn = HNT[gb]
        if g < 3:
            pass1(g + 1)
        if g == 2:
            conv_weight_prep()
        for c in range(4):
            ba = bank_get()
            bg = bank_get()
            ia, ig = c // 2, 2 + c // 2
            mm_group(bf(ba)[:, :], [(WIN[:, kc, c * 128:(c + 1) * 128], hn[:, kc, :]) for kc in range(8)],
                     [r_hnt[gb], r_win[ia]], ba)
            mm_group(bf(bg)[:, :], [(WIN[:, kc, 512 + c * 128:512 + (c + 1) * 128], hn[:, kc, :]) for kc in range(8)],
                     [r_hnt[gb], r_win[ig]], bg)
            s = c % 2
            op(ACT, lambda: nc.scalar.activation(out=THA[s][:], in_=bf(bg)[:, :], func=AF.Tanh, scale=0.5),
               reads=[bres(bg)], writes=[r_tha[s]])
            op(DVE, lambda: nc.vector.scalar_tensor_tensor(out=UT[:, c, 30 + g * 512:30 + (g + 1) * 512], in0=THA[s][:], scalar=1.0,
                                                          in1=bf(ba)[:, :], op0=ALU.add, op1=ALU.mult),
               reads=[r_tha[s], bres(ba)], writes=[])
            bank_put(ba)
            bank_put(bg)
        for tt in range(4):
            t = 4 * g + tt
            bq, bk, bv = bank_get(), bank_get(), bank_get()
            lh = [hn[:, kc, tt * 128:(tt + 1) * 128] for kc in range(8)]
            mm_group(bf(bq)[:, :], [(lh[kc], WIN[:, kc, 1024:1536]) for kc in range(8)], [r_hnt[gb], r_win[4], r_win[5]], bq)
            mm_group(bf(bk)[:, :], [(lh[kc], WIN[:, kc, 1536:2048]) for kc in range(8)], [r_hnt[gb], r_win[6], r_win[7]], bk)
            pe_win_done[0] = mm_group(bf(bv)[:, :], [(lh[kc], WIN[:, kc, 2048:2560]) for kc in range(8)], [r_hnt[gb], r_win[8], r_win[9]], bv)
            vout = bass.AP(VS, t * 768, [[16 * 768, 128], [192, 4], [128, 2], [1, 64]])
            vin = bf(bv)[:, :].rearrange("p (a b c) -> p a b c", a=4, b=2)
            op(ACT, lambda: nc.scalar.activation(out=vout, in_=vin, func=AF.Copy), reads=[bres(bv)], writes=[])
            if tt == 0:
                PE_prev = r_qtok.r.get("pe"), r_ktok.r.get("pe")
                ACT.wait(*[x for x in PE_prev if x])
                DVE.wait(*[x for x in PE_prev if x])
            tq = rope(bq, QTOK, r_qtok, t, tt)
            tk = rope(bk, KTOK, r_ktok, t, tt)
            bank_put(bq)
            bank_put(bk)
            bank_put(bv)
            if g < 3:
                norm_x_tile(4 * (g + 1) + tt, (g + 1) % 2)
        r_qtok.w = DVE.last()
        r_ktok.w = DVE.last()
        act_last = ACT.last()
        for (src, srcres, dstT, dstres) in ((QTOK, r_qtok, QT, r_qt), (KTOK, r_ktok, KT, r_kt)):
            for hp in range(4):
                b = bank_get()

                def tr(src=src, hp=hp, b=b):
                    last = None
                    for tt in range(4):
                        last = nc.tensor.transpose(bb(b)[:, tt * 128:(tt + 1) * 128], src[:, tt, hp * 128:(hp + 1) * 128], ident[:])
                    return last
                op(PE, tr, reads=[srcres, r_const], writes=[bres(b)], extra=[act_last])
                op(ACT, lambda dstT=dstT, hp=hp, b=b: nc.scalar.activation(out=dstT[:, hp, g * 512:(g + 1) * 512], in_=bb(b)[:, 0:512], func=AF.Copy),
                   reads=[bres(b)], writes=[])
                bank_put(b)
    a1_act, a1_dve, a1_pe = ACT.last(), DVE.last(), PE.last()
    for e in (ACT, DVE, POOL, SP):
        e.wait(pe_win_done[0], a1_act, a1_dve)
    PE.wait(a1_act, a1_dve)
    for r_ in r_qt + r_kt + [r_vs, r_ut]:
        r_.w = None
        r_.r = {}

    if debug:
        d_dbg = DSem(nc, "ddbg")

        def dump(name, ap, shape, dt):
            h = nc.dram_tensor("dbg_" + name, list(shape), dt, kind="ExternalOutput")
            dbg[name] = h
            ts = barrier(also=[SP])
            dma(SP, d_dbg, h.ap(), ap)
            SP.wait((d_dbg.sem, d_dbg.cnt, d_dbg.key))
    else:
        def dump(*a, **k):
            pass

    dump("qt", QT[:].rearrange("p a b -> p (a b)"), [128, 4 * T], BF16)
    dump("kt", KT[:].rearrange("p a b -> p (a b)"), [128, 4 * T], BF16)
    dump("vs", VS[:].rearrange("p a b c -> p (a b c)"), [128, 16 * 768], BF16)
    dump("ut", UT[:].rearrange("p a b -> p (a b)"), [128, 4 * 2080], BF16)

    DIAG = sb("diag", [128, 4, 31, 128], BF16, OV + 73984)
    CF = [sb("cf%d" % i, [128, 4, 512], F32, OV + 105728 + 8192 * i) for i in range(2)]
    CB = sb("cb", [128, 4, 512], BF16, OV + 122112)
    CB2 = sb("cb2", [128, 4, 512], BF16, OV + 126208)
    TT_ = [sb("ttmp%d" % i, [128, 512], F32, OV + 130304 + 2048 * i) for i in range(2)]
    YH = [sb("yh%d" % i, [128, 512], F32, OV + 134400 + 2048 * i) for i in range(2)]
    THC = [sb("thc%d" % i, [128, 512], F32, OV + 138496 + 2048 * i) for i in range(2)]
    MSB = sb("msb", [128, 512], F32, OV + 142592)
    MEANS = [sb("means%d" % i, [128, 512], F32, OV + 144640 + 2048 * i) for i in range(2)]
    RS = [sb("rs%d" % i, [128, 512], F32, OV + 148736 + 2048 * i) for i in range(2)]
    r_diag = [Res() for _ in range(4)]
    r_cf = [[Res() for _ in range(4)] for _ in range(2)]
    r_cb = [Res() for _ in range(4)]
    r_cb2 = [Res() for _ in range(4)]
    r_tt = [Res(), Res()]
    r_yh = [Res(), Res()]
    r_thc = [Res(), Res()]
    r_msb = Res()
    r_means = [Res(), Res()]
    r_rs = [Res(), Res()]
    for c in range(4):
        idb = bass.AP(ident, 0, [[128, 128], [0, 31], [1, 128]])
        wb_ = bass.AP(wT, c, [[124, 128], [4, 31], [0, 128]])
        op(DVE, lambda c=c, idb=idb, wb_=wb_: nc.vector.tensor_tensor(out=DIAG[:, c, :, :], in0=idb, in1=wb_, op=ALU.mult),
           reads=[r_cw, r_const], writes=[r_diag[c]])

    def conv_mm(g, c):
        gs = g % 2
        b = bank_get()
        mm_group(bf(b)[:, :], [(DIAG[:, c, j, :], UT[:, c, g * 512 + j:g * 512 + j + 512]) for j in range(31)],
                 [r_diag[c], r_ut], b)
        cfs = CF[gs][:, c, :]
        op(ACT, lambda: nc.scalar.activation(out=cfs, in_=bf(b)[:, :], func=AF.Identity, bias=cvec[:, c:c + 1]),
           reads=[bres(b), r_small], writes=[r_cf[gs][c]])
        bank_put(b)
        op(DVE, lambda: nc.vector.tensor_copy(out=CB[:, c, :], in_=cfs), reads=[r_cf[gs][c]], writes=[r_cb[c]])
        op(ACT, lambda: nc.scalar.activation(out=CB2[:, c, :], in_=cfs, func=AF.Square), reads=[r_cf[gs][c]], writes=[r_cb2[c]])

    def conv_stats(g, fixed_banks=None, defer_finish=False):
        gs = g % 2
        bm, bs = fixed_banks if fixed_banks is not None else (bank_get(), bank_get())
        mm_group(bf(bm)[:, :], [(ones512[:], CB[:, c, :]) for c in range(4)], r_cb + [r_misc], bm)
        mm_group(bf(bs)[:, :], [(ones512[:], CB2[:, c, :]) for c in range(4)], r_cb2, bs)
        op(ACT, lambda: nc.scalar.activation(out=MEANS[gs][:], in_=bf(bm)[:, :], func=AF.Copy), reads=[bres(bm)], writes=[r_means[gs]])
        op(DVE, lambda: nc.vector.tensor_tensor(out=MSB[:], in0=MEANS[gs][:], in1=MEANS[gs][:], op=ALU.mult), reads=[r_means[gs]], writes=[r_msb])
        op(DVE, lambda: nc.vector.tensor_tensor(out=RS[gs][:], in0=bf(bs)[:, :], in1=MSB[:], op=ALU.subtract),
           reads=[bres(bs), r_msb], writes=[r_rs[gs]])
        if fixed_banks is None:
            bank_put(bm)
            bank_put(bs)
        def finish():
            op(ACT, lambda: nc.scalar.activation(out=RS[gs][:], in_=RS[gs][:], func=AF.Sqrt, bias=epst[:, 0:1]), reads=[r_misc], writes=[r_rs[gs]])
            op(DVE, lambda: nc.vector.reciprocal(out=RS[gs][:], in_=RS[gs][:]), writes=[r_rs[gs]])
        if defer_finish:
            return finish
        finish()

    def conv_p2a(g, c):
        gs = g % 2
        s_ = c % 2
        op(DVE, lambda: nc.vector.tensor_tensor(out=TT_[s_][:], in0=CF[gs][:, c, :], in1=MEANS[gs][:], op=ALU.subtract),
           reads=[r_cf[gs][c], r_means[gs]], writes=[r_tt[s_]])
        op(DVE, lambda: nc.vector.tensor_tensor(out=TT_[s_][:], in0=TT_[s_][:], in1=RS[gs][:], op=ALU.mult),
           reads=[r_rs[gs]], writes=[r_tt[s_]])

    def conv_p2b(g, c):
        s_ = c % 2
        sl = slice(g * 512, (g + 1) * 512)
        op(ACT, lambda: nc.scalar.activation(out=THC[s_][:], in_=TT_[s_][:], func=AF.Tanh, scale=cvh[:, c:c + 1], bias=cvh[:, 4 + c:5 + c]),
           reads=[r_tt[s_], r_misc], writes=[r_thc[s_]])
        op(DVE, lambda: nc.vector.tensor_scalar(out=YH[s_][:], in0=TT_[s_][:], scalar1=cvh[:, c:c + 1], scalar2=cvh[:, 4 + c:5 + c],
                                               op0=ALU.mult, op1=ALU.add),
           reads=[r_tt[s_]], writes=[r_yh[s_]])
        op(DVE, lambda: nc.vector.scalar_tensor_tensor(out=MIX[:, c, sl], in0=THC[s_][:], scalar=1.0, in1=YH[s_][:], op0=ALU.add, op1=ALU.mult),
           reads=[r_thc[s_], r_yh[s_]], writes=[])

    def conv_p2(g, c):
        gs = g % 2
        s_ = c % 2
        sl = slice(g * 512, (g + 1) * 512)
        op(DVE, lambda: nc.vector.tensor_tensor(out=TT_[s_][:], in0=CF[gs][:, c, :], in1=MEANS[gs][:], op=ALU.subtract),
           reads=[r_cf[gs][c], r_means[gs]], writes=[r_tt[s_]])
        op(DVE, lambda: nc.vector.tensor_tensor(out=TT_[s_][:], in0=TT_[s_][:], in1=RS[gs][:], op=ALU.mult),
           reads=[r_rs[gs]], writes=[r_tt[s_]])
        op(ACT, lambda: nc.scalar.activation(out=THC[s_][:], in_=TT_[s_][:], func=AF.Tanh, scale=cvh[:, c:c + 1], bias=cvh[:, 4 + c:5 + c]),
           reads=[r_tt[s_], r_misc], writes=[r_thc[s_]])
        op(DVE, lambda: nc.vector.tensor_scalar(out=YH[s_][:], in0=TT_[s_][:], scalar1=cvh[:, c:c + 1], scalar2=cvh[:, 4 + c:5 + c],
                                               op0=ALU.mult, op1=ALU.add),
           reads=[r_tt[s_]], writes=[r_yh[s_]])
        op(DVE, lambda: nc.vector.scalar_tensor_tensor(out=MIX[:, c, sl], in0=THC[s_][:], scalar=1.0, in1=YH[s_][:], op0=ALU.add, op1=ALU.mult),
           reads=[r_thc[s_], r_yh[s_]], writes=[])

    GATE = sb("gate", [128, 8, 64], F32, OV + 153600)
    CMP = sb("cmp", [128, 8, 8, 8], F32, OV + 155648)
    RANK = sb("rank", [128, 64], F32, OV + 157696)
    MBT = sb("mbt", [128, 8, 64], BF16, OV + 157952)
    MASKT2 = sb("maskt2", [64, 1024], BF16, OV + 163072)
    r_maskt = Res()
    r_gate = Res()
    r_cmp = Res()
    r_rank = Res()
    r_mbt = Res()
    r_km = Res()
    def gate_stage_a():
        op(DVE, lambda: nc.vector.memset(KMD[:], 0.0), writes=[r_km])
        op(DVE, lambda: nc.vector.memset(MBT[:], 0.0), writes=[r_mbt])
        for hp in range(4):
            op(DVE, lambda hp=hp: nc.vector.tensor_reduce(out=KM[:, hp, :], in_=KT[:, hp, :].rearrange("p (n k) -> p n k", n=8), axis=AX.X, op=ALU.add),
               writes=[r_km])
            op(DVE, lambda hp=hp: nc.vector.tensor_scalar(out=KMD[0:64, hp, 0:8], in0=KM[0:64, hp, :], scalar1=1.0 / 256.0, scalar2=None, op0=ALU.mult), writes=[r_km])
            op(DVE, lambda hp=hp: nc.vector.tensor_scalar(out=KMD[64:128, hp, 8:16], in0=KM[64:128, hp, :], scalar1=1.0 / 256.0, scalar2=None, op0=ALU.mult), writes=[r_km])

    def gate_stage_b():
        for ti in range(8):
            t = 8 + ti
            own = t // 2
            b = bank_get()

            def gm(t=t, b=b):
                last = None
                for hp in range(4):
                    last = nc.tensor.matmul(bf(b)[:, hp * 16:(hp + 1) * 16], QT[:, hp, t * 128:(t + 1) * 128], KMD[:, hp, :], start=True, stop=True)
                return last
            op(PE, gm, reads=[r_km], writes=[bres(b)])
            op(ACT, lambda ti=ti, b=b: nc.scalar.activation(out=GATE[:, ti, :], in_=bf(b)[:, 0:64], func=AF.Copy), reads=[bres(b)], writes=[r_gate])
            bank_put(b)
            gbase = ti * 64
            in0 = bass.AP(GATE, gbase, [[512, 128], [8, 8], [0, own], [1, own]])
            in1 = bass.AP(GATE, gbase, [[512, 128], [8, 8], [1, own], [0, own]])
            cm = bass.AP(CMP, 0, [[512, 128], [64, 8], [8, own], [1, own]])
            op(DVE, lambda in0=in0, in1=in1, cm=cm: nc.vector.tensor_tensor(out=cm, in0=in0, in1=in1, op=ALU.is_gt), reads=[r_gate], writes=[r_cmp])
            rk = bass.AP(RANK, 0, [[64, 128], [8, 8], [1, own]])
            op(DVE, lambda cm=cm, rk=rk: nc.vector.tensor_reduce(out=rk, in_=cm, axis=AX.X, op=ALU.add), reads=[r_cmp], writes=[r_rank])
            mb = bass.AP(MBT, ti * 64, [[512, 128], [8, 8], [1, own]])
            op(DVE, lambda rk=rk, mb=mb: nc.vector.tensor_scalar(out=mb, in0=rk, scalar1=2.5, scalar2=NEG, op0=ALU.is_gt, op1=ALU.mult), reads=[r_rank], writes=[r_mbt])

    BLKR = sb("blkr", [8, 8, 256], BF16, OV + 158976)
    r_blkr = Res()

    def gate_stage_c():
        op(POOL, lambda: nc.gpsimd.memset(BLKR[:], 1.0), writes=[r_blkr])
        op(POOL, lambda: nc.gpsimd.affine_select(out=BLKR[:], in_=BLKR[:], pattern=[[-1, 8], [0, 256]],
                                                 compare_op=ALU.is_equal, fill=0.0, base=0, channel_multiplier=1), writes=[r_blkr])
        for grp in range(2):
            b = bank_get()

            def tr(grp=grp, b=b):
                last = None
                for i in range(4):
                    ti = grp * 4 + i
                    last = nc.tensor.transpose(bb(b)[0:64, i * 128:(i + 1) * 128], MBT[:, ti, :], ident[:])
                return last
            op(PE, tr, reads=[r_mbt, r_const], writes=[bres(b)])
            op(ACT, lambda grp=grp, b=b: nc.scalar.activation(out=MASKT2[0:64, grp * 512:(grp + 1) * 512], in_=bb(b)[0:64, 0:512], func=AF.Copy),
               reads=[bres(b)], writes=[r_maskt])
            bank_put(b)

    KTZP = sb("ktzp", [128, T], BF16, MIXO + 6 * 4096)
    QTZP = sb("qtzp", [128, T], BF16, MIXO + 7 * 4096)
    r_ktzp, r_qtzp, r_qmp, r_kbp = Res(), Res(), Res(), Res()
    d_kbp, d_qmp = DSem(nc, "dkbp"), DSem(nc, "dqmp")

    def prebuild_head0():
        op(DVE, lambda: nc.vector.memset(KTZP[:], 0.0), writes=[r_ktzp])
        op(DVE, lambda: nc.vector.memset(QTZP[:], 0.0), writes=[r_qtzp])
        op(DVE, lambda: nc.vector.tensor_copy(out=KTZP[0:64, :], in_=KT[0:64, 0, :]), writes=[r_ktzp])
        op(DVE, lambda: nc.vector.tensor_copy(out=QTZP[0:64, :], in_=QT[0:64, 0, :]), writes=[r_qtzp])
        blkr_f0 = BLKR[:].rearrange("p a b -> p (a b)")
        SP.wait(r_ktzp.w)
        r_kbp.w = dma(SP, d_kbp, KTZP[64:72, :], blkr_f0, reads=[r_blkr], nowait_w=True)
        SP.wait(r_qtzp.w)
        dma(SP, d_qmp, QTZP[64:72, 1024:2048], MASKT2[0:8, :], reads=[r_maskt], writes=[r_qmp])

    for g in range(4):
        if g == 3:
            prebuild_head0()
        if g >= 1:
            conv_stats(g - 1)
        conv_mm(g, 0)
        conv_mm(g, 1)
        if g >= 1:
            conv_p2(g - 1, 0)
            conv_p2(g - 1, 1)
        conv_mm(g, 2)
        conv_mm(g, 3)
        if g >= 1:
            conv_p2(g - 1, 2)
            conv_p2(g - 1, 3)
        if g == 0:
            gate_stage_a()
        elif g == 1:
            gate_stage_b()
        elif g == 2:
            gate_stage_c()
    conv_last_pe = PE.last()
    conv_tail_pieces = []
    conv_p2_last = [None]

    PT2 = [sb("pt%d" % i, [128, 1024], BF16, OV + 57344 + 2048 * i) for i in range(4)]
    NSB = [sb("nsb%d" % i, [128, 512], F32, OV + 65536 + 2048 * i) for i in range(3)]
    DSB = [sb("dsb%d" % i, [128, 512], F32, OV + 71680 + 2048 * i) for i in range(3)]
    r_nsb = [Res() for _ in range(3)]
    r_dsb = [Res() for _ in range(3)]
    KTZ = [sb("ktz%d" % i, [128, T], BF16, OV + 77824 + 4096 * i) for i in range(2)]
    QTZ = [sb("qtz%d" % i, [128, T], BF16, OV + 86016 + 4096 * i) for i in range(2)]
    WOUT = sb("wout", [128, 8, D], BF16, OV + 146688)
    r_pt = [Res() for _ in range(4)]
    r_ktz = [Res(), Res()]
    r_qtz = [Res(), Res()]
    r_qm = [Res(), Res()]
    d_qm = [DSem(nc, "dqm0"), DSem(nc, "dqm1")]
    r_att = Res()
    r_wout = Res()
    d_wout = DSem(nc, "dwout")
    for e in (ACT, DVE, POOL, SP):
        e.wait(conv_last_pe)

    op(POOL, lambda: nc.gpsimd.memset(KTZ[0][:], 0.0), writes=[r_ktz[0]])
    op(POOL, lambda: nc.gpsimd.memset(QTZ[0][:], 0.0), writes=[r_qtz[0]])

    def slot1_zero_fill():
        op(DVE, lambda: nc.vector.memset(KTZ[1][:], 0.0), writes=[r_ktz[1]])
        op(DVE, lambda: nc.vector.memset(QTZ[1][:], 0.0), writes=[r_qtz[1]])
        r_kb[1].w = dma(SP, d_kb[1], KTZ[1][0:8, :], blkr_f, reads=[r_blkr], writes=[r_ktz[1]])
    blkr_f = BLKR[:].rearrange("p a b -> p (a b)")
    d_kb = [DSem(nc, "dkb0"), DSem(nc, "dkb1")]
    dma(SP, d_kb[0], KTZ[0][64:72, :], blkr_f, reads=[r_blkr], writes=[r_ktz[0]])
    r_kb = [Res(), Res()]
    r_kb[0].w = r_ktz[0].w

    KTZ.append(KTZP)
    QTZ.append(QTZP)
    r_ktz.append(r_ktzp)
    r_qtz.append(r_qtzp)
    r_qm.append(r_qmp)
    r_kb.append(r_kbp)
    heads = [(hp, hh) for hp in range(4) for hh in range(2)]

    def head_setup(hi):
        hp, hh = heads[hi]
        h = 2 * hp + hh
        r0 = 64 * hh
        op(DVE, lambda: nc.vector.tensor_copy(out=KTZ[hh][r0:r0 + 64, :], in_=KT[r0:r0 + 64, hp, :]), writes=[r_ktz[hh]])
        op(DVE, lambda: nc.vector.tensor_copy(out=QTZ[hh][r0:r0 + 64, :], in_=QT[r0:r0 + 64, hp, :]), writes=[r_qtz[hh]])
        mrow = slice(64, 72) if hh == 0 else slice(0, 8)
        Q_ = SP
        Q_.wait(r_qtz[hh].w)
        dma(Q_, d_qm[hh], QTZ[hh][mrow, 1024:2048], MASKT2[8 * h:8 * h + 8, :], reads=[r_maskt], writes=[r_qm[hh]])

    assert sorted(free_banks) == list(range(7))
    units = []
    for hi, (hp, hh) in enumerate(heads):
        for qc in range(4):
            for p_ in range(2 * qc + 2):
                units.append((hi, hp, hh, qc, p_))
    LOOK = 2
    sinfo = {}
    ocount = [0]
    ncount = [0]
    r_attq = [Res() for _ in range(4)]
    ocur = {}
    setup_done = set()

    def issue_s(u):
        hi, hp, hh, qc, p_ = units[u]
        if hi not in setup_done:
            setup_done.add(hi)
            head_setup(hi)
        if qc == 2 and p_ == 0 and hi + 1 < len(heads) and (hi + 1) not in setup_done:
            setup_done.add(hi + 1)
            head_setup(hi + 1)
        sd = u % 3
        sl = 2 if hi == 0 else hh
        qb = qc * 512
        offs = []
        for j in range(2):
            kt = 2 * p_ + j
            offs.append(max(0, kt - 4 * qc) * 128)

        def f():
            last = None
            for j in range(2):
                kt = 2 * p_ + j
                off = offs[j]
                diag = kt >= 4 * qc
                last = nc.tensor.matmul(dbl[sd][:, j * 512 + off:(j + 1) * 512], KTZ[sl][:, kt * 128:(kt + 1) * 128],
                                        QTZ[sl][:, qb + off:qb + 512], start=True, stop=not diag)
                if diag:
                    last = nc.tensor.matmul(dbl[sd][:, j * 512 + off:j * 512 + off + 128], ident[:], tri[:], start=False, stop=True)
            return last
        op(PE, f, reads=[r_ktz[sl], r_qtz[sl], r_qm[sl], r_kb[sl], r_tri, r_const], writes=[bres(2 * sd), bres(2 * sd + 1)])
        sinfo[u] = (sd, offs)

    def issue_rest(u):
        hi, hp, hh, qc, p_ = units[u]
        sd, offs = sinfo.pop(u)
        slot = u % 4
        qb = qc * 512
        nkt = 4 * qc + 4
        o0 = offs[0]
        op(ACT, lambda: nc.scalar.activation(out=PT2[slot][:, o0:1024], in_=dbl[sd][:, o0:1024], func=AF.Exp, scale=0.125),
           reads=[bres(2 * sd), bres(2 * sd + 1)], writes=[r_pt[slot]])
        if p_ == 0:
            ocur[(hi, qc)] = 6 + (ocount[0] % 2)
            ocount[0] += 1
        ob = ocur[(hi, qc)]

        def pv():
            last = None
            for j in range(2):
                kt = 2 * p_ + j
                off = offs[j]
                last = nc.tensor.matmul(bf(ob)[:, off:512], VS[:, kt, hp, 64 * hh:64 * hh + 128], PT2[slot][:, j * 512 + off:(j + 1) * 512],
                                        start=(kt == 0), stop=(kt == nkt - 1))
            return last
        op(PE, pv, reads=[r_pt[slot], r_vs], writes=[bres(ob)])
        if p_ == 2 * qc + 1:
            s3 = ncount[0] % 3
            ncount[0] += 1
            num = slice(0, 64) if hh == 0 else slice(64, 128)
            den = slice(64, 128) if hh == 0 else slice(0, 64)
            op(DVE, lambda: nc.vector.tensor_copy(out=NSB[s3][:, :], in_=bf(ob)[:, :]), reads=[bres(ob)], writes=[r_nsb[s3]])
            op(DVE, lambda: nc.vector.reciprocal(out=DSB[s3][num, :], in_=NSB[s3][den, :]), reads=[r_nsb[s3]], writes=[r_dsb[s3]])
            r_attq[qc].w = op(POOL, lambda: nc.gpsimd.tensor_tensor(out=MIX[num, 4 + hp, qb:qb + 512], in0=NSB[s3][num, :], in1=DSB[s3][num, :], op=ALU.mult),
                              reads=[r_nsb[s3], r_dsb[s3]], writes=[])
            del ocur[(hi, qc)]

    setup_done.add(0)
    stats3_finish = conv_stats(3, fixed_banks=(7, 6), defer_finish=True)
    slot1_zero_fill()
    conv_tail_pieces.extend([
        stats3_finish,
        lambda: conv_p2a(3, 0),
        lambda: (conv_p2b(3, 0), conv_p2a(3, 1)),
        lambda: (conv_p2b(3, 1), conv_p2a(3, 2)),
        lambda: (conv_p2b(3, 2), conv_p2a(3, 3)),
        lambda: conv_p2b(3, 3),
    ])
    for u in range(min(LOOK, len(units))):
        issue_s(u)
    for u in range(len(units)):
        if u + LOOK < len(units):
            issue_s(u + LOOK)
        issue_rest(u)
        if u % 2 == 1 and conv_tail_pieces:
            conv_tail_pieces.pop(0)()
            conv_p2_last[0] = DVE.last()
            if not conv_tail_pieces:
                w_out_v = w_out_h.ap().rearrange("(kc p) n -> p kc n", p=128)
                dma(POOL, d_wout, WOUT[:], w_out_v, writes=[r_wout], extra=[ACT.last(), DVE.last(), r_kb[0].w, r_kb[1].w, r_kb[2].w])
    att_done = [PE.last(), ACT.last(), DVE.last(), POOL.last()]
    SP.wait(*att_done)
    POOL.wait(*att_done)
    dump("maskt2", MASKT2[:], [64, 1024], BF16)
    dump("qtz0", QTZ[0][:], [128, T], BF16)
    dump("qtz1", QTZ[1][:], [128, T], BF16)
    dump("ktz0", KTZ[0][:], [128, T], BF16)
    dump("attt", MIX[:, 4:8, :].rearrange("p a b -> p (a b)"), [128, 4 * T], BF16)

    H = sb("h", [128, NT, D], F32, OV + 0)
    r_h = [Res() for _ in range(NT)]
    d_h = [DSem(nc, "dh%d" % i) for i in range(NT)]
    NK = [6, 6, 5, 5]
    KOFF = [0, 6, 12, 17]
    WU = [sb("wu0", [128, 8, 2, 768], BF16, OV + 65536), sb("wu1", [128, 8, 2, 768], BF16, OV + 109824)]
    WD = [sb("wd0", [128, 6, D], BF16, OV + 90112), sb("wd1", [128, 6, D], BF16, OV + 134400)]
    THF = [sb("thf%d" % i, [128, 512], F32, OV + 102400 + 2048 * i) for i in range(2)]
    ACTT = [sb("actt%d" % i, [128, 6, 512], BF16, OV + 146688 + 6144 * i) for i in range(2)]
    T1 = [sb("t1_%d" % i, [128, 512], F32, OV + 158976 + 2048 * i) for i in range(2)]
    XNB = sb("xnb", [128, D], BF16, OV + 163072)
    r_wu = [Res(), Res()]
    r_wd = [Res(), Res()]
    d_wu = [DSem(nc, "dwu0"), DSem(nc, "dwu1")]
    d_wd = [DSem(nc, "dwd0"), DSem(nc, "dwd1")]
    r_thf = [Res(), Res()]
    r_actt = [Res(), Res()]
    r_t1 = [Res(), Res()]
    w_up_v = w_up_h.ap().rearrange("(kc p) n -> p kc n", p=128)
    w_down_v = w_down_h.ap().rearrange("(kt p) n -> p kt n", p=128)

    def load_ffn(s, extra=()):
        bufi = (s + 1) % 2
        nk, k0 = NK[s], KOFF[s]
        dma(POOL, d_wu[bufi], WU[bufi][:, :, 0, 0:nk * 128], w_up_v[:, :, k0 * 128:(k0 + nk) * 128], writes=[r_wu[bufi]], extra=extra)
        t1 = dma(POOL, d_wu[bufi], WU[bufi][:, :, 1, 0:nk * 128], w_up_v[:, :, DFF + k0 * 128:DFF + (k0 + nk) * 128], nowait_w=True)
        r_wu[bufi].w = t1
        dma(POOL, d_wd[bufi], WD[bufi][:, 0:nk, :], w_down_v[:, k0:k0 + nk, :], writes=[r_wd[bufi]], extra=extra)

    load_ffn(0)
    load_ffn(1)
    r_hn = [Res() for _ in range(4)]
    r_xnb = [Res(), Res()]

    def sq_tile(t, g):
        op(ACT, lambda: nc.scalar.activation(out=bf(7)[:, :], in_=H[:, t, 0:512], func=AF.Square, accum_out=ssq[:, t:t + 1]),
           reads=[r_h[t]], writes=[r_ssq_g[g]])
        op(ACT, lambda: nc.scalar.activation(out=bf(7)[:, :], in_=H[:, t, 512:1024], func=AF.Square, accum_out=ssq2[:, t:t + 1]),
           reads=[r_h[t]], writes=[r_ssq_g[g]])

    def rstd_group(g):
        cs = slice(4 * g, 4 * g + 4)
        op(DVE, lambda: nc.vector.tensor_tensor(out=ssq[:, cs], in0=ssq[:, cs], in1=ssq2[:, cs], op=ALU.add), writes=[r_ssq_g[g]])
        op(ACT, lambda: nc.scalar.activation(out=std[:, cs], in_=ssq[:, cs], func=AF.Sqrt, scale=1.0 / D, bias=epst[:, 0:1]),
           reads=[r_ssq_g[g], r_misc], writes=[r_rstd_g[g]])
        op(DVE, lambda: nc.vector.reciprocal(out=rstd[:, cs], in_=std[:, cs]), writes=[r_rstd_g[g]])
        op(DVE, lambda: nc.vector.memset(ssq[:, cs], 0.0), reads=[r_rstd_g[g]], writes=[r_ssq_g[g]])
        op(DVE, lambda: nc.vector.memset(ssq2[:, cs], 0.0), writes=[r_ssq_g[g]])

    def norm_h_group(g, gslot, extra=()):
        for tt in range(4):
            t = 4 * g + tt
            b = bank_get()
            for half in range(2):
                cs = slice(half * 512, (half + 1) * 512)
                op(DVE, lambda: nc.vector.scalar_tensor_tensor(out=XNB[:, cs], in0=H[:, t, cs], scalar=rstd[:, t:t + 1], in1=G[gslot][:, cs],
                                                              op0=ALU.mult, op1=ALU.mult),
                   reads=[r_h[t], r_rstd_g[g], r_g[gslot]], writes=[r_xnb[half]])

                def tr(half=half, b=b):
                    last = None
                    for kc in range(4 * half, 4 * half + 4):
                        last = nc.tensor.transpose(bb(b)[:, kc * 128:(kc + 1) * 128], XNB[:, kc * 128:(kc + 1) * 128], ident[:])
                    return last
                op(PE, tr, reads=[r_xnb[half], r_const], writes=[bres(b)] if half == 0 else [])
            bres(b).w = PE.last()
            op(ACT, lambda: nc.scalar.activation(out=MIX[:, :, t * 128:(t + 1) * 128], in_=bb(b)[:, 0:1024].rearrange("p (a b) -> p a b", a=8), func=AF.Copy),
               reads=[bres(b)], writes=[r_hn[g]] if tt == 0 else [], extra=extra)
            bank_put(b)
        r_hn[g].w = ACT.last()

    for t in range(NT):
        dma(SP, d_h[t], H[:, t, :], x_d[t * 128:(t + 1) * 128, :], writes=[r_h[t]])

    def a3_group(g):
        for tt in range(4):
            t = 4 * g + tt
            for half in range(2):
                b = bank_get()
                mm_group(bf(b)[:, :], [(MIX[:, kc, t * 128:(t + 1) * 128], WOUT[:, kc, half * 512:(half + 1) * 512]) for kc in range(8)],
                         [r_wout, r_attq[g]], b, extra=[conv_p2_last[0]])
                hs = H[:, t, half * 512:(half + 1) * 512]
                op(DVE, lambda: nc.vector.tensor_tensor(out=hs, in0=bf(b)[:, :], in1=hs, op=ALU.add), reads=[bres(b)], writes=[r_h[t]])
                bank_put(b)
            sq_tile(t, g)
        rstd_group(g)
        return PE.last()

    XN2 = [sb("xn2_%d" % i, [128, D], BF16, OV + 102400 + 2048 * i) for i in range(3)] + [XNB]
    r_xn2 = [Res() for _ in range(4)]

    def norm2_stt(g):
        for tt in range(4):
            t = 4 * g + tt
            op(DVE, lambda: nc.vector.scalar_tensor_tensor(out=XN2[tt][:], in0=H[:, t, :], scalar=rstd[:, t:t + 1], in1=G[1][:],
                                                          op0=ALU.mult, op1=ALU.mult),
               reads=[r_h[t], r_rstd_g[g], r_g[1]], writes=[r_xn2[tt]])

    def norm2_tr(g, extra=()):
        for tt in range(4):
            t = 4 * g + tt
            b = bank_get()

            def tr(b=b, tt=tt):
                last = None
                for kc in range(8):
                    last = nc.tensor.transpose(bb(b)[:, kc * 128:(kc + 1) * 128], XN2[tt][:, kc * 128:(kc + 1) * 128], ident[:])
                return last
            op(PE, tr, reads=[r_xn2[tt], r_const], writes=[bres(b)])
            op(ACT, lambda: nc.scalar.activation(out=MIX[:, :, t * 128:(t + 1) * 128], in_=bb(b)[:, 0:1024].rearrange("p (a b) -> p a b", a=8), func=AF.Copy),
               reads=[bres(b)], writes=[r_hn[g]] if tt == 0 else [], extra=extra)
            bank_put(b)
        r_hn[g].w = ACT.last()
        return PE.last()

    a3_done = {}
    norm2_pe_last = None
    for g in range(5):
        if g < 4:
            a3_done[g] = a3_group(g)
        if g >= 1:
            norm2_pe_last = norm2_tr(g - 1, extra=[a3_done[g - 1]])
        if g < 4:
            norm2_stt(g)
    ACT.wait(norm2_pe_last)
    load_gain(2, 0)
    load_gain(3, 1)
    dump("h1", H[:, 0:2, :].rearrange("p a b -> p (a b)"), [128, 2 * D], F32)

    fsteps = [(s, tg) for s in range(4) for tg in range(4)]

    def ffn_up(i):
        s, tg = fsteps[i]
        bufi = (s + 1) % 2
        ab = i % 2
        nk = NK[s]
        for kt in range(nk):
            bg_, bu_ = bank_get(), bank_get()
            rhs = [MIX[:, kc, tg * 512:(tg + 1) * 512] for kc in range(8)]
            mm_group(bf(bg_)[:, :], [(WU[bufi][:, kc, 0, kt * 128:(kt + 1) * 128], rhs[kc]) for kc in range(8)], [r_hn[tg], r_wu[bufi]], bg_)
            mm_group(bf(bu_)[:, :], [(WU[bufi][:, kc, 1, kt * 128:(kt + 1) * 128], rhs[kc]) for kc in range(8)], [r_hn[tg], r_wu[bufi]], bu_)
            sl = kt % 2
            op(ACT, lambda: nc.scalar.activation(out=THF[sl][:], in_=bf(bg_)[:, :], func=AF.Tanh, scale=0.5), reads=[bres(bg_)], writes=[r_thf[sl]])
            op(DVE, lambda: nc.vector.scalar_tensor_tensor(out=T1[sl][:], in0=THF[sl][:], scalar=1.0, in1=bf(bg_)[:, :], op0=ALU.add, op1=ALU.mult),
               reads=[r_thf[sl], bres(bg_)], writes=[r_t1[sl]])
            op(DVE, lambda: nc.vector.scalar_tensor_tensor(out=ACTT[ab][:, kt, :], in0=T1[sl][:], scalar=0.5, in1=bf(bu_)[:, :], op0=ALU.mult, op1=ALU.mult),
               reads=[r_t1[sl], bres(bu_)], writes=[r_actt[ab]] if kt == 0 else [])
            bank_put(bg_)
            bank_put(bu_)
        r_actt[ab].w = DVE.last()

    def ffn_down(i):
        s, tg = fsteps[i]
        bufi = (s + 1) % 2
        ab = i % 2
        nk = NK[s]
        for tt in range(4):
            t = 4 * tg + tt
            for half in range(2):
                b = bank_get()
                mm_group(bf(b)[:, :], [(ACTT[ab][:, kt, tt * 128:(tt + 1) * 128], WD[bufi][:, kt, half * 512:(half + 1) * 512]) for kt in range(nk)],
                         [r_actt[ab], r_wd[bufi]], b)
                hs = H[:, t, half * 512:(half + 1) * 512]
                op(DVE, lambda: nc.vector.tensor_tensor(out=hs, in0=bf(b)[:, :], in1=hs, op=ALU.add), reads=[bres(b)], writes=[r_h[t]])
                bank_put(b)
            if s == 3:
                sq_tile(t, tg)
        if tg == 3 and s + 2 < 4:
            load_ffn(s + 2)

    WG = sb("wg", [128, 8, D], BF16, OV + 109824)
    WP = sb("wp", [128, 2, D], BF16, OV + 109824 + 16384)
    PTT = sb("ptt", [128, 2, T], BF16, OV + 130304)
    PBALL = sb("pball", [128, NT, 256], BF16, OV + 138496)
    r_pball = Res()
    d_pball = DSem(nc, "dpball")
    r_wg = Res()
    d_wg = DSem(nc, "dwg")
    r_ptt = Res()
    d_out = [DSem(nc, "dout%d" % i) for i in range(4)]
    outs = []

    def ple_prefetch():
        w_gate_v = w_gate_h.ap().rearrange("(kc p) n -> p kc n", p=128)
        w_ple_v = w_ple_h.ap().rearrange("(kc p) n -> p kc n", p=128)
        pe_t = PE.last()
        dma(POOL, d_wg, WG[:], w_gate_v, extra=[pe_t])
        tgp = dma(POOL, d_wg, WP[:], w_ple_v)
        r_wg.w = tgp
        dma(POOL, d_pball, PBALL[:], p_d.rearrange("(t p) f -> p t f", p=128), writes=[r_pball])
        ACT.wait(pe_t)

    def p_transposes():
        for t in range(NT):
            b = bank_get()

            def tr(b=b, t=t):
                nc.tensor.transpose(bb(b)[:, 0:128], PBALL[:, t, 0:128], ident[:])
                return nc.tensor.transpose(bb(b)[:, 128:256], PBALL[:, t, 128:256], ident[:])
            op(PE, tr, reads=[r_pball, r_const], writes=[bres(b)])
            op(ACT, lambda: nc.scalar.activation(out=PTT[:, :, t * 128:(t + 1) * 128], in_=bb(b)[:, 0:256].rearrange("p (a b) -> p a b", a=2), func=AF.Copy),
               reads=[bres(b)], writes=[])
            bank_put(b)
        r_ptt.w = ACT.last()

    def ple_group(g):
        for tt in range(4):
            t = 4 * g + tt
            for half in range(2):
                bga, bpl = bank_get(), bank_get()
                mm_group(bf(bga)[:, :], [(MIX[:, kc, t * 128:(t + 1) * 128], WG[:, kc, half * 512:(half + 1) * 512]) for kc in range(8)], [r_hn[g], r_wg], bga)
                mm_group(bf(bpl)[:, :], [(PTT[:, kc, t * 128:(t + 1) * 128], WP[:, kc, half * 512:(half + 1) * 512]) for kc in range(2)], [r_ptt, r_wg], bpl)
                sl = (2 * t + half) % 2
                op(ACT, lambda: nc.scalar.activation(out=THF[sl][:], in_=bf(bga)[:, :], func=AF.Tanh, scale=0.5), reads=[bres(bga)], writes=[r_thf[sl]])
                op(DVE, lambda: nc.vector.scalar_tensor_tensor(out=T1[sl][:], in0=THF[sl][:], scalar=1.0, in1=bf(bpl)[:, :], op0=ALU.add, op1=ALU.mult),
                   reads=[r_thf[sl], bres(bpl)], writes=[r_t1[sl]])
                hs = H[:, t, half * 512:(half + 1) * 512]
                op(DVE, lambda: nc.vector.scalar_tensor_tensor(out=hs, in0=T1[sl][:], scalar=0.5, in1=hs, op0=ALU.mult, op1=ALU.add),
                   reads=[r_t1[sl]], writes=[r_h[t]])
                bank_put(bga)
                bank_put(bpl)
            sq_tile(t, g)
        rstd_group(g)

    XNS = sb("xns", [128, 4, D], BF16, OV + 138496)
    r_xns = [Res() for _ in range(4)]

    def norm3_stt(g, extra=()):
        for tt in range(4):
            t = 4 * g + tt
            op(DVE, lambda: nc.vector.scalar_tensor_tensor(out=XNS[:, tt, :], in0=H[:, t, :], scalar=rstd[:, t:t + 1], in1=G[0][:],
                                                          op0=ALU.mult, op1=ALU.mult),
               reads=[r_h[t], r_rstd_g[g], r_g[0]], writes=[r_xns[tt]], extra=extra)

    def norm3_tr(g):
        for tt in range(4):
            t = 4 * g + tt
            b = bank_get()

            def tr(b=b, tt=tt):
                last = None
                for kc in range(8):
                    last = nc.tensor.transpose(bb(b)[:, kc * 128:(kc + 1) * 128], XNS[:, tt, kc * 128:(kc + 1) * 128], ident[:])
                return last
            op(PE, tr, reads=[r_xns[tt], r_const], writes=[bres(b)])
            op(ACT, lambda: nc.scalar.activation(out=MIX[:, :, t * 128:(t + 1) * 128], in_=bb(b)[:, 0:1024].rearrange("p (a b) -> p a b", a=8), func=AF.Copy),
               reads=[bres(b)], writes=[r_hn[g]] if tt == 0 else [])
            bank_put(b)
        r_hn[g].w = ACT.last()

    def final_group(g):
        for tt in range(4):
            t = 4 * g + tt
            op(DVE, lambda: nc.vector.scalar_tensor_tensor(out=H[:, t, :], in0=H[:, t, :], scalar=rstd[:, t:t + 1], in1=G[1][:], op0=ALU.mult, op1=ALU.mult),
               reads=[r_rstd_g[g], r_g[1]], writes=[r_h[t]])
            outs.append(dma(SP, d_out[t % 4], out_d[t * 128:(t + 1) * 128, :], H[:, t, :], reads=[r_h[t]]))

    ffn_up(0)
    for i in range(len(fsteps)):
        if i + 1 < len(fsteps):
            ffn_up(i + 1)
        ffn_down(i)
        s, tg = fsteps[i]
        if (s, tg) == (2, 3):
            ple_prefetch()
        if s == 3:
            rstd_group(tg)
            if tg == 0:
                p_transposes()
                norm3_stt(tg, extra=[PE.last()])
            else:
                norm3_stt(tg)
            if tg >= 1:
                ple_group(tg - 1)
            norm3_tr(tg)
            if tg >= 2:
                final_group(tg - 2)
    final_group(2)
    ple_group(3)
    final_group(3)
    SP.wait(*outs[-4:])
    for e in engines:
        e.wait(*outs[-4:])
    return nc, dbg


_CACHE = {}


def _in_maps(inputs, cores):
    f = lambda a: np.ascontiguousarray(np.asarray(a, dtype=np.float32))
    x = f(inputs["x"])
    p = f(inputs["p"])[0]
    pos = np.ascontiguousarray(np.asarray(inputs["positions"]).astype(np.int32))
    gains = np.ascontiguousarray(np.stack([f(inputs["norm_mix_g"])[0], f(inputs["norm_ffn_g"])[0],
                                           f(inputs["norm_ple_g"])[0], f(inputs["final_norm_g"])], axis=0))
    cvec = np.ascontiguousarray(np.stack([f(inputs["conv_b"])[0], f(inputs["conv_ln_g"])[0], f(inputs["conv_ln_b"])[0]], axis=0))
    shared = {
        "gains": gains, "w_in": f(inputs["w_in"])[0], "conv_w": f(inputs["conv_w"])[0], "cvec": cvec,
        "w_out": f(inputs["w_out"])[0], "w_up": f(inputs["w_ffn_up"])[0], "w_down": f(inputs["w_ffn_down"])[0],
        "w_gate": f(inputs["w_ple_gate"])[0], "w_ple": f(inputs["w_ple_proj"])[0],
    }
    maps = []
    for b in cores:
        m = dict(shared)
        m["x"] = np.ascontiguousarray(x[b])
        m["p"] = np.ascontiguousarray(p[b])
        m["pos"] = np.ascontiguousarray(pos[b])
        maps.append(m)
    return maps


def kernel(**inputs):
    if "nc" not in _CACHE:
        _CACHE["nc"] = build(False)[0]
    nc = _CACHE["nc"]
    maps = _in_maps(inputs, list(range(8)))
    res = run_bass_kernel_spmd(nc, maps, core_ids=list(range(8)))
    out = np.stack([np.asarray(r["out"], dtype=np.float32) for r in res.results], axis=0)
    return out
```

```python
import math
import numpy as np
import concourse.bass as bass
import concourse.mybir as mybir
from concourse.bass_utils import run_bass_kernel_spmd

F32 = mybir.dt.float32
BF16 = mybir.dt.bfloat16
I32 = mybir.dt.int32
AF = mybir.ActivationFunctionType
ALU = mybir.AluOpType
AX = mybir.AxisListType

T = 2048
D = 1024
NT = 16
DFF = 2816
EPS = 1e-6
NEG = -480.0
BASE = 16512
SB_LIMIT = 229344 - BASE
TWO_PI = 2.0 * math.pi


class Res:
    __slots__ = ("w", "r")

    def __init__(self):
        self.w = None
        self.r = {}


class Eng:
    def __init__(self, nc, eng, name):
        self.e = eng
        self.key = name
        self.sem = nc.alloc_semaphore(name + "_cnt")
        self.n = 0
        self.seen = {}

    def wait(self, *ts):
        for t in ts:
            if t is None:
                continue
            if isinstance(t, (list, tuple)) and len(t) and isinstance(t[0], (list, tuple)):
                self.wait(*t)
                continue
            sem, n, key = t
            if self.seen.get(key, 0) >= n:
                continue
            self.seen[key] = n
            self.e.wait_ge(sem, n)

    def mark(self, inst):
        self.n += 1
        inst.then_inc(self.sem, 1)
        return (self.sem, self.n, self.key)

    def last(self):
        return (self.sem, self.n, self.key) if self.n else None


class DSem:
    def __init__(self, nc, name):
        self.sem = nc.alloc_semaphore(name)
        self.cnt = 0
        self.key = name


def op(E, fn, reads=(), writes=(), extra=()):
    for r in reads:
        E.wait(r.w)
    for w in writes:
        E.wait(w.w)
        E.wait(*w.r.values())
    E.wait(*extra)
    t = E.mark(fn())
    for r in reads:
        r.r[E.key] = t
    for w in writes:
        w.w = t
        w.r = {}
    return t


def dma(Q, ds, out_ap, in_ap, reads=(), writes=(), extra=(), nowait_w=False):
    for r in reads:
        Q.wait(r.w)
    for w in writes:
        if not nowait_w:
            Q.wait(w.w)
        Q.wait(*w.r.values())
    Q.wait(*extra)
    inst = Q.e.dma_start(out=out_ap, in_=in_ap)
    ds.cnt += 16
    inst.then_inc(ds.sem, 16)
    t = (ds.sem, ds.cnt, ds.key)
    for r in reads:
        r.r[ds.key] = t
    for w in writes:
        w.w = t
        w.r = {}
    return t


def build(debug=False):
    nc = bass.Bass("TRN2", target_bir_lowering=False)
    dt_in = lambda name, shape, dt=F32: nc.dram_tensor(name, shape, dt, kind="ExternalInput")
    x_h = dt_in("x", [T, D])
    p_h = dt_in("p", [T, 256])
    pos_h = dt_in("pos", [T], I32)
    gains_h = dt_in("gains", [4, D])
    w_in_h = dt_in("w_in", [D, 2560])
    conv_w_h = dt_in("conv_w", [31, 512])
    cvec_h = dt_in("cvec", [3, 512])
    w_out_h = dt_in("w_out", [D, D])
    w_up_h = dt_in("w_up", [D, 2 * DFF])
    w_down_h = dt_in("w_down", [DFF, D])
    w_gate_h = dt_in("w_gate", [D, D])
    w_ple_h = dt_in("w_ple", [256, D])
    out_h = nc.dram_tensor("out", [T, D], F32, kind="ExternalOutput")
    x_d, p_d, out_d = x_h.ap(), p_h.ap(), out_h.ap()
    dbg = {}

    def sb(name, shape, dt, off):
        assert off % 32 == 0, (name, off)
        nbytes = int(np.prod(shape[1:])) * (4 if dt in (F32, I32) else 2)
        assert off + nbytes <= SB_LIMIT, (name, off, nbytes)
        return nc.alloc_sbuf_tensor_at(name, list(shape), dt, offset=BASE + off)

    PE = Eng(nc, nc.tensor, "pe")
    ACT = Eng(nc, nc.scalar, "act")
    DVE = Eng(nc, nc.vector, "dve")
    POOL = Eng(nc, nc.gpsimd, "pool")
    SP = Eng(nc, nc.sync, "sp")
    engines = [PE, ACT, DVE, POOL]

    def barrier(also=()):
        ts = [e.last() for e in engines]
        for e in engines:
            e.wait(*ts)
        for q in also:
            q.wait(*ts)
        return ts

    dbl = [nc.alloc_psum_tensor("pd%d" % i, [128, 1024], F32) for i in range(4)]
    bank_res = [Res() for _ in range(8)]
    free_banks = list(range(7))

    def bank_get():
        return free_banks.pop(0)

    def bank_put(i):
        free_banks.append(i)

    def bf(i):
        return dbl[i // 2][:, (i % 2) * 512:(i % 2 + 1) * 512]

    def bb(i):
        return bf(i).bitcast(BF16)

    def bres(i):
        return bank_res[i]

    ident = sb("ident", [128, 128], BF16, 0)
    tri = sb("tri", [128, 128], BF16, 256)
    ones512 = sb("ones512", [128, 128], BF16, 512)
    blk = sb("blk", [8, 8, 128], BF16, 768)
    invf = sb("invf", [128, 8], F32, 2816)
    epst = sb("epst", [128, 1], F32, 2848)
    posi = sb("posi", [128, 16], I32, 2880)
    posf = sb("posf", [128, 16], F32, 2944)
    COS = sb("cos", [128, 16, 8], F32, 3008)
    SIN = sb("sin", [128, 16, 8], F32, 3520)
    ANG = sb("ang", [128, 16, 8], F32, 4032)
    ssq = sb("ssq", [128, 16], F32, 4544)
    std = sb("std", [128, 16], F32, 4608)
    rstd = sb("rstd", [128, 16], F32, 4672)
    cvec = sb("cvec", [128, 12], F32, 4736)
    cvh = sb("cvh", [128, 8], F32, 4800)
    wT = sb("wT", [128, 124], BF16, 4832)
    KM = sb("km", [128, 4, 8], F32, 5088)
    KMD = sb("kmd", [128, 4, 16], BF16, 5216)
    ssq2 = sb("ssq2", [128, 16], F32, 5344 + 32)
    G0 = sb("g0", [128, D], F32, 5440)
    G1 = sb("g1", [128, D], F32, 9536)
    MIXO = 13632
    MIX = sb("mix", [128, 8, T], BF16, MIXO)
    ANG2 = sb("ang2", [128, 256], F32, MIXO)
    QQ = sb("qq", [128, 256], F32, MIXO + 1024)
    KI = sb("ki", [128, 256], I32, MIXO + 2048)
    KF = sb("kf", [128, 256], F32, MIXO + 3072)
    MM = sb("mm", [128, 256], F32, MIXO + 4096)
    OV = MIXO + 32768
    assert OV % 32 == 0

    r_const = Res()
    r_g = [Res(), Res()]
    d_g = [DSem(nc, "dg0"), DSem(nc, "dg1")]
    G = [G0, G1]

    def load_gain(idx, slot, extra=()):
        src = bass.AP(gains_h, idx * D, [[0, 128], [1, D]])
        return dma(SP, d_g[slot], G[slot][:], src, writes=[r_g[slot]], extra=extra)


    XS = [sb("xs%d" % i, [128, D], F32, OV + 131328 + 4096 * i) for i in range(3)] + [sb("xs3", [128, D], F32, OV + 161024)]
    r_xs = [Res() for _ in range(4)]
    d_xs = [DSem(nc, "dxs%d" % i) for i in range(4)]
    for tt in range(4):
        dma(SP, d_xs[tt], XS[tt][:], x_d[tt * 128:(tt + 1) * 128, :], writes=[r_xs[tt]])

    d_small = DSem(nc, "dsmall")
    r_small = Res()
    with nc.allow_non_contiguous_dma(reason="tiny param gathers"):
        for q4 in range(4):
            t_sm = dma(SP, d_small, posi[:, q4 * 4:(q4 + 1) * 4],
                       bass.AP(pos_h, q4 * 512, [[1, 128], [128, 4]]), nowait_w=True)
        for v in range(3):
            t_sm = dma(SP, d_small, cvec[:, v * 4:(v + 1) * 4],
                       bass.AP(cvec_h, v * 512, [[1, 128], [128, 4]]), nowait_w=True)
    r_small.w = t_sm
    load_gain(0, 0)

    WIN = sb("win", [128, 8, 2560], BF16, OV + 73984)
    w_in_v = w_in_h.ap().rearrange("(kc p) n -> p kc n", p=128)
    r_win = [Res() for _ in range(10)]
    d_win = [DSem(nc, "dwin%d" % i) for i in range(10)]
    for i in (0, 2, 1, 3, 4, 5, 6, 7, 8, 9):
        dma(POOL, d_win[i], WIN[:, :, i * 256:(i + 1) * 256], w_in_v[:, :, i * 256:(i + 1) * 256], writes=[r_win[i]])

    op(POOL, lambda: nc.gpsimd.memset(ident[:], 1.0), writes=[r_const])
    op(POOL, lambda: nc.gpsimd.affine_select(out=ident[:], in_=ident[:], pattern=[[-1, 128]],
                                             compare_op=ALU.is_equal, fill=0.0, base=0, channel_multiplier=1),
       writes=[r_const])
    r_tri = Res()
    op(POOL, lambda: nc.gpsimd.memset(tri[:], NEG), writes=[r_tri])
    op(POOL, lambda: nc.gpsimd.affine_select(out=tri[:], in_=tri[:], pattern=[[-1, 128]],
                                             compare_op=ALU.is_gt, fill=0.0, base=0, channel_multiplier=1),
       writes=[r_tri])
    r_blk = Res()
    op(POOL, lambda: nc.gpsimd.memset(blk[:], 1.0), writes=[r_blk])
    op(POOL, lambda: nc.gpsimd.affine_select(out=blk[:], in_=blk[:], pattern=[[-1, 8], [0, 128]],
                                             compare_op=ALU.is_equal, fill=0.0, base=0, channel_multiplier=1),
       writes=[r_blk])
    r_misc = Res()
    op(DVE, lambda: nc.vector.memset(ones512[:], 1.0 / 512.0), writes=[r_misc])
    op(DVE, lambda: nc.vector.memset(epst[:], EPS), writes=[r_misc])
    for f in range(8):
        val = 500000.0 ** (-(2.0 * f) / 16.0)
        op(DVE, lambda f=f, val=val: nc.vector.memset(invf[:, f:f + 1], float(np.float32(val))), writes=[r_misc])
    op(DVE, lambda: nc.vector.memset(ssq[:], 0.0), writes=[r_misc])
    op(DVE, lambda: nc.vector.memset(ssq2[:], 0.0), writes=[r_misc])

    r_rope = Res()

    QT = sb("qt", [128, 4, T], BF16, OV + 0)
    KT = sb("kt", [128, 4, T], BF16, OV + 16384)
    VS = sb("vs", [128, 16, 4, 192], BF16, OV + 32768)
    UT = sb("ut", [128, 4, 2080], BF16, OV + 57344)
    HNT = [sb("hnt%d" % i, [128, 8, 512], BF16, OV + 114944 + 8192 * i) for i in range(2)]
    XN = [sb("xn%d" % i, [128, D], BF16, OV + 143616 + 2048 * i) for i in range(2)]
    QTOK = sb("qtok", [128, 4, 512], BF16, OV + 147712)
    KTOK = sb("ktok", [128, 4, 512], BF16, OV + 151808)
    THA = [sb("tha%d" % i, [128, 512], F32, OV + 155904 + 2048 * i) for i in range(2)]
    RTMP = [sb("rtmp%d" % i, [128, 8, 8], F32, OV + 160000 + 256 * i) for i in range(4)]

    r_qt = [Res() for _ in range(4)]
    r_kt = [Res() for _ in range(4)]
    r_vs = Res()
    r_ut = Res()
    r_hnt = [Res(), Res()]
    r_ssq_g = [Res() for _ in range(4)]
    r_rstd_g = [Res() for _ in range(4)]
    r_xn = [Res(), Res()]
    r_qtok = Res()
    r_ktok = Res()
    r_tha = [Res(), Res()]
    r_rtmp = [Res() for _ in range(4)]
    r_junk = Res()
    r_ssq = Res()
    r_rstd = Res()

    d_cw = DSem(nc, "dcw")

    op(DVE, lambda: nc.vector.memset(VS[:, :, :, 64:128], 1.0), writes=[r_vs])
    op(DVE, lambda: nc.vector.memset(UT[:, :, 0:30], 0.0), writes=[r_ut])

    def sumsq_pass(src_fn, nslots_res, t_list):
        pass

    def pass1(g):
        cs = slice(4 * g, 4 * g + 4)
        for tt in range(4):
            t = 4 * g + tt
            if g > 0:
                dma(SP, d_xs[tt], XS[tt][:], x_d[t * 128:(t + 1) * 128, :], writes=[r_xs[tt]])
            op(ACT, lambda tt=tt, t=t: nc.scalar.activation(out=bf(7)[:, :], in_=XS[tt][:, 0:512], func=AF.Square, accum_out=ssq[:, t:t + 1]),
               reads=[r_xs[tt], r_misc], writes=[r_ssq_g[g]])
            op(ACT, lambda tt=tt, t=t: nc.scalar.activation(out=bf(7)[:, :], in_=XS[tt][:, 512:1024], func=AF.Square, accum_out=ssq2[:, t:t + 1]),
               reads=[r_xs[tt]], writes=[r_ssq_g[g]])
        op(DVE, lambda: nc.vector.tensor_tensor(out=ssq[:, cs], in0=ssq[:, cs], in1=ssq2[:, cs], op=ALU.add), writes=[r_ssq_g[g]])
        op(ACT, lambda: nc.scalar.activation(out=std[:, cs], in_=ssq[:, cs], func=AF.Sqrt, scale=1.0 / D, bias=epst[:, 0:1]),
           reads=[r_ssq_g[g], r_misc], writes=[r_rstd_g[g]])
        op(DVE, lambda: nc.vector.reciprocal(out=rstd[:, cs], in_=std[:, cs]), writes=[r_rstd_g[g]])
        op(DVE, lambda: nc.vector.memset(ssq[:, cs], 0.0), reads=[r_rstd_g[g]], writes=[r_ssq_g[g]])
        op(DVE, lambda: nc.vector.memset(ssq2[:, cs], 0.0), writes=[r_ssq_g[g]])

    def finish_rstd():
        op(DVE, lambda: nc.vector.tensor_tensor(out=ssq[:], in0=ssq[:], in1=ssq2[:], op=ALU.add), writes=[r_ssq])
        op(ACT, lambda: nc.scalar.activation(out=std[:], in_=ssq[:], func=AF.Sqrt, scale=1.0 / D, bias=epst[:, 0:1]),
           reads=[r_ssq, r_misc], writes=[r_rstd])
        op(DVE, lambda: nc.vector.reciprocal(out=rstd[:], in_=std[:]), writes=[r_rstd])
        op(DVE, lambda: nc.vector.memset(ssq[:], 0.0), reads=[r_rstd], writes=[r_ssq])
        op(DVE, lambda: nc.vector.memset(ssq2[:], 0.0), writes=[r_ssq])

    def norm_tile(src_ap, src_res, t, gslot, dst_ap, dst_res, xslot, rres=None):
        op(DVE, lambda: nc.vector.scalar_tensor_tensor(out=XN[xslot][:], in0=src_ap, scalar=rstd[:, t:t + 1], in1=G[gslot][:],
                                                      op0=ALU.mult, op1=ALU.mult),
           reads=[src_res, rres if rres is not None else r_rstd, r_g[gslot]], writes=[r_xn[xslot]])
        b = bank_get()

        def tr():
            last = None
            for kc in range(8):
                last = nc.tensor.transpose(bb(b)[:, kc * 128:(kc + 1) * 128], XN[xslot][:, kc * 128:(kc + 1) * 128], ident[:])
            return last
        op(PE, tr, reads=[r_xn[xslot], r_const], writes=[bres(b)])
        op(ACT, lambda: nc.scalar.activation(out=dst_ap, in_=bb(b)[:, 0:1024].rearrange("p (a b) -> p a b", a=8), func=AF.Copy),
           reads=[bres(b)], writes=[dst_res])
        bank_put(b)

    def rope(ps_i, dst, dst_res, t, tt):
        src = bf(ps_i)[:, :].rearrange("p (h d) -> p h d", h=8)
        dv = dst[:, tt, :].rearrange("p (h d) -> p h d", h=8)
        op(ACT, lambda: nc.scalar.activation(out=dv[:, :, 16:64], in_=src[:, :, 16:64], func=AF.Copy),
           reads=[bres(ps_i)], writes=[])
        cb = bass.AP(COS, t * 8, [[128, 128], [0, 8], [1, 8]])
        snb = bass.AP(SIN, t * 8, [[128, 128], [0, 8], [1, 8]])
        x1 = src[:, :, 0:8]
        x2 = src[:, :, 8:16]
        op(DVE, lambda: nc.vector.tensor_tensor(out=RTMP[0][:], in0=x1, in1=cb, op=ALU.mult), reads=[bres(ps_i), r_rope], writes=[r_rtmp[0]])
        op(DVE, lambda: nc.vector.tensor_tensor(out=RTMP[1][:], in0=x2, in1=snb, op=ALU.mult), reads=[bres(ps_i)], writes=[r_rtmp[1]])
        op(DVE, lambda: nc.vector.tensor_tensor(out=RTMP[2][:], in0=x2, in1=cb, op=ALU.mult), reads=[bres(ps_i)], writes=[r_rtmp[2]])
        op(DVE, lambda: nc.vector.tensor_tensor(out=RTMP[3][:], in0=x1, in1=snb, op=ALU.mult), reads=[bres(ps_i)], writes=[r_rtmp[3]])
        op(DVE, lambda: nc.vector.tensor_tensor(out=dv[:, :, 0:8], in0=RTMP[0][:], in1=RTMP[1][:], op=ALU.subtract),
           reads=[r_rtmp[0], r_rtmp[1]], writes=[])
        return op(DVE, lambda: nc.vector.tensor_tensor(out=dv[:, :, 8:16], in0=RTMP[2][:], in1=RTMP[3][:], op=ALU.add),
                  reads=[r_rtmp[2], r_rtmp[3]], writes=[])

    def mm_group(out_ap, pairs, reads, bank_i, extra=()):
        def f():
            last = None
            n = len(pairs)
            for i, (l, r) in enumerate(pairs):
                last = nc.tensor.matmul(out_ap, l, r, start=(i == 0), stop=(i == n - 1))
            return last
        return op(PE, f, reads=reads, writes=[bres(bank_i)], extra=extra)

    def norm_x_tile(t, gbuf):
        tt = t % 4
        norm_tile(XS[tt][:], r_xs[tt], t, 0, HNT[gbuf][:, :, tt * 128:(tt + 1) * 128], r_hnt[gbuf], t % 2, rres=r_rstd_g[t // 4])

    CWB = sb("cwb", [124, 128], BF16, OV + 165120)
    CWF2 = sb("cwf2", [124, 128], F32, OV + 165376)
    r_cw = Res()
    cw_v = conv_w_h.ap().rearrange("j (c p) -> (j c) p", p=128)
    dma(SP, d_cw, CWF2[:], cw_v, writes=[r_cw])

    op(DVE, lambda: nc.vector.tensor_scalar(out=CWB[:], in0=CWF2[:], scalar1=0.5, scalar2=None, op0=ALU.mult), reads=[r_cw], writes=[r_cw])

    def conv_weight_prep():
        b = bank_get()
        op(PE, lambda: nc.tensor.transpose(bb(b)[:, 0:124], CWB[:], ident[0:124, 0:124]), reads=[r_cw, r_const], writes=[bres(b)])
        op(ACT, lambda: nc.scalar.activation(out=wT[:], in_=bb(b)[:, 0:124], func=AF.Copy), reads=[bres(b)], writes=[r_cw])
        bank_put(b)

    def build_rope_tables():
        op(DVE, lambda: nc.vector.tensor_copy(out=posf[:], in_=posi[:]), reads=[r_small], writes=[r_rope])
        posb = bass.AP(posf, 0, [[16, 128], [1, 16], [0, 8]])
        invb = bass.AP(invf, 0, [[8, 128], [0, 16], [1, 8]])
        op(DVE, lambda: nc.vector.tensor_tensor(out=ANG[:], in0=posb, in1=invb, op=ALU.mult), reads=[r_misc], writes=[r_rope])
        angf = ANG[:].rearrange("p a b -> p (a b)")
        op(DVE, lambda: nc.vector.tensor_copy(out=ANG2[:, 0:128], in_=angf), writes=[r_rope])
        op(DVE, lambda: nc.vector.tensor_scalar(out=ANG2[:, 128:256], in0=angf, scalar1=0.5 * math.pi, scalar2=None, op0=ALU.add), writes=[r_rope])
        op(DVE, lambda: nc.vector.tensor_scalar(out=QQ[:], in0=ANG2[:], scalar1=1.0 / TWO_PI, scalar2=None, op0=ALU.mult), writes=[r_rope])
        op(DVE, lambda: nc.vector.tensor_copy(out=KI[:], in_=QQ[:]), writes=[r_rope])
        op(DVE, lambda: nc.vector.tensor_copy(out=KF[:], in_=KI[:]), writes=[r_rope])
        op(DVE, lambda: nc.vector.scalar_tensor_tensor(out=QQ[:], in0=KF[:], scalar=-TWO_PI, in1=ANG2[:], op0=ALU.mult, op1=ALU.add), writes=[r_rope])
        op(DVE, lambda: nc.vector.tensor_scalar(out=MM[:], in0=QQ[:], scalar1=math.pi, scalar2=None, op0=ALU.is_gt), writes=[r_rope])
        op(DVE, lambda: nc.vector.scalar_tensor_tensor(out=ANG2[:], in0=MM[:], scalar=-TWO_PI, in1=QQ[:], op0=ALU.mult, op1=ALU.add), writes=[r_rope])
        op(DVE, lambda: nc.vector.tensor_scalar(out=ANG2[:], in0=ANG2[:], scalar1=-math.pi, scalar2=math.pi, op0=ALU.max, op1=ALU.min), writes=[r_rope])
        op(ACT, lambda: nc.scalar.activation(out=SIN[:].rearrange("p a b -> p (a b)"), in_=ANG2[:, 0:128], func=AF.Sin), reads=[r_rope], writes=[r_rope])
        op(ACT, lambda: nc.scalar.activation(out=COS[:].rearrange("p a b -> p (a b)"), in_=ANG2[:, 128:256], func=AF.Sin), writes=[r_rope])
        op(DVE, lambda: nc.vector.tensor_scalar(out=cvh[:], in0=cvec[:, 4:12], scalar1=0.5, scalar2=None, op0=ALU.mult), reads=[r_small], writes=[r_misc])


    pe_win_done = [None]
    pass1(0)
    for t in range(4):
        norm_x_tile(t, 0)
    build_rope_tables()
    load_gain(1, 1)

    for g in range(4):
        gb = g % 2
        hn = HNT[gb]
        if g < 3:
            pass1(g + 1)
        if g == 2:
            conv_weight_prep()
        for c in range(4):
            ba = bank_get()
            bg = bank_get()
            ia, ig = c // 2, 2 + c // 2
            mm_group(bf(ba)[:, :], [(WIN[:, kc, c * 128:(c + 1) * 128], hn[:, kc, :]) for kc in range(8)],
                     [r_hnt[gb], r_win[ia]], ba)
            mm_group(bf(bg)[:, :], [(WIN[:, kc, 512 + c * 128:512 + (c + 1) * 128], hn[:, kc, :]) for kc in range(8)],
                     [r_hnt[gb], r_win[ig]], bg)
            s = c % 2
            op(ACT, lambda: nc.scalar.activation(out=THA[s][:], in_=bf(bg)[:, :], func=AF.Tanh, scale=0.5),
               reads=[bres(bg)], writes=[r_tha[s]])
            op(DVE, lambda: nc.vector.scalar_tensor_tensor(out=UT[:, c, 30 + g * 512:30 + (g + 1) * 512], in0=THA[s][:], scalar=1.0,
                                                          in1=bf(ba)[:, :], op0=ALU.add, op1=ALU.mult),
               reads=[r_tha[s], bres(ba)], writes=[])
            bank_put(ba)
            bank_put(bg)
        for tt in range(4):
            t = 4 * g + tt
            bq, bk, bv = bank_get(), bank_get(), bank_get()
            lh = [hn[:, kc, tt * 128:(tt + 1) * 128] for kc in range(8)]
            mm_group(bf(bq)[:, :], [(lh[kc], WIN[:, kc, 1024:1536]) for kc in range(8)], [r_hnt[gb], r_win[4], r_win[5]], bq)
            mm_group(bf(bk)[:, :], [(lh[kc], WIN[:, kc, 1536:2048]) for kc in range(8)], [r_hnt[gb], r_win[6], r_win[7]], bk)
            pe_win_done[0] = mm_group(bf(bv)[:, :], [(lh[kc], WIN[:, kc, 2048:2560]) for kc in range(8)], [r_hnt[gb], r_win[8], r_win[9]], bv)
            vout = bass.AP(VS, t * 768, [[16 * 768, 128], [192, 4], [128, 2], [1, 64]])
            vin = bf(bv)[:, :].rearrange("p (a b c) -> p a b c", a=4, b=2)
            op(ACT, lambda: nc.scalar.activation(out=vout, in_=vin, func=AF.Copy), reads=[bres(bv)], writes=[])
            if tt == 0:
                PE_prev = r_qtok.r.get("pe"), r_ktok.r.get("pe")
                ACT.wait(*[x for x in PE_prev if x])
                DVE.wait(*[x for x in PE_prev if x])
            tq = rope(bq, QTOK, r_qtok, t, tt)
            tk = rope(bk, KTOK, r_ktok, t, tt)
            bank_put(bq)
            bank_put(bk)
            bank_put(bv)
            if g < 3:
                norm_x_tile(4 * (g + 1) + tt, (g + 1) % 2)
        r_qtok.w = DVE.last()
        r_ktok.w = DVE.last()
        act_last = ACT.last()
        for (src, srcres, dstT, dstres) in ((QTOK, r_qtok, QT, r_qt), (KTOK, r_ktok, KT, r_kt)):
            for hp in range(4):
                b = bank_get()

                def tr(src=src, hp=hp, b=b):
                    last = None
                    for tt in range(4):
                        last = nc.tensor.transpose(bb(b)[:, tt * 128:(tt + 1) * 128], src[:, tt, hp * 128:(hp + 1) * 128], ident[:])
                    return last
                op(PE, tr, reads=[srcres, r_const], writes=[bres(b)], extra=[act_last])
                op(ACT, lambda dstT=dstT, hp=hp, b=b: nc.scalar.activation(out=dstT[:, hp, g * 512:(g + 1) * 512], in_=bb(b)[:, 0:512], func=AF.Copy),
                   reads=[bres(b)], writes=[])
                bank_put(b)
    a1_act, a1_dve, a1_pe = ACT.last(), DVE.last(), PE.last()
    for e in (ACT, DVE, POOL, SP):
        e.wait(pe_win_done[0], a1_act, a1_dve)
    PE.wait(a1_act, a1_dve)
    for r_ in r_qt + r_kt + [r_vs, r_ut]:
        r_.w = None
        r_.r = {}

    if debug:
        d_dbg = DSem(nc, "ddbg")

        def dump(name, ap, shape, dt):
            h = nc.dram_tensor("dbg_" + name, list(shape), dt, kind="ExternalOutput")
            dbg[name] = h
            ts = barrier(also=[SP])
            dma(SP, d_dbg, h.ap(), ap)
            SP.wait((d_dbg.sem, d_dbg.cnt, d_dbg.key))
    else:
        def dump(*a, **k):
            pass

    dump("qt", QT[:].rearrange("p a b -> p (a b)"), [128, 4 * T], BF16)
    dump("kt", KT[:].rearrange("p a b -> p (a b)"), [128, 4 * T], BF16)
    dump("vs", VS[:].rearrange("p a b c -> p (a b c)"), [128, 16 * 768], BF16)
    dump("ut", UT[:].rearrange("p a b -> p (a b)"), [128, 4 * 2080], BF16)

    DIAG = sb("diag", [128, 4, 31, 128], BF16, OV + 73984)
    CF = [sb("cf%d" % i, [128, 4, 512], F32, OV + 105728 + 8192 * i) for i in range(2)]
    CB = sb("cb", [128, 4, 512], BF16, OV + 122112)
    CB2 = sb("cb2", [128, 4, 512], BF16, OV + 126208)
    TT_ = [sb("ttmp%d" % i, [128, 512], F32, OV + 130304 + 2048 * i) for i in range(2)]
    YH = [sb("yh%d" % i, [128, 512], F32, OV + 134400 + 2048 * i) for i in range(2)]
    THC = [sb("thc%d" % i, [128, 512], F32, OV + 138496 + 2048 * i) for i in range(2)]
    MSB = sb("msb", [128, 512], F32, OV + 142592)
    MEANS = [sb("means%d" % i, [128, 512], F32, OV + 144640 + 2048 * i) for i in range(2)]
    RS = [sb("rs%d" % i, [128, 512], F32, OV + 148736 + 2048 * i) for i in range(2)]
    r_diag = [Res() for _ in range(4)]
    r_cf = [[Res() for _ in range(4)] for _ in range(2)]
    r_cb = [Res() for _ in range(4)]
    r_cb2 = [Res() for _ in range(4)]
    r_tt = [Res(), Res()]
    r_yh = [Res(), Res()]
    r_thc = [Res(), Res()]
    r_msb = Res()
    r_means = [Res(), Res()]
    r_rs = [Res(), Res()]
    for c in range(4):
        idb = bass.AP(ident, 0, [[128, 128], [0, 31], [1, 128]])
        wb_ = bass.AP(wT, c, [[124, 128], [4, 31], [0, 128]])
        op(DVE, lambda c=c, idb=idb, wb_=wb_: nc.vector.tensor_tensor(out=DIAG[:, c, :, :], in0=idb, in1=wb_, op=ALU.mult),
           reads=[r_cw, r_const], writes=[r_diag[c]])

    def conv_mm(g, c):
        gs = g % 2
        b = bank_get()
        mm_group(bf(b)[:, :], [(DIAG[:, c, j, :], UT[:, c, g * 512 + j:g * 512 + j + 512]) for j in range(31)],
                 [r_diag[c], r_ut], b)
        cfs = CF[gs][:, c, :]
        op(ACT, lambda: nc.scalar.activation(out=cfs, in_=bf(b)[:, :], func=AF.Identity, bias=cvec[:, c:c + 1]),
           reads=[bres(b), r_small], writes=[r_cf[gs][c]])
        bank_put(b)
        op(DVE, lambda: nc.vector.tensor_copy(out=CB[:, c, :], in_=cfs), reads=[r_cf[gs][c]], writes=[r_cb[c]])
        op(ACT, lambda: nc.scalar.activation(out=CB2[:, c, :], in_=cfs, func=AF.Square), reads=[r_cf[gs][c]], writes=[r_cb2[c]])

    def conv_stats(g, fixed_banks=None, defer_finish=False):
        gs = g % 2
        bm, bs = fixed_banks if fixed_banks is not None else (bank_get(), bank_get())
        mm_group(bf(bm)[:, :], [(ones512[:], CB[:, c, :]) for c in range(4)], r_cb + [r_misc], bm)
        mm_group(bf(bs)[:, :], [(ones512[:], CB2[:, c, :]) for c in range(4)], r_cb2, bs)
        op(ACT, lambda: nc.scalar.activation(out=MEANS[gs][:], in_=bf(bm)[:, :], func=AF.Copy), reads=[bres(bm)], writes=[r_means[gs]])
        op(DVE, lambda: nc.vector.tensor_tensor(out=MSB[:], in0=MEANS[gs][:], in1=MEANS[gs][:], op=ALU.mult), reads=[r_means[gs]], writes=[r_msb])
        op(DVE, lambda: nc.vector.tensor_tensor(out=RS[gs][:], in0=bf(bs)[:, :], in1=MSB[:], op=ALU.subtract),
           reads=[bres(bs), r_msb], writes=[r_rs[gs]])
        if fixed_banks is None:
            bank_put(bm)
            bank_put(bs)
        def finish():
            op(ACT, lambda: nc.scalar.activation(out=RS[gs][:], in_=RS[gs][:], func=AF.Sqrt, bias=epst[:, 0:1]), reads=[r_misc], writes=[r_rs[gs]])
            op(DVE, lambda: nc.vector.reciprocal(out=RS[gs][:], in_=RS[gs][:]), writes=[r_rs[gs]])
        if defer_finish:
            return finish
        finish()

    def conv_p2a(g, c):
        gs = g % 2
        s_ = c % 2
        op(DVE, lambda: nc.vector.tensor_tensor(out=TT_[s_][:], in0=CF[gs][:, c, :], in1=MEANS[gs][:], op=ALU.subtract),
           reads=[r_cf[gs][c], r_means[gs]], writes=[r_tt[s_]])
        op(DVE, lambda: nc.vector.tensor_tensor(out=TT_[s_][:], in0=TT_[s_][:], in1=RS[gs][:], op=ALU.mult),
           reads=[r_rs[gs]], writes=[r_tt[s_]])

    def conv_p2b(g, c):
        s_ = c % 2
        sl = slice(g * 512, (g + 1) * 512)
        op(ACT, lambda: nc.scalar.activation(out=THC[s_][:], in_=TT_[s_][:], func=AF.Tanh, scale=cvh[:, c:c + 1], bias=cvh[:, 4 + c:5 + c]),
           reads=[r_tt[s_], r_misc], writes=[r_thc[s_]])
        op(DVE, lambda: nc.vector.tensor_scalar(out=YH[s_][:], in0=TT_[s_][:], scalar1=cvh[:, c:c + 1], scalar2=cvh[:, 4 + c:5 + c],
                                               op0=ALU.mult, op1=ALU.add),
           reads=[r_tt[s_]], writes=[r_yh[s_]])
        op(DVE, lambda: nc.vector.scalar_tensor_tensor(out=MIX[:, c, sl], in0=THC[s_][:], scalar=1.0, in1=YH[s_][:], op0=ALU.add, op1=ALU.mult),
           reads=[r_thc[s_], r_yh[s_]], writes=[])

    def conv_p2(g, c):
        gs = g % 2
        s_ = c % 2
        sl = slice(g * 512, (g + 1) * 512)
        op(DVE, lambda: nc.vector.tensor_tensor(out=TT_[s_][:], in0=CF[gs][:, c, :], in1=MEANS[gs][:], op=ALU.subtract),
           reads=[r_cf[gs][c], r_means[gs]], writes=[r_tt[s_]])
        op(DVE, lambda: nc.vector.tensor_tensor(out=TT_[s_][:], in0=TT_[s_][:], in1=RS[gs][:], op=ALU.mult),
           reads=[r_rs[gs]], writes=[r_tt[s_]])
        op(ACT, lambda: nc.scalar.activation(out=THC[s_][:], in_=TT_[s_][:], func=AF.Tanh, scale=cvh[:, c:c + 1], bias=cvh[:, 4 + c:5 + c]),
           reads=[r_tt[s_], r_misc], writes=[r_thc[s_]])
        op(DVE, lambda: nc.vector.tensor_scalar(out=YH[s_][:], in0=TT_[s_][:], scalar1=cvh[:, c:c + 1], scalar2=cvh[:, 4 + c:5 + c],
                                               op0=ALU.mult, op1=ALU.add),
           reads=[r_tt[s_]], writes=[r_yh[s_]])
        op(DVE, lambda: nc.vector.scalar_tensor_tensor(out=MIX[:, c, sl], in0=THC[s_][:], scalar=1.0, in1=YH[s_][:], op0=ALU.add, op1=ALU.mult),
           reads=[r_thc[s_], r_yh[s_]], writes=[])

    GATE = sb("gate", [128, 8, 64], F32, OV + 153600)
    CMP = sb("cmp", [128, 8, 8, 8], F32, OV + 155648)
    RANK = sb("rank", [128, 64], F32, OV + 157696)
    MBT = sb("mbt", [128, 8, 64], BF16, OV + 157952)
    MASKT2 = sb("maskt2", [64, 1024], BF16, OV + 163072)
    r_maskt = Res()
    r_gate = Res()
    r_cmp = Res()
    r_rank = Res()
    r_mbt = Res()
    r_km = Res()
    def gate_stage_a():
        op(DVE, lambda: nc.vector.memset(KMD[:], 0.0), writes=[r_km])
        op(DVE, lambda: nc.vector.memset(MBT[:], 0.0), writes=[r_mbt])
        for hp in range(4):
            op(DVE, lambda hp=hp: nc.vector.tensor_reduce(out=KM[:, hp, :], in_=KT[:, hp, :].rearrange("p (n k) -> p n k", n=8), axis=AX.X, op=ALU.add),
               writes=[r_km])
            op(DVE, lambda hp=hp: nc.vector.tensor_scalar(out=KMD[0:64, hp, 0:8], in0=KM[0:64, hp, :], scalar1=1.0 / 256.0, scalar2=None, op0=ALU.mult), writes=[r_km])
            op(DVE, lambda hp=hp: nc.vector.tensor_scalar(out=KMD[64:128, hp, 8:16], in0=KM[64:128, hp, :], scalar1=1.0 / 256.0, scalar2=None, op0=ALU.mult), writes=[r_km])

    def gate_stage_b():
        for ti in range(8):
            t = 8 + ti
            own = t // 2
            b = bank_get()

            def gm(t=t, b=b):
                last = None
                for hp in range(4):
                    last = nc.tensor.matmul(bf(b)[:, hp * 16:(hp + 1) * 16], QT[:, hp, t * 128:(t + 1) * 128], KMD[:, hp, :], start=True, stop=True)
                return last
            op(PE, gm, reads=[r_km], writes=[bres(b)])
            op(ACT, lambda ti=ti, b=b: nc.scalar.activation(out=GATE[:, ti, :], in_=bf(b)[:, 0:64], func=AF.Copy), reads=[bres(b)], writes=[r_gate])
            bank_put(b)
            gbase = ti * 64
            in0 = bass.AP(GATE, gbase, [[512, 128], [8, 8], [0, own], [1, own]])
            in1 = bass.AP(GATE, gbase, [[512, 128], [8, 8], [1, own], [0, own]])
            cm = bass.AP(CMP, 0, [[512, 128], [64, 8], [8, own], [1, own]])
            op(DVE, lambda in0=in0, in1=in1, cm=cm: nc.vector.tensor_tensor(out=cm, in0=in0, in1=in1, op=ALU.is_gt), reads=[r_gate], writes=[r_cmp])
            rk = bass.AP(RANK, 0, [[64, 128], [8, 8], [1, own]])
            op(DVE, lambda cm=cm, rk=rk: nc.vector.tensor_reduce(out=rk, in_=cm, axis=AX.X, op=ALU.add), reads=[r_cmp], writes=[r_rank])
            mb = bass.AP(MBT, ti * 64, [[512, 128], [8, 8], [1, own]])
            op(DVE, lambda rk=rk, mb=mb: nc.vector.tensor_scalar(out=mb, in0=rk, scalar1=2.5, scalar2=NEG, op0=ALU.is_gt, op1=ALU.mult), reads=[r_rank], writes=[r_mbt])

    BLKR = sb("blkr", [8, 8, 256], BF16, OV + 158976)
    r_blkr = Res()

    def gate_stage_c():
        op(POOL, lambda: nc.gpsimd.memset(BLKR[:], 1.0), writes=[r_blkr])
        op(POOL, lambda: nc.gpsimd.affine_select(out=BLKR[:], in_=BLKR[:], pattern=[[-1, 8], [0, 256]],
                                                 compare_op=ALU.is_equal, fill=0.0, base=0, channel_multiplier=1), writes=[r_blkr])
        for grp in range(2):
            b = bank_get()

            def tr(grp=grp, b=b):
                last = None
                for i in range(4):
                    ti = grp * 4 + i
                    last = nc.tensor.transpose(bb(b)[0:64, i * 128:(i + 1) * 128], MBT[:, ti, :], ident[:])
                return last
            op(PE, tr, reads=[r_mbt, r_const], writes=[bres(b)])
            op(ACT, lambda grp=grp, b=b: nc.scalar.activation(out=MASKT2[0:64, grp * 512:(grp + 1) * 512], in_=bb(b)[0:64, 0:512], func=AF.Copy),
               reads=[bres(b)], writes=[r_maskt])
            bank_put(b)

    KTZP = sb("ktzp", [128, T], BF16, MIXO + 6 * 4096)
    QTZP = sb("qtzp", [128, T], BF16, MIXO + 7 * 4096)
    r_ktzp, r_qtzp, r_qmp, r_kbp = Res(), Res(), Res(), Res()
    d_kbp, d_qmp = DSem(nc, "dkbp"), DSem(nc, "dqmp")

    def prebuild_head0():
        op(DVE, lambda: nc.vector.memset(KTZP[:], 0.0), writes=[r_ktzp])
        op(DVE, lambda: nc.vector.memset(QTZP[:], 0.0), writes=[r_qtzp])
        op(DVE, lambda: nc.vector.tensor_copy(out=KTZP[0:64, :], in_=KT[0:64, 0, :]), writes=[r_ktzp])
        op(DVE, lambda: nc.vector.tensor_copy(out=QTZP[0:64, :], in_=QT[0:64, 0, :]), writes=[r_qtzp])
        blkr_f0 = BLKR[:].rearrange("p a b -> p (a b)")
        SP.wait(r_ktzp.w)
        r_kbp.w = dma(SP, d_kbp, KTZP[64:72, :], blkr_f0, reads=[r_blkr], nowait_w=True)
        SP.wait(r_qtzp.w)
        dma(SP, d_qmp, QTZP[64:72, 1024:2048], MASKT2[0:8, :], reads=[r_maskt], writes=[r_qmp])

    for g in range(4):
        if g == 3:
            prebuild_head0()
        fin = conv_stats(g - 1, defer_finish=True) if g >= 1 else None
        conv_mm(g, 0)
        conv_mm(g, 1)
        if g >= 1:
            fin()
            conv_p2a(g - 1, 0)
            conv_p2a(g - 1, 1)
        conv_mm(g, 2)
        if g >= 1:
            conv_p2b(g - 1, 0)
            conv_p2a(g - 1, 2)
        conv_mm(g, 3)
        if g >= 1:
            conv_p2b(g - 1, 1)
            conv_p2a(g - 1, 3)
            conv_p2b(g - 1, 2)
            conv_p2b(g - 1, 3)
        if g == 0:
            gate_stage_a()
        elif g == 1:
            gate_stage_b()
        elif g == 2:
            gate_stage_c()
    conv_last_pe = PE.last()
    conv_tail_pieces = []
    conv_p2_last = [None]

    PT2 = [sb("pt%d" % i, [128, 1024], BF16, OV + 57344 + 2048 * i) for i in range(4)]
    NSB = [sb("nsb%d" % i, [128, 512], F32, OV + 65536 + 2048 * i) for i in range(3)]
    DSB = [sb("dsb%d" % i, [128, 512], F32, OV + 71680 + 2048 * i) for i in range(3)]
    r_nsb = [Res() for _ in range(3)]
    r_dsb = [Res() for _ in range(3)]
    KTZ = [sb("ktz%d" % i, [128, T], BF16, OV + 77824 + 4096 * i) for i in range(2)]
    QTZ = [sb("qtz%d" % i, [128, T], BF16, OV + 86016 + 4096 * i) for i in range(2)]
    WOUT = sb("wout", [128, 8, D], BF16, OV + 146688)
    r_pt = [Res() for _ in range(4)]
    r_ktz = [Res(), Res()]
    r_qtz = [Res(), Res()]
    r_qm = [Res(), Res()]
    d_qm = [DSem(nc, "dqm0"), DSem(nc, "dqm1")]
    r_att = Res()
    r_wout = Res()
    d_wout = DSem(nc, "dwout")
    for e in (ACT, DVE, POOL, SP):
        e.wait(conv_last_pe)

    op(POOL, lambda: nc.gpsimd.memset(KTZ[0][:], 0.0), writes=[r_ktz[0]])
    op(POOL, lambda: nc.gpsimd.memset(QTZ[0][:], 0.0), writes=[r_qtz[0]])

    def slot1_zero_fill():
        op(DVE, lambda: nc.vector.memset(KTZ[1][:], 0.0), writes=[r_ktz[1]])
        op(DVE, lambda: nc.vector.memset(QTZ[1][:], 0.0), writes=[r_qtz[1]])
        r_kb[1].w = dma(SP, d_kb[1], KTZ[1][0:8, :], blkr_f, reads=[r_blkr], writes=[r_ktz[1]])
    blkr_f = BLKR[:].rearrange("p a b -> p (a b)")
    d_kb = [DSem(nc, "dkb0"), DSem(nc, "dkb1")]
    dma(SP, d_kb[0], KTZ[0][64:72, :], blkr_f, reads=[r_blkr], writes=[r_ktz[0]])
    r_kb = [Res(), Res()]
    r_kb[0].w = r_ktz[0].w

    KTZ.append(KTZP)
    QTZ.append(QTZP)
    r_ktz.append(r_ktzp)
    r_qtz.append(r_qtzp)
    r_qm.append(r_qmp)
    r_kb.append(r_kbp)
    heads = [(hp, hh) for hp in range(4) for hh in range(2)]

    def head_setup(hi):
        hp, hh = heads[hi]
        h = 2 * hp + hh
        r0 = 64 * hh
        op(DVE, lambda: nc.vector.tensor_copy(out=KTZ[hh][r0:r0 + 64, :], in_=KT[r0:r0 + 64, hp, :]), writes=[r_ktz[hh]])
        op(DVE, lambda: nc.vector.tensor_copy(out=QTZ[hh][r0:r0 + 64, :], in_=QT[r0:r0 + 64, hp, :]), writes=[r_qtz[hh]])
        mrow = slice(64, 72) if hh == 0 else slice(0, 8)
        Q_ = SP
        Q_.wait(r_qtz[hh].w)
        dma(Q_, d_qm[hh], QTZ[hh][mrow, 1024:2048], MASKT2[8 * h:8 * h + 8, :], reads=[r_maskt], writes=[r_qm[hh]])

    assert sorted(free_banks) == list(range(7))
    units = []
    for hi, (hp, hh) in enumerate(heads):
        for qc in range(4):
            for p_ in range(2 * qc + 2):
                units.append((hi, hp, hh, qc, p_))
    LOOK = 2
    sinfo = {}
    ocount = [0]
    ncount = [0]
    r_attq = [Res() for _ in range(4)]
    ocur = {}
    setup_done = set()

    def issue_s(u):
        hi, hp, hh, qc, p_ = units[u]
        if hi not in setup_done:
            setup_done.add(hi)
            head_setup(hi)
        if qc == 2 and p_ == 0 and hi + 1 < len(heads) and (hi + 1) not in setup_done:
            setup_done.add(hi + 1)
            head_setup(hi + 1)
        sd = u % 3
        sl = 2 if hi == 0 else hh
        qb = qc * 512
        offs = []
        for j in range(2):
            kt = 2 * p_ + j
            offs.append(max(0, kt - 4 * qc) * 128)

        def f():
            last = None
            for j in range(2):
                kt = 2 * p_ + j
                off = offs[j]
                diag = kt >= 4 * qc
                last = nc.tensor.matmul(dbl[sd][:, j * 512 + off:(j + 1) * 512], KTZ[sl][:, kt * 128:(kt + 1) * 128],
                                        QTZ[sl][:, qb + off:qb + 512], start=True, stop=not diag)
                if diag:
                    last = nc.tensor.matmul(dbl[sd][:, j * 512 + off:j * 512 + off + 128], ident[:], tri[:], start=False, stop=True)
            return last
        op(PE, f, reads=[r_ktz[sl], r_qtz[sl], r_qm[sl], r_kb[sl], r_tri, r_const], writes=[bres(2 * sd), bres(2 * sd + 1)])
        sinfo[u] = (sd, offs)

    def issue_rest(u):
        hi, hp, hh, qc, p_ = units[u]
        sd, offs = sinfo.pop(u)
        slot = u % 4
        qb = qc * 512
        nkt = 4 * qc + 4
        o0 = offs[0]
        op(ACT, lambda: nc.scalar.activation(out=PT2[slot][:, o0:1024], in_=dbl[sd][:, o0:1024], func=AF.Exp, scale=0.125),
           reads=[bres(2 * sd), bres(2 * sd + 1)], writes=[r_pt[slot]])
        if p_ == 0:
            ocur[(hi, qc)] = 6 + (ocount[0] % 2)
            ocount[0] += 1
        ob = ocur[(hi, qc)]

        def pv():
            last = None
            for j in range(2):
                kt = 2 * p_ + j
                off = offs[j]
                last = nc.tensor.matmul(bf(ob)[:, off:512], VS[:, kt, hp, 64 * hh:64 * hh + 128], PT2[slot][:, j * 512 + off:(j + 1) * 512],
                                        start=(kt == 0), stop=(kt == nkt - 1))
            return last
        op(PE, pv, reads=[r_pt[slot], r_vs], writes=[bres(ob)])
        if p_ == 2 * qc + 1:
            s3 = ncount[0] % 3
            ncount[0] += 1
            num = slice(0, 64) if hh == 0 else slice(64, 128)
            den = slice(64, 128) if hh == 0 else slice(0, 64)
            op(DVE, lambda: nc.vector.tensor_copy(out=NSB[s3][:, :], in_=bf(ob)[:, :]), reads=[bres(ob)], writes=[r_nsb[s3]])
            op(DVE, lambda: nc.vector.reciprocal(out=DSB[s3][num, :], in_=NSB[s3][den, :]), reads=[r_nsb[s3]], writes=[r_dsb[s3]])
            r_attq[qc].w = op(POOL, lambda: nc.gpsimd.tensor_tensor(out=MIX[num, 4 + hp, qb:qb + 512], in0=NSB[s3][num, :], in1=DSB[s3][num, :], op=ALU.mult),
                              reads=[r_nsb[s3], r_dsb[s3]], writes=[])
            del ocur[(hi, qc)]

    setup_done.add(0)
    stats3_finish = conv_stats(3, fixed_banks=(7, 6), defer_finish=True)
    slot1_zero_fill()
    conv_tail_pieces.extend([
        stats3_finish,
        lambda: conv_p2a(3, 0),
        lambda: (conv_p2b(3, 0), conv_p2a(3, 1)),
        lambda: (conv_p2b(3, 1), conv_p2a(3, 2)),
        lambda: (conv_p2b(3, 2), conv_p2a(3, 3)),
        lambda: conv_p2b(3, 3),
    ])
    for u in range(min(LOOK, len(units))):
        issue_s(u)
    for u in range(len(units)):
        if u + LOOK < len(units):
            issue_s(u + LOOK)
        issue_rest(u)
        if u % 2 == 1 and conv_tail_pieces:
            conv_tail_pieces.pop(0)()
            conv_p2_last[0] = DVE.last()
            if not conv_tail_pieces:
                w_out_v = w_out_h.ap().rearrange("(kc p) n -> p kc n", p=128)
                dma(POOL, d_wout, WOUT[:], w_out_v, writes=[r_wout], extra=[ACT.last(), DVE.last(), r_kb[0].w, r_kb[1].w, r_kb[2].w])
    att_done = [PE.last(), ACT.last(), DVE.last(), POOL.last()]
    SP.wait(*att_done)
    POOL.wait(*att_done)
    dump("maskt2", MASKT2[:], [64, 1024], BF16)
    dump("qtz0", QTZ[0][:], [128, T], BF16)
    dump("qtz1", QTZ[1][:], [128, T], BF16)
    dump("ktz0", KTZ[0][:], [128, T], BF16)
    dump("attt", MIX[:, 4:8, :].rearrange("p a b -> p (a b)"), [128, 4 * T], BF16)

    H = sb("h", [128, NT, D], F32, OV + 0)
    r_h = [Res() for _ in range(NT)]
    d_h = [DSem(nc, "dh%d" % i) for i in range(NT)]
    NK = [6, 6, 5, 5]
    KOFF = [0, 6, 12, 17]
    WU = [sb("wu0", [128, 8, 2, 768], BF16, OV + 65536), sb("wu1", [128, 8, 2, 768], BF16, OV + 109824)]
    WD = [sb("wd0", [128, 6, D], BF16, OV + 90112), sb("wd1", [128, 6, D], BF16, OV + 134400)]
    THF = [sb("thf%d" % i, [128, 512], F32, OV + 102400 + 2048 * i) for i in range(2)]
    ACTT = [sb("actt%d" % i, [128, 6, 512], BF16, OV + 146688 + 6144 * i) for i in range(2)]
    T1 = [sb("t1_%d" % i, [128, 512], F32, OV + 158976 + 2048 * i) for i in range(2)]
    XNB = sb("xnb", [128, D], BF16, OV + 163072)
    r_wu = [Res(), Res()]
    r_wd = [Res(), Res()]
    d_wu = [DSem(nc, "dwu0"), DSem(nc, "dwu1")]
    d_wd = [DSem(nc, "dwd0"), DSem(nc, "dwd1")]
    r_thf = [Res(), Res()]
    r_actt = [Res(), Res()]
    r_t1 = [Res(), Res()]
    w_up_v = w_up_h.ap().rearrange("(kc p) n -> p kc n", p=128)
    w_down_v = w_down_h.ap().rearrange("(kt p) n -> p kt n", p=128)

    def load_ffn(s, extra=()):
        bufi = (s + 1) % 2
        nk, k0 = NK[s], KOFF[s]
        dma(POOL, d_wu[bufi], WU[bufi][:, :, 0, 0:nk * 128], w_up_v[:, :, k0 * 128:(k0 + nk) * 128], writes=[r_wu[bufi]], extra=extra)
        t1 = dma(POOL, d_wu[bufi], WU[bufi][:, :, 1, 0:nk * 128], w_up_v[:, :, DFF + k0 * 128:DFF + (k0 + nk) * 128], nowait_w=True)
        r_wu[bufi].w = t1
        dma(POOL, d_wd[bufi], WD[bufi][:, 0:nk, :], w_down_v[:, k0:k0 + nk, :], writes=[r_wd[bufi]], extra=extra)

    load_ffn(0)
    load_ffn(1)
    r_hn = [Res() for _ in range(4)]
    r_xnb = [Res(), Res()]

    def sq_tile(t, g):
        op(ACT, lambda: nc.scalar.activation(out=bf(7)[:, :], in_=H[:, t, 0:512], func=AF.Square, accum_out=ssq[:, t:t + 1]),
           reads=[r_h[t]], writes=[r_ssq_g[g]])
        op(ACT, lambda: nc.scalar.activation(out=bf(7)[:, :], in_=H[:, t, 512:1024], func=AF.Square, accum_out=ssq2[:, t:t + 1]),
           reads=[r_h[t]], writes=[r_ssq_g[g]])

    def rstd_group(g):
        cs = slice(4 * g, 4 * g + 4)
        op(DVE, lambda: nc.vector.tensor_tensor(out=ssq[:, cs], in0=ssq[:, cs], in1=ssq2[:, cs], op=ALU.add), writes=[r_ssq_g[g]])
        op(ACT, lambda: nc.scalar.activation(out=std[:, cs], in_=ssq[:, cs], func=AF.Sqrt, scale=1.0 / D, bias=epst[:, 0:1]),
           reads=[r_ssq_g[g], r_misc], writes=[r_rstd_g[g]])
        op(DVE, lambda: nc.vector.reciprocal(out=rstd[:, cs], in_=std[:, cs]), writes=[r_rstd_g[g]])
        op(DVE, lambda: nc.vector.memset(ssq[:, cs], 0.0), reads=[r_rstd_g[g]], writes=[r_ssq_g[g]])
        op(DVE, lambda: nc.vector.memset(ssq2[:, cs], 0.0), writes=[r_ssq_g[g]])

    def norm_h_group(g, gslot, extra=()):
        for tt in range(4):
            t = 4 * g + tt
            b = bank_get()
            for half in range(2):
                cs = slice(half * 512, (half + 1) * 512)
                op(DVE, lambda: nc.vector.scalar_tensor_tensor(out=XNB[:, cs], in0=H[:, t, cs], scalar=rstd[:, t:t + 1], in1=G[gslot][:, cs],
                                                              op0=ALU.mult, op1=ALU.mult),
                   reads=[r_h[t], r_rstd_g[g], r_g[gslot]], writes=[r_xnb[half]])

                def tr(half=half, b=b):
                    last = None
                    for kc in range(4 * half, 4 * half + 4):
                        last = nc.tensor.transpose(bb(b)[:, kc * 128:(kc + 1) * 128], XNB[:, kc * 128:(kc + 1) * 128], ident[:])
                    return last
                op(PE, tr, reads=[r_xnb[half], r_const], writes=[bres(b)] if half == 0 else [])
            bres(b).w = PE.last()
            op(ACT, lambda: nc.scalar.activation(out=MIX[:, :, t * 128:(t + 1) * 128], in_=bb(b)[:, 0:1024].rearrange("p (a b) -> p a b", a=8), func=AF.Copy),
               reads=[bres(b)], writes=[r_hn[g]] if tt == 0 else [], extra=extra)
            bank_put(b)
        r_hn[g].w = ACT.last()

    for t in range(NT):
        dma(SP, d_h[t], H[:, t, :], x_d[t * 128:(t + 1) * 128, :], writes=[r_h[t]])

    def a3_group(g):
        for tt in range(4):
            t = 4 * g + tt
            for half in range(2):
                b = bank_get()
                mm_group(bf(b)[:, :], [(MIX[:, kc, t * 128:(t + 1) * 128], WOUT[:, kc, half * 512:(half + 1) * 512]) for kc in range(8)],
                         [r_wout, r_attq[g]], b, extra=[conv_p2_last[0]])
                hs = H[:, t, half * 512:(half + 1) * 512]
                op(DVE, lambda: nc.vector.tensor_tensor(out=hs, in0=bf(b)[:, :], in1=hs, op=ALU.add), reads=[bres(b)], writes=[r_h[t]])
                bank_put(b)
            sq_tile(t, g)
        rstd_group(g)
        return PE.last()

    XN2 = [sb("xn2_%d" % i, [128, D], BF16, OV + 102400 + 2048 * i) for i in range(3)] + [XNB]
    r_xn2 = [Res() for _ in range(4)]

    def norm2_stt(g):
        for tt in range(4):
            t = 4 * g + tt
            op(DVE, lambda: nc.vector.scalar_tensor_tensor(out=XN2[tt][:], in0=H[:, t, :], scalar=rstd[:, t:t + 1], in1=G[1][:],
                                                          op0=ALU.mult, op1=ALU.mult),
               reads=[r_h[t], r_rstd_g[g], r_g[1]], writes=[r_xn2[tt]])

    def norm2_tr(g, extra=()):
        for tt in range(4):
            t = 4 * g + tt
            b = bank_get()

            def tr(b=b, tt=tt):
                last = None
                for kc in range(8):
                    last = nc.tensor.transpose(bb(b)[:, kc * 128:(kc + 1) * 128], XN2[tt][:, kc * 128:(kc + 1) * 128], ident[:])
                return last
            op(PE, tr, reads=[r_xn2[tt], r_const], writes=[bres(b)])
            op(ACT, lambda: nc.scalar.activation(out=MIX[:, :, t * 128:(t + 1) * 128], in_=bb(b)[:, 0:1024].rearrange("p (a b) -> p a b", a=8), func=AF.Copy),
               reads=[bres(b)], writes=[r_hn[g]] if tt == 0 else [], extra=extra)
            bank_put(b)
        r_hn[g].w = ACT.last()
        return PE.last()

    a3_done = {}
    norm2_pe_last = None
    for g in range(5):
        if g < 4:
            a3_done[g] = a3_group(g)
        if g >= 1:
            norm2_pe_last = norm2_tr(g - 1, extra=[a3_done[g - 1]])
        if g < 4:
            norm2_stt(g)
    ACT.wait(norm2_pe_last)
    load_gain(2, 0)
    load_gain(3, 1)
    dump("h1", H[:, 0:2, :].rearrange("p a b -> p (a b)"), [128, 2 * D], F32)

    fsteps = [(s, tg) for s in range(4) for tg in range(4)]

    def ffn_up(i):
        s, tg = fsteps[i]
        bufi = (s + 1) % 2
        ab = i % 2
        nk = NK[s]
        for kt in range(nk):
            bg_, bu_ = bank_get(), bank_get()
            rhs = [MIX[:, kc, tg * 512:(tg + 1) * 512] for kc in range(8)]
            mm_group(bf(bg_)[:, :], [(WU[bufi][:, kc, 0, kt * 128:(kt + 1) * 128], rhs[kc]) for kc in range(8)], [r_hn[tg], r_wu[bufi]], bg_)
            mm_group(bf(bu_)[:, :], [(WU[bufi][:, kc, 1, kt * 128:(kt + 1) * 128], rhs[kc]) for kc in range(8)], [r_hn[tg], r_wu[bufi]], bu_)
            sl = kt % 2
            op(ACT, lambda: nc.scalar.activation(out=THF[sl][:], in_=bf(bg_)[:, :], func=AF.Tanh, scale=0.5), reads=[bres(bg_)], writes=[r_thf[sl]])
            op(DVE, lambda: nc.vector.scalar_tensor_tensor(out=T1[sl][:], in0=THF[sl][:], scalar=1.0, in1=bf(bg_)[:, :], op0=ALU.add, op1=ALU.mult),
               reads=[r_thf[sl], bres(bg_)], writes=[r_t1[sl]])
            op(DVE, lambda: nc.vector.scalar_tensor_tensor(out=ACTT[ab][:, kt, :], in0=T1[sl][:], scalar=0.5, in1=bf(bu_)[:, :], op0=ALU.mult, op1=ALU.mult),
               reads=[r_t1[sl], bres(bu_)], writes=[r_actt[ab]] if kt == 0 else [])
            bank_put(bg_)
            bank_put(bu_)
        r_actt[ab].w = DVE.last()

    def ffn_down(i):
        s, tg = fsteps[i]
        bufi = (s + 1) % 2
        ab = i % 2
        nk = NK[s]
        for tt in range(4):
            t = 4 * tg + tt
            for half in range(2):
                b = bank_get()
                mm_group(bf(b)[:, :], [(ACTT[ab][:, kt, tt * 128:(tt + 1) * 128], WD[bufi][:, kt, half * 512:(half + 1) * 512]) for kt in range(nk)],
                         [r_actt[ab], r_wd[bufi]], b)
                hs = H[:, t, half * 512:(half + 1) * 512]
                op(DVE, lambda: nc.vector.tensor_tensor(out=hs, in0=bf(b)[:, :], in1=hs, op=ALU.add), reads=[bres(b)], writes=[r_h[t]])
                bank_put(b)
            if s == 3:
                sq_tile(t, tg)
        if tg == 3 and s + 2 < 4:
            load_ffn(s + 2)

    WG = sb("wg", [128, 8, D], BF16, OV + 109824)
    WP = sb("wp", [128, 2, D], BF16, OV + 109824 + 16384)
    PTT = sb("ptt", [128, 2, T], BF16, OV + 130304)
    PBALL = sb("pball", [128, NT, 256], BF16, OV + 138496)
    r_pball = Res()
    d_pball = DSem(nc, "dpball")
    r_wg = Res()
    d_wg = DSem(nc, "dwg")
    r_ptt = Res()
    d_out = [DSem(nc, "dout%d" % i) for i in range(4)]
    outs = []

    def ple_prefetch():
        w_gate_v = w_gate_h.ap().rearrange("(kc p) n -> p kc n", p=128)
        w_ple_v = w_ple_h.ap().rearrange("(kc p) n -> p kc n", p=128)
        pe_t = PE.last()
        dma(POOL, d_wg, WG[:], w_gate_v, extra=[pe_t])
        tgp = dma(POOL, d_wg, WP[:], w_ple_v)
        r_wg.w = tgp
        dma(POOL, d_pball, PBALL[:], p_d.rearrange("(t p) f -> p t f", p=128), writes=[r_pball])
        ACT.wait(pe_t)

    def p_transposes():
        for t in range(NT):
            b = bank_get()

            def tr(b=b, t=t):
                nc.tensor.transpose(bb(b)[:, 0:128], PBALL[:, t, 0:128], ident[:])
                return nc.tensor.transpose(bb(b)[:, 128:256], PBALL[:, t, 128:256], ident[:])
            op(PE, tr, reads=[r_pball, r_const], writes=[bres(b)])
            op(ACT, lambda: nc.scalar.activation(out=PTT[:, :, t * 128:(t + 1) * 128], in_=bb(b)[:, 0:256].rearrange("p (a b) -> p a b", a=2), func=AF.Copy),
               reads=[bres(b)], writes=[])
            bank_put(b)
        r_ptt.w = ACT.last()

    def ple_group(g):
        for tt in range(4):
            t = 4 * g + tt
            for half in range(2):
                bga, bpl = bank_get(), bank_get()
                mm_group(bf(bga)[:, :], [(MIX[:, kc, t * 128:(t + 1) * 128], WG[:, kc, half * 512:(half + 1) * 512]) for kc in range(8)], [r_hn[g], r_wg], bga)
                mm_group(bf(bpl)[:, :], [(PTT[:, kc, t * 128:(t + 1) * 128], WP[:, kc, half * 512:(half + 1) * 512]) for kc in range(2)], [r_ptt, r_wg], bpl)
                sl = (2 * t + half) % 2
                op(ACT, lambda: nc.scalar.activation(out=THF[sl][:], in_=bf(bga)[:, :], func=AF.Tanh, scale=0.5), reads=[bres(bga)], writes=[r_thf[sl]])
                op(DVE, lambda: nc.vector.scalar_tensor_tensor(out=T1[sl][:], in0=THF[sl][:], scalar=1.0, in1=bf(bpl)[:, :], op0=ALU.add, op1=ALU.mult),
                   reads=[r_thf[sl], bres(bpl)], writes=[r_t1[sl]])
                hs = H[:, t, half * 512:(half + 1) * 512]
                op(DVE, lambda: nc.vector.scalar_tensor_tensor(out=hs, in0=T1[sl][:], scalar=0.5, in1=hs, op0=ALU.mult, op1=ALU.add),
                   reads=[r_t1[sl]], writes=[r_h[t]])
                bank_put(bga)
                bank_put(bpl)
            sq_tile(t, g)
        rstd_group(g)

    XNS = sb("xns", [128, 4, D], BF16, OV + 138496)
    r_xns = [Res() for _ in range(4)]

    def norm3_stt(g, extra=()):
        for tt in range(4):
            t = 4 * g + tt
            op(DVE, lambda: nc.vector.scalar_tensor_tensor(out=XNS[:, tt, :], in0=H[:, t, :], scalar=rstd[:, t:t + 1], in1=G[0][:],
                                                          op0=ALU.mult, op1=ALU.mult),
               reads=[r_h[t], r_rstd_g[g], r_g[0]], writes=[r_xns[tt]], extra=extra)

    def norm3_tr(g):
        for tt in range(4):
            t = 4 * g + tt
            b = bank_get()

            def tr(b=b, tt=tt):
                last = None
                for kc in range(8):
                    last = nc.tensor.transpose(bb(b)[:, kc * 128:(kc + 1) * 128], XNS[:, tt, kc * 128:(kc + 1) * 128], ident[:])
                return last
            op(PE, tr, reads=[r_xns[tt], r_const], writes=[bres(b)])
            op(ACT, lambda: nc.scalar.activation(out=MIX[:, :, t * 128:(t + 1) * 128], in_=bb(b)[:, 0:1024].rearrange("p (a b) -> p a b", a=8), func=AF.Copy),
               reads=[bres(b)], writes=[r_hn[g]] if tt == 0 else [])
            bank_put(b)
        r_hn[g].w = ACT.last()

    def final_group(g):
        for tt in range(4):
            t = 4 * g + tt
            op(DVE, lambda: nc.vector.scalar_tensor_tensor(out=H[:, t, :], in0=H[:, t, :], scalar=rstd[:, t:t + 1], in1=G[1][:], op0=ALU.mult, op1=ALU.mult),
               reads=[r_rstd_g[g], r_g[1]], writes=[r_h[t]])
            outs.append(dma(SP, d_out[t % 4], out_d[t * 128:(t + 1) * 128, :], H[:, t, :], reads=[r_h[t]]))

    ffn_up(0)
    for i in range(len(fsteps)):
        if i + 1 < len(fsteps):
            ffn_up(i + 1)
        ffn_down(i)
        s, tg = fsteps[i]
        if (s, tg) == (2, 3):
            ple_prefetch()
        if s == 3:
            rstd_group(tg)
            if tg == 0:
                p_transposes()
                norm3_stt(tg, extra=[PE.last()])
            else:
                norm3_stt(tg)
            if tg >= 1:
                ple_group(tg - 1)
            norm3_tr(tg)
            if tg >= 2:
                final_group(tg - 2)
    final_group(2)
    ple_group(3)
    final_group(3)
    SP.wait(*outs[-4:])
    for e in engines:
        e.wait(*outs[-4:])
    return nc, dbg


_CACHE = {}


def _in_maps(inputs, cores):
    f = lambda a: np.ascontiguousarray(np.asarray(a, dtype=np.float32))
    x = f(inputs["x"])
    p = f(inputs["p"])[0]
    pos = np.ascontiguousarray(np.asarray(inputs["positions"]).astype(np.int32))
    gains = np.ascontiguousarray(np.stack([f(inputs["norm_mix_g"])[0], f(inputs["norm_ffn_g"])[0],
                                           f(inputs["norm_ple_g"])[0], f(inputs["final_norm_g"])], axis=0))
    cvec = np.ascontiguousarray(np.stack([f(inputs["conv_b"])[0], f(inputs["conv_ln_g"])[0], f(inputs["conv_ln_b"])[0]], axis=0))
    shared = {
        "gains": gains, "w_in": f(inputs["w_in"])[0], "conv_w": f(inputs["conv_w"])[0], "cvec": cvec,
        "w_out": f(inputs["w_out"])[0], "w_up": f(inputs["w_ffn_up"])[0], "w_down": f(inputs["w_ffn_down"])[0],
        "w_gate": f(inputs["w_ple_gate"])[0], "w_ple": f(inputs["w_ple_proj"])[0],
    }
    maps = []
    for b in cores:
        m = dict(shared)
        m["x"] = np.ascontiguousarray(x[b])
        m["p"] = np.ascontiguousarray(p[b])
        m["pos"] = np.ascontiguousarray(pos[b])
        maps.append(m)
    return maps


def kernel(**inputs):
    if "nc" not in _CACHE:
        _CACHE["nc"] = build(False)[0]
    nc = _CACHE["nc"]
    maps = _in_maps(inputs, list(range(8)))
    res = run_bass_kernel_spmd(nc, maps, core_ids=list(range(8)))
    out = np.stack([np.asarray(r["out"], dtype=np.float32) for r in res.results], axis=0)
    return out
```
